# Optimizing a Trainium2 kernel written in Bass

```python
import math
import jax, jax.numpy as jnp
from jax import lax
import numpy as np

D_MODEL = 1024
BATCH = 8
SEQ = 4096
DEPTH = 2

MLA_HEADS = 8
MLA_NOPE = 64
MLA_ROPE = 32
MLA_QK = MLA_NOPE + MLA_ROPE
MLA_V = 64
MLA_Q_RANK = 384
MLA_KV_RANK = 256
MLA_DIM = MLA_HEADS * MLA_V
ROPE_THETA = 10000.0
ATTN_BLOCK = 128

RWKV_HEAD = 64
RWKV_HEADS = 8
RWKV_DIM = RWKV_HEADS * RWKV_HEAD
RWKV_W_RANK = 64
RWKV_A_RANK = 64
RWKV_V_RANK = 32
RWKV_G_RANK = 128
RWKV_GN_EPS = 64e-5

SSM_HEAD = 64
SSM_HEADS = 16
SSM_DIM = SSM_HEADS * SSM_HEAD
SSM_GROUPS = 2
SSM_HPG = SSM_HEADS // SSM_GROUPS
SSM_STATE = 128
SSM_CONV = 4
SSM_CHUNK = 256
SSM_CONV_DIM = SSM_DIM + 2 * SSM_GROUPS * SSM_STATE
SSM_NORM_EPS = 1e-5
DT_MIN = 1e-3
DT_MAX = 1e-1

N_BRANCH = 3
D_FF = 4 * D_MODEL
NORM_EPS = 1e-6
MAX_POS_OFFSET = 4096

MLA_IN = MLA_Q_RANK + MLA_KV_RANK + MLA_ROPE
RWKV_IN = 3 * RWKV_DIM + RWKV_W_RANK + RWKV_A_RANK + RWKV_G_RANK
SSM_IN = SSM_DIM + SSM_CONV_DIM + SSM_HEADS
GATE_IN = N_BRANCH * D_MODEL
IN_DIM = MLA_IN + RWKV_IN + SSM_IN + GATE_IN

kernel_name = 'hybrid_mla_rwkv7_mamba2_block'


def rms_norm(x, gain, eps):
    xf = x.astype(jnp.float32)
    y = xf * lax.rsqrt(jnp.mean(xf * xf, axis=-1, keepdims=True) + eps)
    return (y * gain.astype(jnp.float32)).astype(x.dtype)


def token_shift(t):
    return jnp.pad(t[:, :-1], ((0, 0), (1, 0), (0, 0)))


def rope(t, positions):
    half = t.shape[-1] // 2
    inv_freq = ROPE_THETA ** (-jnp.arange(half, dtype=jnp.float32) / half)
    ang = positions.astype(jnp.float32)[:, :, None, None] * inv_freq
    cos, sin = jnp.cos(ang), jnp.sin(ang)
    tf = t.astype(jnp.float32)
    t1, t2 = tf[..., :half], tf[..., half:]
    return jnp.concatenate([t1 * cos - t2 * sin, t1 * sin + t2 * cos], axis=-1).astype(t.dtype)


def causal_block_attention(q, k, v):
    seq = q.shape[1]
    scale = q.shape[-1] ** -0.5
    outs = []
    for i in range(seq // ATTN_BLOCK):
        lo, hi = i * ATTN_BLOCK, (i + 1) * ATTN_BLOCK
        s = jnp.einsum('bqhd,bkhd->bhqk', q[:, lo:hi], k[:, :hi], preferred_element_type=jnp.float32) * scale
        mask = (lo + jnp.arange(ATTN_BLOCK))[:, None] >= jnp.arange(hi)[None, :]
        s = jnp.where(mask, s, -jnp.inf)
        probs = jax.nn.softmax(s, axis=-1).astype(v.dtype)
        outs.append(jnp.einsum('bhqk,bkhd->bqhd', probs, v[:, :hi]))
    return jnp.concatenate(outs, axis=1)


def mla_branch(p, positions, q_norm_g, kv_norm_g, w_uq, w_ukv, q_head_g, k_head_g):
    bsz, seq, _ = p.shape
    c_q, c_kv, k_rope = jnp.split(p, [MLA_Q_RANK, MLA_Q_RANK + MLA_KV_RANK], axis=-1)
    c_q = rms_norm(c_q, q_norm_g, NORM_EPS)
    c_kv = rms_norm(c_kv, kv_norm_g, NORM_EPS)
    q = (c_q @ w_uq).reshape(bsz, seq, MLA_HEADS, MLA_QK)
    kv = (c_kv @ w_ukv).reshape(bsz, seq, MLA_HEADS, MLA_NOPE + MLA_V)
    k_nope, v = kv[..., :MLA_NOPE], kv[..., MLA_NOPE:]
    k_rope = jnp.broadcast_to(k_rope[:, :, None, :], (bsz, seq, MLA_HEADS, MLA_ROPE))
    k = jnp.concatenate([k_nope, k_rope], axis=-1)
    q = rms_norm(q, q_head_g, NORM_EPS)
    k = rms_norm(k, k_head_g, NORM_EPS)
    q = jnp.concatenate([q[..., :MLA_NOPE], rope(q[..., MLA_NOPE:], positions)], axis=-1)
    k = jnp.concatenate([k[..., :MLA_NOPE], rope(k[..., MLA_NOPE:], positions)], axis=-1)
    o = causal_block_attention(q, k, v)
    return o.reshape(bsz, seq, MLA_DIM)


def rwkv7_scan(r, decay, k, v, a, b):
    bsz, _, nh, n = r.shape

    def step(state, inp):
        r_t, w_t, k_t, v_t, a_t, b_t = inp
        sa = jnp.einsum('bhvk,bhk->bhv', state, a_t)
        state = state * w_t[:, :, None, :] + sa[..., None] * b_t[:, :, None, :] + v_t[..., None] * k_t[:, :, None, :]
        return state, jnp.einsum('bhvk,bhk->bhv', state, r_t)

    xs = tuple(jnp.swapaxes(t, 0, 1) for t in (r, decay, k, v, a, b))
    state0 = jnp.zeros((bsz, nh, n, n), jnp.float32)
    _, ys = lax.scan(step, state0, xs)
    return jnp.swapaxes(ys, 0, 1)


def rwkv7_branch(p, v_first, v_res, mu, w0, w2, a0, a2, g2, k_k, k_a, r_k, ln_g, ln_b):
    bsz, seq, _ = p.shape
    f32 = jnp.float32
    p = p + (token_shift(p) - p) * mu
    r, k, v, xw, xa, xg = jnp.split(
        p, [RWKV_DIM, 2 * RWKV_DIM, 3 * RWKV_DIM, 3 * RWKV_DIM + RWKV_W_RANK,
            3 * RWKV_DIM + RWKV_W_RANK + RWKV_A_RANK], axis=-1)
    w = -jax.nn.softplus(-(w0 + jnp.tanh(xw) @ w2).astype(f32)) - 0.5
    decay = jnp.exp(-jnp.exp(w))
    if v_res is None:
        v_first = v
    else:
        v0, v1, v2 = v_res
        v = v + (v_first - v) * jax.nn.sigmoid(v0 + (v @ v1) @ v2)
    a = jax.nn.sigmoid(a0 + xa @ a2)
    g = jax.nn.sigmoid(xg) @ g2

    def heads(t):
        return t.reshape(bsz, seq, RWKV_HEADS, RWKV_HEAD).astype(f32)

    kk = heads(k * k_k)
    kk = kk / jnp.maximum(jnp.sqrt(jnp.sum(kk * kk, axis=-1, keepdims=True)), 1e-12)
    k = k * (1.0 + (a - 1.0) * k_a)
    rh, kh, vh, ah = heads(r), heads(k), heads(v), heads(a)
    y = rwkv7_scan(rh, heads(decay), kh, vh, -kk, kk * ah)
    mean = jnp.mean(y, axis=-1, keepdims=True)
    var = jnp.mean(jnp.square(y - mean), axis=-1, keepdims=True)
    y = (y - mean) * lax.rsqrt(var + RWKV_GN_EPS)
    y = y * ln_g.astype(f32).reshape(RWKV_HEADS, RWKV_HEAD) + ln_b.astype(f32).reshape(RWKV_HEADS, RWKV_HEAD)
    y = y + jnp.sum(rh * kh * r_k.astype(f32), axis=-1, keepdims=True) * vh
    y = y.reshape(bsz, seq, RWKV_DIM) * g.astype(f32)
    return y.astype(p.dtype), v_first


def causal_depthwise_conv(x, w, b):
    width, ch = w.shape
    y = lax.conv_general_dilated(
        x, w[:, None, :], window_strides=(1,), padding=[(width - 1, 0)],
        dimension_numbers=('NWC', 'WIO', 'NWC'), feature_group_count=ch)
    return y + b


def segsum(a):
    t = a.shape[-1]
    rep = jnp.broadcast_to(a[..., :, None], a.shape + (t,))
    strict = jnp.tril(jnp.ones((t, t), dtype=bool), k=-1)
    cs = jnp.cumsum(jnp.where(strict, rep, 0.0), axis=-2)
    return jnp.where(jnp.tril(jnp.ones((t, t), dtype=bool)), cs, -jnp.inf)


def pad_seq(t, pad):
    return jnp.pad(t, [(0, 0), (0, pad)] + [(0, 0)] * (t.ndim - 2))


def ssd_chunked(x, da, bm, cm):
    bsz, seq = x.shape[:2]
    pad = (-seq) % SSM_CHUNK
    if pad:
        x, da, bm, cm = pad_seq(x, pad), pad_seq(da, pad), pad_seq(bm, pad), pad_seq(cm, pad)
    nc = (seq + pad) // SSM_CHUNK
    x = x.reshape(bsz, nc, SSM_CHUNK, SSM_GROUPS, SSM_HPG, SSM_HEAD)
    bm = bm.reshape(bsz, nc, SSM_CHUNK, SSM_GROUPS, SSM_STATE)
    cm = cm.reshape(bsz, nc, SSM_CHUNK, SSM_GROUPS, SSM_STATE)
    da = da.reshape(bsz, nc, SSM_CHUNK, SSM_GROUPS, SSM_HPG).transpose(0, 3, 4, 1, 2)
    a_cs = jnp.cumsum(da, axis=-1)
    decay_in = jnp.exp(segsum(da))
    cb = jnp.einsum('bclgn,bcsgn->bgcls', cm, bm)
    y_diag = jnp.einsum('bgcls,bgecls,bcsgep->bclgep', cb, decay_in, x)
    decay_to_end = jnp.exp(a_cs[..., -1:] - a_cs)
    states = jnp.einsum('bcsgn,bgecs,bcsgep->bcgepn', bm, decay_to_end, x)
    chunk_tot = jnp.pad(a_cs[..., -1], ((0, 0), (0, 0), (0, 0), (1, 0)))
    decay_chunk = jnp.exp(segsum(chunk_tot))
    states = jnp.pad(states, ((0, 0), (1, 0), (0, 0), (0, 0), (0, 0), (0, 0)))
    states = jnp.einsum('bgezc,bcgepn->bzgepn', decay_chunk, states)[:, :-1]
    y_off = jnp.einsum('bclgn,bcgepn,bgecl->bclgep', cm, states, jnp.exp(a_cs))
    y = (y_diag + y_off).reshape(bsz, nc * SSM_CHUNK, SSM_GROUPS, SSM_HPG, SSM_HEAD)
    return y[:, :seq]


def mamba2_branch(p, conv_w, conv_b, dt_bias, a_log, d_skip, norm_g):
    bsz, seq, _ = p.shape
    f32 = jnp.float32
    z, xbc, dt = jnp.split(p, [SSM_DIM, SSM_DIM + SSM_CONV_DIM], axis=-1)
    xbc = jax.nn.silu(causal_depthwise_conv(xbc, conv_w, conv_b))
    xs, b_in, c_in = jnp.split(xbc, [SSM_DIM, SSM_DIM + SSM_GROUPS * SSM_STATE], axis=-1)
    xs = xs.reshape(bsz, seq, SSM_GROUPS, SSM_HPG, SSM_HEAD).astype(f32)
    b_in = b_in.reshape(bsz, seq, SSM_GROUPS, SSM_STATE).astype(f32)
    c_in = c_in.reshape(bsz, seq, SSM_GROUPS, SSM_STATE).astype(f32)
    dt = jax.nn.softplus(dt.astype(f32) + dt_bias.astype(f32)).reshape(bsz, seq, SSM_GROUPS, SSM_HPG)
    a = -jnp.exp(a_log.astype(f32)).reshape(SSM_GROUPS, SSM_HPG)
    y = ssd_chunked(xs * dt[..., None], dt * a, b_in, c_in)
    y = y + xs * d_skip.astype(f32).reshape(SSM_GROUPS, SSM_HPG)[..., None]
    y = y.reshape(bsz, seq, SSM_DIM) * jax.nn.silu(z.astype(f32))
    y = y.reshape(bsz, seq, SSM_GROUPS, SSM_DIM // SSM_GROUPS)
    y = y * lax.rsqrt(jnp.mean(y * y, axis=-1, keepdims=True) + SSM_NORM_EPS)
    y = y.reshape(bsz, seq, SSM_DIM) * norm_g.astype(f32)
    return y.astype(p.dtype)


def setup_inputs(seed: int = 0) -> dict:
    key = jax.random.key(seed)
    ks = iter(list(jax.random.split(key, 64)))
    L = DEPTH

    def nrm(shape, scale):
        return jax.random.normal(next(ks), shape, jnp.float32) * scale

    def gain(shape):
        return 1.0 + nrm(shape, 0.02)

    def unif(shape, lo, hi):
        return jax.random.uniform(next(ks), shape, jnp.float32, minval=lo, maxval=hi)

    x = nrm((BATCH, SEQ, D_MODEL), 1.0)
    start = jax.random.randint(next(ks), (BATCH, 1), 0, MAX_POS_OFFSET, dtype=jnp.int32)
    positions = start + jnp.arange(SEQ, dtype=jnp.int32)[None, :]

    dt = jnp.exp(unif((L, SSM_HEADS), math.log(DT_MIN), math.log(DT_MAX)))
    dt_bias = dt + jnp.log(-jnp.expm1(-dt))

    return {
        'x': x,
        'positions': positions,
        'norm_mix_g': gain((L, D_MODEL)),
        'w_in': nrm((L, D_MODEL, IN_DIM), D_MODEL ** -0.5),
        'mla_q_norm_g': gain((L, MLA_Q_RANK)),
        'mla_kv_norm_g': gain((L, MLA_KV_RANK)),
        'mla_w_uq': nrm((L, MLA_Q_RANK, MLA_HEADS * MLA_QK), MLA_Q_RANK ** -0.5),
        'mla_w_ukv': nrm((L, MLA_KV_RANK, MLA_HEADS * (MLA_NOPE + MLA_V)), MLA_KV_RANK ** -0.5),
        'mla_q_head_g': gain((L, MLA_QK)),
        'mla_k_head_g': gain((L, MLA_QK)),
        'rwkv_mu': unif((L, RWKV_IN), 0.0, 1.0),
        'rwkv_w0': unif((L, RWKV_DIM), -6.5, -1.5),
        'rwkv_w2': nrm((L, RWKV_W_RANK, RWKV_DIM), RWKV_W_RANK ** -0.5),
        'rwkv_a0': nrm((L, RWKV_DIM), 0.1),
        'rwkv_a2': nrm((L, RWKV_A_RANK, RWKV_DIM), RWKV_A_RANK ** -0.5),
        'rwkv_g2': nrm((L, RWKV_G_RANK, RWKV_DIM), RWKV_G_RANK ** -0.5),
        'rwkv_v0': 1.0 + nrm((L - 1, RWKV_DIM), 0.1),
        'rwkv_v1': nrm((L - 1, RWKV_DIM, RWKV_V_RANK), RWKV_DIM ** -0.5),
        'rwkv_v2': nrm((L - 1, RWKV_V_RANK, RWKV_DIM), RWKV_V_RANK ** -0.5),
        'rwkv_k_k': 0.85 + nrm((L, RWKV_DIM), 0.02),
        'rwkv_k_a': gain((L, RWKV_DIM)),
        'rwkv_r_k': nrm((L, RWKV_HEADS, RWKV_HEAD), 0.1),
        'rwkv_ln_g': gain((L, RWKV_DIM)),
        'rwkv_ln_b': nrm((L, RWKV_DIM), 0.02),
        'ssm_conv_w': nrm((L, SSM_CONV, SSM_CONV_DIM), SSM_CONV ** -0.5),
        'ssm_conv_b': nrm((L, SSM_CONV_DIM), 0.02),
        'ssm_dt_bias': dt_bias,
        'ssm_a_log': jnp.log(unif((L, SSM_HEADS), 1.0, 16.0)),
        'ssm_d': 1.0 + nrm((L, SSM_HEADS), 0.1),
        'ssm_norm_g': gain((L, SSM_DIM)),
        'w_br_mla': nrm((L, MLA_DIM, D_MODEL), MLA_DIM ** -0.5),
        'w_br_rwkv': nrm((L, RWKV_DIM, D_MODEL), RWKV_DIM ** -0.5),
        'w_br_ssm': nrm((L, SSM_DIM, D_MODEL), SSM_DIM ** -0.5),
        'w_out': nrm((L, D_MODEL, D_MODEL), D_MODEL ** -0.5),
        'norm_ffn_g': gain((L, D_MODEL)),
        'w_ff1': nrm((L, D_MODEL, D_FF), D_MODEL ** -0.5),
        'w_ff2': nrm((L, D_FF, D_MODEL), D_FF ** -0.5),
    }


def reference(x, positions, norm_mix_g, w_in, mla_q_norm_g, mla_kv_norm_g, mla_w_uq, mla_w_ukv,
              mla_q_head_g, mla_k_head_g, rwkv_mu, rwkv_w0, rwkv_w2, rwkv_a0, rwkv_a2, rwkv_g2,
              rwkv_v0, rwkv_v1, rwkv_v2, rwkv_k_k, rwkv_k_a, rwkv_r_k, rwkv_ln_g, rwkv_ln_b,
              ssm_conv_w, ssm_conv_b, ssm_dt_bias, ssm_a_log, ssm_d, ssm_norm_g,
              w_br_mla, w_br_rwkv, w_br_ssm, w_out, norm_ffn_g, w_ff1, w_ff2):
    bsz, seq, _ = x.shape
    splits = [MLA_IN, MLA_IN + RWKV_IN, MLA_IN + RWKV_IN + SSM_IN]
    v_first = None
    for l in range(DEPTH):
        h = rms_norm(x, norm_mix_g[l], NORM_EPS)
        proj = h @ w_in[l]
        p_mla, p_rwkv, p_ssm, p_gate = jnp.split(proj, splits, axis=-1)

        o_mla = mla_branch(p_mla, positions, mla_q_norm_g[l], mla_kv_norm_g[l], mla_w_uq[l],
                           mla_w_ukv[l], mla_q_head_g[l], mla_k_head_g[l])
        v_res = None if l == 0 else (rwkv_v0[l - 1], rwkv_v1[l - 1], rwkv_v2[l - 1])
        o_rwkv, v_first = rwkv7_branch(p_rwkv, v_first, v_res, rwkv_mu[l], rwkv_w0[l], rwkv_w2[l],
                                       rwkv_a0[l], rwkv_a2[l], rwkv_g2[l], rwkv_k_k[l], rwkv_k_a[l],
                                       rwkv_r_k[l], rwkv_ln_g[l], rwkv_ln_b[l])
        o_ssm = mamba2_branch(p_ssm, ssm_conv_w[l], ssm_conv_b[l], ssm_dt_bias[l], ssm_a_log[l],
                              ssm_d[l], ssm_norm_g[l])

        gates = jax.nn.sigmoid(p_gate.astype(jnp.float32)).astype(x.dtype)
        gates = gates.reshape(bsz, seq, N_BRANCH, D_MODEL)
        merged = (gates[:, :, 0] * (o_mla @ w_br_mla[l])
                  + gates[:, :, 1] * (o_rwkv @ w_br_rwkv[l])
                  + gates[:, :, 2] * (o_ssm @ w_br_ssm[l]))
        x = x + merged @ w_out[l]

        h = rms_norm(x, norm_ffn_g[l], NORM_EPS)
        x = x + jnp.square(jax.nn.relu(h @ w_ff1[l])) @ w_ff2[l]
    return x
```

```python
import math
from contextlib import ExitStack
import numpy as np
import concourse.bass as bass
import concourse.mybir as mybir
from concourse.bass_utils import run_bass_kernel_spmd

F32 = mybir.dt.float32
BF16 = mybir.dt.bfloat16
I32 = mybir.dt.int32
AF = mybir.ActivationFunctionType
ALU = mybir.AluOpType
AX = mybir.AxisListType

D = 1024
DEPTH = 2
H_MLA = 8
QK = 96
NOPE = 64
ROPE = 32
QR = 384
KVR = 256
MLA_IN = QR + KVR + ROPE
RW = 512
RWKV_IN = 3 * RW + 64 + 64 + 128
SSM_D = 1024
SSM_CONV_DIM = 1536
SSM_IN = SSM_D + SSM_CONV_DIM + 16
GATE_IN = 3 * D
IN_DIM = MLA_IN + RWKV_IN + SSM_IN + GATE_IN
C_RWKV = MLA_IN
C_SSM = MLA_IN + RWKV_IN
C_GATE = C_SSM + SSM_IN
DFF = 4096
EPS = 1e-6

NLANES = 12
SKIP_RWKV = False
SKIP = set()
RWKV_STAGE = 9
RWKV_SUB = 9
RWKV_H2 = (0, 1)
RWKV_MM = (0, 1, 2)


class Buf:
    __slots__ = ("name", "w", "r")

    def __init__(self, name):
        self.name = name
        self.w = None
        self.r = {}


class Prog:
    ENGS = ("pe", "dve", "act", "pool", "sp")

    def __init__(self, nc):
        self.nc = nc
        self.es = ExitStack()
        self.sem = {n: self.es.enter_context(nc.semaphore("s_" + n)) for n in self.ENGS}
        self.cnt = {n: 0 for n in self.ENGS}
        self.ops = {n: [] for n in self.ENGS}
        self.seen = {n: {} for n in self.ENGS}
        self.lanes = {}
        for q in ("sp", "pool", "act"):
            self.lanes[q] = [[self.es.enter_context(nc.semaphore("d_%s%d" % (q, i))), 0]
                             for i in range(NLANES)]
        self.lane_rr = {"sp": 0, "pool": 0, "act": 0}
        self.dma_rr = 0
        self.nbuf = 0
        self.ninst = 0

    def buf(self, name=None):
        self.nbuf += 1
        return Buf(name or "b%d" % self.nbuf)

    def _collect(self, eng, reads, writes, is_dma):
        waits = {}

        def need(tok, raw):
            if tok is None:
                return
            key, sem, val, src = tok
            if not is_dma and src == eng:
                if eng == "pe" or not raw:
                    return
            cur = waits.get(key)
            if cur is None or cur[1] < val:
                waits[key] = (sem, val)

        for b in reads:
            need(b.w, True)
        for b in writes:
            need(b.w, False)
            for t in b.r.values():
                need(t, False)
        return waits

    def _emit_waits(self, eng, waits):
        seen = self.seen[eng]
        for key, (sem, val) in waits.items():
            if seen.get(key, 0) >= val:
                continue
            seen[key] = val
            self.ops[eng].append(("w", sem, val))

    def _commit(self, tok, reads, writes):
        for b in writes:
            b.w = tok
            b.r = {}
        for b in reads:
            b.r[tok[0]] = tok

    def op(self, eng, fn, reads=(), writes=()):
        waits = self._collect(eng, reads, writes, False)
        self._emit_waits(eng, waits)
        self.cnt[eng] += 1
        self.ops[eng].append(("o", fn, self.sem[eng], 1))
        self.ninst += 1
        tok = (eng, self.sem[eng], self.cnt[eng], eng)
        self._commit(tok, reads, writes)

    def dma(self, out, in_, reads=(), writes=(), q=None):
        if q is None:
            q = ("sp", "pool")[self.dma_rr % 2]
            self.dma_rr += 1
        li = self.lane_rr[q]
        self.lane_rr[q] = (li + 1) % NLANES
        lane = self.lanes[q][li]
        key = "d_%s%d" % (q, li)
        waits = self._collect(q, reads, writes, True)
        if lane[1] > 0:
            cur = waits.get(key)
            if cur is None or cur[1] < lane[1]:
                waits[key] = (lane[0], lane[1])
        self._emit_waits(q, waits)
        lane[1] += 16
        self.ops[q].append(("o", lambda e, o=out, i=in_: e.dma_start(out=o, in_=i), lane[0], 16))
        self.ninst += 1
        tok = (key, lane[0], lane[1], "dma")
        self._commit(tok, reads, writes)
        return tok

    def wait_tok(self, eng, tok):
        self._emit_waits(eng, {tok[0]: (tok[1], tok[2])})

    def flush(self):
        nc = self.nc
        ops = self.ops
        with nc.Block() as block:
            def run(e, lst):
                for it in lst:
                    if it[0] == "w":
                        e.wait_ge(it[1], it[2])
                    else:
                        it[1](e).then_inc(it[2], it[3])

            if ops["sp"]:
                @block.sync
                def _(e):
                    run(e, ops["sp"])
            if ops["pe"]:
                @block.tensor
                def _(e):
                    run(e, ops["pe"])
            if ops["dve"]:
                @block.vector
                def _(e):
                    run(e, ops["dve"])
            if ops["act"]:
                @block.scalar
                def _(e):
                    run(e, ops["act"])
            if ops["pool"]:
                @block.gpsimd
                def _(e):
                    run(e, ops["pool"])
        self.ops = {n: [] for n in self.ENGS}

    def final_wait(self, eng, bufs):
        waits = {}
        for b in bufs:
            if b.w is not None:
                key, sem, val, src = b.w
                cur = waits.get(key)
                if cur is None or cur[1] < val:
                    waits[key] = (sem, val)
        self._emit_waits(eng, waits)


def bc_rows(ap_dram_row, n):
    return ap_dram_row.partition_broadcast(128)


class Ctx:
    pass


def build_consts():
    c = {}
    c["ident_f"] = np.eye(128, dtype=np.float32)
    t = np.arange(128)
    same = (t[:, None] // 64) == (t[None, :] // 64)
    c["tri_incl_bd"] = (same & (t[:, None] <= t[None, :])).astype(np.float32)
    c["tri_excl_bd"] = (same & (t[:, None] < t[None, :])).astype(np.float32)
    c["tri_after_bd"] = (same & (t[:, None] > t[None, :])).astype(np.float32)
    c["low_strict_bd"] = (same & (t[:, None] > t[None, :])).astype(np.float32)
    c["tri_incl"] = (t[:, None] <= t[None, :]).astype(np.float32)
    c["tri_gt"] = (t[:, None] > t[None, :]).astype(np.float32)
    c["ones"] = np.ones((128, 128), dtype=np.float32)
    half = ROPE // 2
    inv = (np.float32(10000.0) ** (-np.arange(half, dtype=np.float32) / np.float32(half))).astype(np.float32)
    c["inv_freq"] = inv.reshape(1, half)
    hm = np.zeros((8, 512), dtype=np.float32)
    for h in range(8):
        hm[h, h * 64:(h + 1) * 64] = 1.0
    c["headmask"] = hm
    return c


CONST_SHAPES = {k: v.shape for k, v in build_consts().items()}

PARAM_SHAPES = {
    "norm_mix_g": (DEPTH, D), "w_in": (DEPTH, D, IN_DIM), "mla_q_norm_g": (DEPTH, QR),
    "mla_kv_norm_g": (DEPTH, KVR), "mla_w_uq": (DEPTH, QR, 768), "mla_w_ukv": (DEPTH, KVR, 1024),
    "mla_q_head_g": (DEPTH, QK), "mla_k_head_g": (DEPTH, QK), "rwkv_mu": (DEPTH, RWKV_IN),
    "rwkv_w0": (DEPTH, RW), "rwkv_w2": (DEPTH, 64, RW), "rwkv_a0": (DEPTH, RW),
    "rwkv_a2": (DEPTH, 64, RW), "rwkv_g2": (DEPTH, 128, RW), "rwkv_v0": (DEPTH - 1, RW),
    "rwkv_v1": (DEPTH - 1, RW, 32), "rwkv_v2": (DEPTH - 1, 32, RW), "rwkv_k_k": (DEPTH, RW),
    "rwkv_k_a": (DEPTH, RW), "rwkv_r_k": (DEPTH, RW), "rwkv_ln_g": (DEPTH, RW),
    "rwkv_ln_b": (DEPTH, RW), "ssm_conv_w": (DEPTH, 4, SSM_CONV_DIM), "ssm_conv_b": (DEPTH, SSM_CONV_DIM),
    "ssm_dt_bias": (DEPTH, 16), "ssm_a_log": (DEPTH, 16), "ssm_d": (DEPTH, 16),
    "ssm_norm_g": (DEPTH, SSM_D), "w_br_mla": (DEPTH, 512, D), "w_br_rwkv": (DEPTH, RW, D),
    "w_br_ssm": (DEPTH, SSM_D, D), "w_out": (DEPTH, D, D), "norm_ffn_g": (DEPTH, D),
    "w_ff1": (DEPTH, D, DFF), "w_ff2": (DEPTH, DFF, D),
}


def cast_split(P, k, out, in_, reads, writes):
    eng = ("dve", "act", "pool")[k % 3]
    if eng == "act":
        P.op("act", lambda e: e.copy(out, in_), reads, writes)
    else:
        P.op(eng, lambda e: e.tensor_copy(out, in_), reads, writes)


def rstd_from_sumsq(P, out, ss, n, eps, reads, writes):
    P.op("dve", lambda e: e.tensor_scalar(out, ss, 1.0 / n, eps, ALU.mult, ALU.add), reads, writes)
    P.op("act", lambda e: e.activation(out, out, AF.Sqrt), writes, writes)
    P.op("dve", lambda e: e.reciprocal(out, out), writes, writes)


def phase_inproj(P, g, l, x_ap, b_x, T):
    nc = P.nc
    NB = T // 128
    with ExitStack() as es:
        def sb(name, shape, dt):
            return es.enter_context(nc.sbuf_tensor("p1_%d_" % l + name, shape, dt))

        def ps(name, shape, dt):
            return es.enter_context(nc.psum_tensor("p1_%d_" % l + name, shape, dt))

        hT = sb("hT", [128, 8, T + 1], BF16)
        b_hT = P.buf("hT")
        gain = sb("gain", [128, D], F32)
        b_gain = P.buf()
        ident = sb("ident", [128, 128], BF16)
        identf = sb("identf", [128, 128], F32)
        b_ident = P.buf()
        mu = sb("mu", [128, RWKV_IN], F32)
        omu = sb("omu", [128, RWKV_IN], F32)
        b_mu = P.buf()
        xin = [sb("xin%d" % i, [128, D], F32) for i in range(2)]
        b_xin = [P.buf() for _ in range(2)]
        sq = sb("sq", [128, D], F32)
        b_sq = P.buf()
        ss = [sb("ss%d" % i, [128, 1], F32) for i in range(2)]
        b_ss = [P.buf() for _ in range(2)]
        hb = [sb("hb%d" % i, [128, D], BF16) for i in range(2)]
        b_hb = [P.buf() for _ in range(2)]
        ptr = [ps("ptr%d" % i, [128, 8, 128], BF16) for i in range(2)]
        b_ptr = [P.buf() for _ in range(2)]

        P.dma(gain[:], g["norm_mix_g"][l].partition_broadcast(128), [], [b_gain])
        P.dma(identf[:], g["ident_f"][:, :], [], [b_ident])
        P.op("dve", lambda e: e.tensor_copy(ident[:], identf[:]), [b_ident], [b_ident])
        P.dma(mu[:], g["rwkv_mu"][l].partition_broadcast(128), [], [b_mu])
        P.op("dve", lambda e: e.tensor_scalar(omu[:], mu[:], -1.0, 1.0, ALU.mult, ALU.add), [b_mu], [b_mu])
        P.op("pool", lambda e: e.memset(hT[:, :, 0:1], 0.0), [], [b_hT])

        for tb in range(NB):
            i = tb % 2
            P.dma(xin[i][:], x_ap[tb * 128:(tb + 1) * 128, :], [b_x], [b_xin[i]])
            P.op("act", lambda e, i=i: e.activation(sq[:], xin[i][:], AF.Square, accum_out=ss[i][:]),
                 [b_xin[i]], [b_sq, b_ss[i]])
            rstd_from_sumsq(P, ss[i][:], ss[i][:], D, EPS, [b_ss[i]], [b_ss[i]])
            P.op("dve", lambda e, i=i: e.scalar_tensor_tensor(hb[i][:], xin[i][:], ss[i][:, 0:1], gain[:],
                                                             ALU.mult, ALU.mult),
                 [b_xin[i], b_ss[i], b_gain], [b_hb[i]])

            def tr(e, i=i):
                r = None
                for kc in range(8):
                    r = e.transpose(ptr[i][:, kc, :], hb[i][:, kc * 128:(kc + 1) * 128], ident[:])
                return r
            P.op("pe", tr, [b_hb[i], b_ident], [b_ptr[i]])
            P.op("act" if tb % 2 else "dve",
                 (lambda e, i=i, tb=tb: e.copy(hT[:, :, 1 + tb * 128:1 + (tb + 1) * 128], ptr[i][:]))
                 if tb % 2 else
                 (lambda e, i=i, tb=tb: e.tensor_copy(hT[:, :, 1 + tb * 128:1 + (tb + 1) * 128], ptr[i][:])),
                 [b_ptr[i]], [b_hT])

        CG = 512
        groups = []
        c0 = 0
        bounds = [0, C_RWKV, C_SSM, IN_DIM]
        for bi in range(3):
            a, b = bounds[bi], bounds[bi + 1]
            c = a
            while c < b:
                w = min(CG, b - c)
                groups.append((c, w, bi == 1))
                c += w
        wst = [sb("wst%d" % i, [128, 8, CG], F32) for i in range(2)]
        b_wst = [P.buf() for _ in range(2)]
        wbf = [sb("wbf%d" % i, [128, 16, CG], BF16) for i in range(2)]
        b_wbf = [P.buf() for _ in range(2)]
        pacc = [ps("pacc%d" % i, [128, CG], F32) for i in range(4)]
        b_pacc = [P.buf() for _ in range(4)]
        ost = [sb("ost%d" % i, [128, CG], F32) for i in range(4)]
        b_ost = [P.buf() for _ in range(4)]
        proj = g["proj"]
        b_proj = g["b_proj"]
        w_in = g["w_in"]
        n = 0
        for gi, (c, w, is_rwkv) in enumerate(groups):
            i = gi % 2
            P.dma(wst[i][:, :, 0:w], w_in[l, :, c:c + w].rearrange("(k p) c -> p k c", p=128), [], [b_wst[i]])
            if is_rwkv:
                m0 = c - C_RWKV
                for kc in range(8):
                    eng = ("dve", "pool")[kc % 2]
                    P.op(eng, lambda e, i=i, kc=kc, w=w, m0=m0: e.tensor_tensor(
                        wbf[i][:, kc, 0:w], wst[i][:, kc, 0:w], omu[:, m0:m0 + w], ALU.mult),
                        [b_wst[i], b_mu], [b_wbf[i]])
                    eng = ("pool", "dve")[kc % 2]
                    P.op(eng, lambda e, i=i, kc=kc, w=w, m0=m0: e.tensor_tensor(
                        wbf[i][:, 8 + kc, 0:w], wst[i][:, kc, 0:w], mu[:, m0:m0 + w], ALU.mult),
                        [b_wst[i], b_mu], [b_wbf[i]])
            else:
                for kc in range(8):
                    cast_split(P, kc, wbf[i][:, kc, 0:w], wst[i][:, kc, 0:w], [b_wst[i]], [b_wbf[i]])
            for tb in range(NB):
                j = n % 4
                n += 1

                def mm(e, i=i, j=j, tb=tb, w=w, is_rwkv=is_rwkv):
                    r = None
                    nk = 16 if is_rwkv else 8
                    for kc in range(nk):
                        if kc < 8:
                            lhsT = hT[:, kc, 1 + tb * 128:1 + (tb + 1) * 128]
                        else:
                            lhsT = hT[:, kc - 8, tb * 128:(tb + 1) * 128]
                        r = e.matmul(pacc[j][:, 0:w], lhsT, wbf[i][:, kc, 0:w], start=(kc == 0), stop=(kc == nk - 1))
                    return r
                P.op("pe", mm, [b_hT, b_wbf[i]], [b_pacc[j]])
                if n % 2:
                    P.op("act", lambda e, j=j, w=w: e.copy(ost[j][:, 0:w], pacc[j][:, 0:w]), [b_pacc[j]], [b_ost[j]])
                else:
                    P.op("dve", lambda e, j=j, w=w: e.tensor_copy(ost[j][:, 0:w], pacc[j][:, 0:w]), [b_pacc[j]], [b_ost[j]])
                P.dma(proj[tb * 128:(tb + 1) * 128, c:c + w], ost[j][:, 0:w], [b_ost[j]], [b_proj])
        P.flush()


def phase_ffn(P, g, l, x_in, b_xin_d, x_out, b_xout_d, T):
    nc = P.nc
    SBK = 256
    NBS = SBK // 128
    NS = T // SBK
    with ExitStack() as es:
        def sb(name, shape, dt):
            return es.enter_context(nc.sbuf_tensor("p6_%d_" % l + name, shape, dt))

        def ps(name, shape, dt):
            return es.enter_context(nc.psum_tensor("p6_%d_" % l + name, shape, dt))

        w1 = sb("w1", [128, 8, DFF], BF16)
        b_w1 = P.buf()
        w2 = sb("w2", [128, 32, D], BF16)
        b_w2 = P.buf()
        wst = [sb("wst%d" % i, [128, 1024], F32) for i in range(2)]
        b_wst = [P.buf() for _ in range(2)]
        gain = sb("gain", [128, D], F32)
        b_gain = P.buf()
        ident = sb("ident", [128, 128], BF16)
        identf = sb("identf", [128, 128], F32)
        b_ident = P.buf()
        P.dma(gain[:], g["norm_ffn_g"][l].partition_broadcast(128), [], [b_gain])
        P.dma(identf[:], g["ident_f"][:, :], [], [b_ident])
        P.op("dve", lambda e: e.tensor_copy(ident[:], identf[:]), [b_ident], [b_ident])
        n = 0
        for kc in range(8):
            for half in range(4):
                i = n % 2
                n += 1
                P.dma(wst[i][:], g["w_ff1"][l, kc * 128:(kc + 1) * 128, half * 1024:(half + 1) * 1024], [], [b_wst[i]])
                cast_split(P, n, w1[:, kc, half * 1024:(half + 1) * 1024], wst[i][:], [b_wst[i]], [b_w1])
        for fc in range(32):
            i = n % 2
            n += 1
            P.dma(wst[i][:], g["w_ff2"][l, fc * 128:(fc + 1) * 128, :], [], [b_wst[i]])
            cast_split(P, n, w2[:, fc, :], wst[i][:], [b_wst[i]], [b_w2])

        xin = [sb("xin%d" % i, [128, NBS, D], F32) for i in range(2)]
        b_xin = [P.buf() for _ in range(2)]
        sq = sb("sq", [128, D], F32)
        b_sq = P.buf()
        ss = sb("ss", [128, NBS], F32)
        b_ss = P.buf()
        hb = sb("hb", [128, D], BF16)
        b_hb = P.buf()
        hT = [sb("hT%d" % i, [128, 8, SBK], BF16) for i in range(2)]
        b_hT = [P.buf() for _ in range(2)]
        ptr = [ps("ptr%d" % i, [128, 8, 128], BF16) for i in range(2)]
        b_ptr = [P.buf() for _ in range(2)]
        pu = [ps("pu%d" % i, [128, SBK], F32) for i in range(3)]
        b_pu = [P.buf() for _ in range(3)]
        po = [ps("po%d" % i, [128, 512], F32) for i in range(3)]
        b_po = [P.buf() for _ in range(3)]
        rl = [sb("rl%d" % i, [128, SBK], F32) for i in range(3)]
        b_rl = [P.buf() for _ in range(3)]
        aT = [sb("aT%d" % i, [128, 32, SBK], BF16) for i in range(1)]
        b_aT = [P.buf() for _ in range(1)]
        xo = [sb("xo%d" % i, [128, D], F32) for i in range(2)]
        b_xo = [P.buf() for _ in range(2)]
        nt = 0
        nu = 0
        no = 0
        for s in range(NS):
            i = s % 2
            P.dma(xin[i][:], x_in[s * SBK:(s + 1) * SBK, :].rearrange("(b p) c -> p b c", p=128), [b_xin_d], [b_xin[i]])
            for b4 in range(NBS):
                P.op("act", lambda e, i=i, b4=b4: e.activation(sq[:], xin[i][:, b4, :], AF.Square,
                                                             accum_out=ss[:, b4:b4 + 1]),
                     [b_xin[i]], [b_sq, b_ss])
            rstd_from_sumsq(P, ss[:], ss[:], D, EPS, [b_ss], [b_ss])
            for b4 in range(NBS):
                P.op("dve", lambda e, i=i, b4=b4: e.scalar_tensor_tensor(hb[:], xin[i][:, b4, :], ss[:, b4:b4 + 1],
                                                                       gain[:], ALU.mult, ALU.mult),
                     [b_xin[i], b_ss, b_gain], [b_hb])
                j = nt % 2
                nt += 1

                def tr(e, j=j):
                    r = None
                    for kc in range(8):
                        r = e.transpose(ptr[j][:, kc, :], hb[:, kc * 128:(kc + 1) * 128], ident[:])
                    return r
                P.op("pe", tr, [b_hb, b_ident], [b_ptr[j]])
                P.op("act", lambda e, i=i, j=j, b4=b4: e.copy(hT[i][:, :, b4 * 128:(b4 + 1) * 128], ptr[j][:]),
                     [b_ptr[j]], [b_hT[i]])
            for fc in range(32):
                j = nu % 3
                nu += 1

                def mm(e, i=i, j=j, fc=fc):
                    r = None
                    for kc in range(8):
                        r = e.matmul(pu[j][:], w1[:, kc, fc * 128:(fc + 1) * 128], hT[i][:, kc, :],
                                     start=(kc == 0), stop=(kc == 7))
                    return r
                P.op("pe", mm, [b_w1, b_hT[i]], [b_pu[j]])
                P.op("act", lambda e, j=j: e.activation(rl[j][:], pu[j][:], AF.Relu), [b_pu[j]], [b_rl[j]])
                eng = ("dve", "pool")[fc % 2]
                P.op(eng, lambda e, j=j, fc=fc: e.tensor_tensor(aT[0][:, fc, :], rl[j][:], rl[j][:], ALU.mult),
                     [b_rl[j]], [b_aT[0]])
            for b4 in range(NBS):
                k = (s * NBS + b4) % 2
                for cg in range(2):
                    j = no % 3
                    no += 1

                    def mm2(e, j=j, b4=b4, cg=cg):
                        r = None
                        for fc in range(32):
                            r = e.matmul(po[j][:], aT[0][:, fc, b4 * 128:(b4 + 1) * 128],
                                         w2[:, fc, cg * 512:(cg + 1) * 512], start=(fc == 0), stop=(fc == 31))
                        return r
                    P.op("pe", mm2, [b_aT[0], b_w2], [b_po[j]])
                    P.op("dve", lambda e, i=i, j=j, k=k, b4=b4, cg=cg: e.tensor_tensor(
                        xo[k][:, cg * 512:(cg + 1) * 512], po[j][:], xin[i][:, b4, cg * 512:(cg + 1) * 512], ALU.add),
                        [b_po[j], b_xin[i]], [b_xo[k]])
                t0 = s * SBK + b4 * 128
                P.dma(x_out[t0:t0 + 128, :], xo[k][:], [b_xo[k]], [b_xout_d])
        P.flush()


def bcast_mid(ap2d, n):
    p, f = ap2d.shape
    return ap2d.unsqueeze(1).broadcast_to([p, n, f])


def bcast_last(ap2d, n):
    p, f = ap2d.shape
    return ap2d.unsqueeze(2).broadcast_to([p, f, n])


def load_ident(P, nc, es, g, pref):
    identf = es.enter_context(nc.sbuf_tensor(pref + "identf", [128, 128], F32))
    ident = es.enter_context(nc.sbuf_tensor(pref + "ident", [128, 128], BF16))
    b = P.buf()
    P.dma(identf[:], g["ident_f"][:, :], [], [b])
    P.op("dve", lambda e: e.tensor_copy(ident[:], identf[:]), [b], [b])
    return identf, ident, b


TWO_PI = 2.0 * math.pi


def phase_mla_prep(P, g, l, T):
    nc = P.nc
    NB = T // 128
    proj, b_proj = g["proj"], g["b_proj"]
    with ExitStack() as es:
        def sb(name, shape, dt):
            return es.enter_context(nc.sbuf_tensor("p2_%d_" % l + name, shape, dt))

        def ps(name, shape, dt):
            return es.enter_context(nc.psum_tensor("p2_%d_" % l + name, shape, dt))

        identf, ident, b_ident = load_ident(P, nc, es, g, "p2_%d_" % l)
        b_c = P.buf("consts")
        gcn = sb("gcn", [128, 640], F32)
        P.dma(gcn[:, 0:384], g["mla_q_norm_g"][l].partition_broadcast(128), [], [b_c])
        P.dma(gcn[:, 384:640], g["mla_kv_norm_g"][l].partition_broadcast(128), [], [b_c])
        gq = sb("gq", [128, 96], F32)
        gk = sb("gk", [128, 96], F32)
        P.dma(gq[:], g["mla_q_head_g"][l].partition_broadcast(128), [], [b_c])
        P.dma(gk[:], g["mla_k_head_g"][l].partition_broadcast(128), [], [b_c])
        invf = sb("invf", [128, 16], F32)
        P.dma(invf[:], g["inv_freq"][0].partition_broadcast(128), [], [b_c])
        posi = sb("posi", [128, NB], I32)
        posf = sb("posf", [128, NB], F32)
        P.dma(posi[:], g["positions"][:, :], [], [b_c])
        P.op("dve", lambda e: e.tensor_copy(posf[:], posi[:]), [b_c], [b_c])
        negpi = sb("negpi", [128, 1], F32)
        P.op("pool", lambda e: e.memset(negpi[:], -math.pi), [], [b_c])
        wst = sb("wst", [128, 1024], F32)
        b_wst = P.buf()
        wuq = sb("wuq", [128, 3, 768], BF16)
        wukv = sb("wukv", [128, 2, 1024], BF16)
        b_w = P.buf()
        for kc in range(3):
            P.dma(wst[:, 0:768], g["mla_w_uq"][l, kc * 128:(kc + 1) * 128, :], [], [b_wst])
            P.op("dve", lambda e, kc=kc: e.tensor_copy(wuq[:, kc, :], wst[:, 0:768]), [b_wst], [b_w])
        for kc in range(2):
            P.dma(wst[:], g["mla_w_ukv"][l, kc * 128:(kc + 1) * 128, :], [], [b_wst])
            P.op("dve", lambda e, kc=kc: e.tensor_copy(wukv[:, kc, :], wst[:]), [b_wst], [b_w])

        pm = [sb("pm%d" % i, [128, MLA_IN], F32) for i in range(2)]
        b_pm = [P.buf() for _ in range(2)]
        sq = sb("sq", [128, 1024], F32)
        b_sq = P.buf()
        st = sb("st", [128, 4], F32)
        b_st = P.buf()
        cn = sb("cn", [128, 640], BF16)
        b_cn = P.buf()
        cT = sb("cT", [128, 5, 128], BF16)
        b_cT = P.buf()
        ptr = ps("ptr", [128, 8, 128], BF16)
        b_ptr = P.buf()
        pq = ps("pq", [128, 1024], F32)
        b_pq = P.buf()
        pkv = ps("pkv", [128, 1024], F32)
        b_pkv = P.buf()
        hs = sb("hs", [128, 16], F32)
        b_hs = P.buf()
        qn = sb("qn", [128, 8, 96], F32)
        b_qn = P.buf()
        kn = sb("kn", [128, 8, 96], F32)
        b_kn = P.buf()
        qf = sb("qf", [128, 8, 96], BF16)
        b_qf = P.buf()
        kf = sb("kf", [128, 8, 96], BF16)
        b_kf = P.buf()
        v1 = [sb("v1_%d" % i, [128, 8, 65], BF16) for i in range(2)]
        b_v1 = [P.buf() for _ in range(2)]
        for i in range(2):
            P.op("pool", lambda e, i=i: e.memset(v1[i][:, :, 64:65], 1.0), [], [b_v1[i]])
        ang = sb("ang", [128, 32], F32)
        angu = sb("angu", [128, 32], F32)
        angi = sb("angi", [128, 32], I32)
        cs = sb("cs", [128, 32], F32)
        b_cs = P.buf()
        rt = sb("rt", [128, 4, 8, 16], F32)
        b_rt = P.buf()
        qTs = [sb("qTs%d" % i, [96, 8, 512], BF16) for i in range(2)]
        kTs = [sb("kTs%d" % i, [96, 8, 512], BF16) for i in range(2)]
        b_qTs = [P.buf() for _ in range(2)]
        b_kTs = [P.buf() for _ in range(2)]

        def rope(src, dst, b_src, b_dst):
            t1 = src[:, :, 64:80]
            t2 = src[:, :, 80:96]
            cosb = bcast_mid(cs[:, 0:16], 8)
            sinb = bcast_mid(cs[:, 16:32], 8)
            P.op("dve", lambda e: e.tensor_tensor(rt[:, 0], t1, cosb, ALU.mult), [b_src, b_cs], [b_rt])
            P.op("dve", lambda e: e.tensor_tensor(rt[:, 1], t2, sinb, ALU.mult), [b_src, b_cs], [b_rt])
            P.op("dve", lambda e: e.tensor_tensor(rt[:, 2], t1, sinb, ALU.mult), [b_src, b_cs], [b_rt])
            P.op("dve", lambda e: e.tensor_tensor(rt[:, 3], t2, cosb, ALU.mult), [b_src, b_cs], [b_rt])
            P.op("dve", lambda e: e.tensor_tensor(dst[:, :, 64:80], rt[:, 0], rt[:, 1], ALU.subtract), [b_rt], [b_dst])
            P.op("dve", lambda e: e.tensor_tensor(dst[:, :, 80:96], rt[:, 2], rt[:, 3], ALU.add), [b_rt], [b_dst])
            P.op("pool", lambda e: e.tensor_copy(dst[:, :, 0:64], src[:, :, 0:64]), [b_src], [b_dst])

        for tb in range(NB):
            i = tb % 2
            sgi = (tb // 4) % 2
            P.dma(pm[i][:], proj[tb * 128:(tb + 1) * 128, 0:MLA_IN], [b_proj], [b_pm[i]])
            P.op("act", lambda e, i=i: e.activation(sq[:, 0:384], pm[i][:, 0:384], AF.Square, accum_out=st[:, 0:1]),
                 [b_pm[i]], [b_sq, b_st])
            P.op("act", lambda e, i=i: e.activation(sq[:, 384:640], pm[i][:, 384:640], AF.Square, accum_out=st[:, 1:2]),
                 [b_pm[i]], [b_sq, b_st])
            P.op("act", lambda e, i=i: e.activation(sq[:, 640:672], pm[i][:, 640:672], AF.Square, accum_out=st[:, 2:3]),
                 [b_pm[i]], [b_sq, b_st])
            rstd_from_sumsq(P, st[:, 0:1], st[:, 0:1], QR, EPS, [b_st], [b_st])
            rstd_from_sumsq(P, st[:, 1:2], st[:, 1:2], KVR, EPS, [b_st], [b_st])
            P.op("dve", lambda e, i=i: e.scalar_tensor_tensor(cn[:, 0:384], pm[i][:, 0:384], st[:, 0:1], gcn[:, 0:384],
                                                             ALU.mult, ALU.mult), [b_pm[i], b_st, b_c], [b_cn])
            P.op("dve", lambda e, i=i: e.scalar_tensor_tensor(cn[:, 384:640], pm[i][:, 384:640], st[:, 1:2],
                                                             gcn[:, 384:640], ALU.mult, ALU.mult),
                 [b_pm[i], b_st, b_c], [b_cn])

            def tr(e):
                r = None
                for kc in range(5):
                    r = e.transpose(ptr[:, kc, :], cn[:, kc * 128:(kc + 1) * 128], ident[:])
                return r
            P.op("pe", tr, [b_cn, b_ident], [b_ptr])
            P.op("act", lambda e: e.copy(cT[:], ptr[:, 0:5, :]), [b_ptr], [b_cT])

            def mmq(e):
                r = None
                for c0, w in ((0, 512), (512, 256)):
                    for kc in range(3):
                        r = e.matmul(pq[:, c0:c0 + w], cT[:, kc, :], wuq[:, kc, c0:c0 + w], start=(kc == 0), stop=(kc == 2))
                return r
            P.op("pe", mmq, [b_cT, b_w], [b_pq])

            def mmkv(e):
                r = None
                for c0 in (0, 512):
                    for kc in range(2):
                        r = e.matmul(pkv[:, c0:c0 + 512], cT[:, 3 + kc, :], wukv[:, kc, c0:c0 + 512],
                                     start=(kc == 0), stop=(kc == 1))
                return r
            P.op("pe", mmkv, [b_cT, b_w], [b_pkv])

            P.op("dve", lambda e, tb=tb: e.tensor_scalar(ang[:, 16:32], invf[:], posf[:, tb:tb + 1], None, ALU.mult),
                 [b_c], [b_cs])
            P.op("dve", lambda e: e.tensor_scalar(ang[:, 0:16], ang[:, 16:32], math.pi / 2, None, ALU.add),
                 [b_cs], [b_cs])
            P.op("dve", lambda e: e.tensor_scalar(angu[:], ang[:], 1.0 / TWO_PI, None, ALU.mult), [b_cs], [b_cs])
            P.op("dve", lambda e: e.tensor_copy(angi[:], angu[:]), [b_cs], [b_cs])
            P.op("dve", lambda e: e.tensor_copy(angu[:], angi[:]), [b_cs], [b_cs])
            P.op("dve", lambda e: e.scalar_tensor_tensor(ang[:], angu[:], -TWO_PI, ang[:], ALU.mult, ALU.add),
                 [b_cs], [b_cs])
            P.op("dve", lambda e: e.tensor_scalar(angu[:], ang[:], math.pi, None, ALU.is_gt), [b_cs], [b_cs])
            P.op("dve", lambda e: e.scalar_tensor_tensor(ang[:], angu[:], -TWO_PI, ang[:], ALU.mult, ALU.add),
                 [b_cs], [b_cs])
            P.op("dve", lambda e: e.tensor_scalar(angu[:], ang[:], -math.pi, None, ALU.is_lt), [b_cs], [b_cs])
            P.op("dve", lambda e: e.scalar_tensor_tensor(ang[:], angu[:], TWO_PI, ang[:], ALU.mult, ALU.add),
                 [b_cs], [b_cs])
            P.op("act", lambda e: e.activation(cs[:], ang[:], AF.Sin), [b_cs], [b_cs])

            pq3 = pq[:, 0:768].rearrange("p (h d) -> p h d", h=8)
            P.op("act", lambda e: e.activation(sq[:, 0:768], pq[:, 0:768], AF.Square), [b_pq], [b_sq])
            P.op("dve", lambda e: e.tensor_reduce(hs[:, 0:8], sq[:, 0:768].rearrange("p (h d) -> p h d", h=8),
                                                  AX.X, ALU.add), [b_sq], [b_hs])
            rstd_from_sumsq(P, hs[:, 0:8], hs[:, 0:8], QK, EPS, [b_hs], [b_hs])
            P.op("dve", lambda e: e.tensor_tensor(qn[:], pq3, bcast_last(hs[:, 0:8], 96), ALU.mult),
                 [b_pq, b_hs], [b_qn])
            P.op("pool", lambda e: e.tensor_tensor(qn[:], qn[:], bcast_mid(gq[:], 8), ALU.mult), [b_qn, b_c], [b_qn])
            rope(qn, qf, b_qn, b_qf)

            pkv3 = pkv[:].rearrange("p (h d) -> p h d", h=8)
            P.op("act", lambda e: e.activation(sq[:, 0:512].rearrange("p (h d) -> p h d", h=8), pkv3[:, :, 0:64],
                                               AF.Square), [b_pkv], [b_sq])
            P.op("dve", lambda e: e.tensor_reduce(hs[:, 8:16], sq[:, 0:512].rearrange("p (h d) -> p h d", h=8),
                                                  AX.X, ALU.add), [b_sq], [b_hs])
            P.op("dve", lambda e: e.tensor_scalar(hs[:, 8:16], hs[:, 8:16], st[:, 2:3], None, ALU.add),
                 [b_hs, b_st], [b_hs])
            rstd_from_sumsq(P, hs[:, 8:16], hs[:, 8:16], QK, EPS, [b_hs], [b_hs])
            P.op("dve", lambda e: e.tensor_tensor(kn[:, :, 0:64], pkv3[:, :, 0:64], bcast_last(hs[:, 8:16], 64), ALU.mult),
                 [b_pkv, b_hs], [b_kn])
            P.op("dve", lambda e, i=i: e.tensor_tensor(kn[:, :, 64:96], bcast_last(hs[:, 8:16], 32),
                                                      bcast_mid(pm[i][:, 640:672], 8), ALU.mult),
                 [b_pm[i], b_hs], [b_kn])
            P.op("pool", lambda e: e.tensor_tensor(kn[:], kn[:], bcast_mid(gk[:], 8), ALU.mult), [b_kn, b_c], [b_kn])
            rope(kn, kf, b_kn, b_kf)
            P.op("act", lambda e, i=i: e.copy(v1[i][:, :, 0:64], pkv3[:, :, 64:128]), [b_pkv], [b_v1[i]])
            P.dma(g["v1"][tb * 128:(tb + 1) * 128, :], v1[i][:].rearrange("p h d -> p (h d)"), [b_v1[i]], [g["b_v1"]])

            for (src, b_src, dstT, b_dstT) in ((qf, b_qf, qTs, b_qTs), (kf, b_kf, kTs, b_kTs)):
                def trq(e, src=src):
                    r = None
                    for h in range(8):
                        r = e.transpose(ptr[0:96, h, :], src[:, h, :], ident[:])
                    return r
                P.op("pe", trq, [b_src, b_ident], [b_ptr])
                c0 = (tb % 4) * 128
                P.op("act", lambda e, dstT=dstT, sgi=sgi, c0=c0: e.copy(dstT[sgi][:, :, c0:c0 + 128], ptr[0:96, :, :]),
                     [b_ptr], [b_dstT[sgi]])
            if tb % 4 == 3:
                t0 = (tb // 4) * 512
                P.dma(g["qT"][:, :, t0:t0 + 512].rearrange("h d t -> d h t"), qTs[sgi][:], [b_qTs[sgi]], [g["b_qT"]])
                P.dma(g["kT"][:, :, t0:t0 + 512].rearrange("h d t -> d h t"), kTs[sgi][:], [b_kTs[sgi]], [g["b_kT"]])
        P.flush()


def phase_mla_attn(P, g, l, T):
    nc = P.nc
    NB = T // 128
    NG = T // 512
    scale = QK ** -0.5
    with ExitStack() as es:
        def sb(name, shape, dt):
            return es.enter_context(nc.sbuf_tensor("p3_%d_" % l + name, shape, dt))

        def ps(name, shape, dt):
            return es.enter_context(nc.psum_tensor("p3_%d_" % l + name, shape, dt))

        b_c = P.buf()
        trif = sb("trif", [128, 128], F32)
        tri = sb("tri", [128, 128], BF16)
        P.dma(trif[:], g["tri_incl"][:, :], [], [b_c])
        P.op("dve", lambda e: e.tensor_copy(tri[:], trif[:]), [b_c], [b_c])
        onesf = sb("onesf", [128, 128], F32)
        P.dma(onesf[:], g["ones"][:, :], [], [b_c])
        vall = sb("vall", [128, NB, 8 * 65], BF16)
        b_vall = P.buf()
        P.dma(vall[:], g["v1"].rearrange("(b p) c -> p b c", p=128), [g["b_v1"]], [b_vall])
        qT = [sb("qT%d" % i, [96, T], BF16) for i in range(2)]
        kT = [sb("kT%d" % i, [96, T], BF16) for i in range(2)]
        b_qk = [P.buf() for _ in range(2)]
        pS = [ps("pS%d" % i, [128, 512], F32) for i in range(3)]
        b_pS = [P.buf() for _ in range(3)]
        pO = [ps("pO%d" % i, [128, 512], F32) for i in range(2)]
        b_pO = [P.buf() for _ in range(2)]
        pB = ps("pB", [128, 512], F32)
        b_pB = P.buf()
        pT = [sb("pT%d" % i, [128, 512], BF16) for i in range(3)]
        b_pT = [P.buf() for _ in range(3)]
        osb = [sb("osb%d" % i, [128, 512], F32) for i in range(2)]
        b_osb = [P.buf() for _ in range(2)]
        on = [sb("on%d" % i, [64, 512], BF16) for i in range(2)]
        b_on = [P.buf() for _ in range(2)]
        ns = 0
        no = 0
        for h in range(8):
            i = h % 2
            P.dma(qT[i][:], g["qT"][h], [g["b_qT"]], [b_qk[i]])
            P.dma(kT[i][:], g["kT"][h], [g["b_kT"]], [b_qk[i]])
            for gq in range(NG):
                o = no % 2
                no += 1
                nkb = 4 * gq + 4
                for j in range(nkb):
                    jj = j - 4 * gq
                    q0 = gq * 512 + (jj * 128 if jj > 0 else 0)
                    N = (gq + 1) * 512 - q0
                    s = ns % 3
                    ns += 1
                    P.op("pe", lambda e, i=i, s=s, j=j, q0=q0, N=N: e.matmul(
                        pS[s][:, 0:N], kT[i][:, j * 128:(j + 1) * 128], qT[i][:, q0:q0 + N], start=True, stop=True),
                        [b_qk[i]], [b_pS[s]])
                    P.op("act", lambda e, s=s, N=N: e.activation(pT[s][:, 0:N], pS[s][:, 0:N], AF.Exp, scale=scale),
                         [b_pS[s]], [b_pT[s]])
                    if jj >= 0:
                        P.op("dve", lambda e, s=s: e.tensor_tensor(pT[s][:, 0:128], pT[s][:, 0:128], tri[:], ALU.mult),
                             [b_pT[s], b_c], [b_pT[s]])
                    oc = q0 - gq * 512
                    P.op("pe", lambda e, o=o, s=s, j=j, h=h, oc=oc, N=N, nkb=nkb: e.matmul(
                        pO[o][0:65, oc:oc + N], vall[:, j, h * 65:(h + 1) * 65], pT[s][:, 0:N],
                        start=(j == 0), stop=(j == nkb - 1)), [b_vall, b_pT[s]], [b_pO[o]])
                P.op("act", lambda e, o=o: e.copy(osb[o][0:65, :], pO[o][0:65, :]), [b_pO[o]], [b_osb[o]])
                P.op("dve", lambda e, o=o: e.reciprocal(osb[o][64:65, :], osb[o][64:65, :]), [b_osb[o]], [b_osb[o]])
                P.op("pe", lambda e, o=o: e.matmul(pB[0:64, :], onesf[64:65, 0:64], osb[o][64:65, :], start=True, stop=True),
                     [b_osb[o], b_c], [b_pB])
                P.op("dve", lambda e, o=o: e.tensor_tensor(on[o][:], osb[o][0:64, :], pB[0:64, :], ALU.mult),
                     [b_osb[o], b_pB], [b_on[o]])
                P.dma(g["omlaT"][h * 64:(h + 1) * 64, gq * 512:(gq + 1) * 512], on[o][:], [b_on[o]], [g["b_omlaT"]])
        P.flush()


def build_program(T, nlayers=DEPTH, upto="all", debug=False):
    nc = bass.Bass("TRN2", target_bir_lowering=False)
    g = {}
    x_ap = nc.dram_tensor("x", [T, D], F32, kind="ExternalInput").ap()
    g["positions"] = nc.dram_tensor("positions", [128, T // 128], I32, kind="ExternalInput").ap()
    for k, shp in PARAM_SHAPES.items():
        g[k] = nc.dram_tensor(k, list(shp), F32, kind="ExternalInput").ap()
    for k, shp in CONST_SHAPES.items():
        g[k] = nc.dram_tensor(k, list(shp), F32, kind="ExternalInput").ap()
    y_ap = nc.dram_tensor("y", [T, D], F32, kind="ExternalOutput").ap()
    skind = "ExternalOutput" if debug else "Internal"

    def scratch(name, shape, dt):
        return nc.dram_tensor(name, shape, dt, kind=skind).ap()

    P = Prog(nc)
    g["skip_rwkv"] = SKIP_RWKV
    g["proj"] = scratch("proj", [T, IN_DIM], F32)
    g["qT"] = scratch("qT", [8, 96, T], BF16)
    g["kT"] = scratch("kT", [8, 96, T], BF16)
    g["v1"] = scratch("v1", [T, 8 * 65], BF16)
    g["omlaT"] = scratch("omlaT", [512, T], BF16)
    g["orwkvT"] = scratch("orwkvT", [512, T], BF16)
    g["ossmT"] = scratch("ossmT", [1024, T], BF16)
    g["vfirst"] = scratch("vfirst", [T, RW], F32)
    g["xa"] = scratch("xa", [T, D], F32)
    g["xb"] = scratch("xb", [T, D], F32)
    for k in ("proj", "qT", "kT", "v1", "omlaT", "orwkvT", "ossmT", "vfirst", "xa", "xb", "y", "x"):
        g["b_" + k] = P.buf(k)
    order = ["inproj", "mla_prep", "mla_attn", "rwkv", "ssm", "merge", "ffn"]
    stop = order.index(upto) if upto != "all" else len(order) - 1
    cur, b_cur = x_ap, g["b_x"]
    last_bufs = []
    for l in range(nlayers):
        final_layer = (l == nlayers - 1)
        phase_inproj(P, g, l, cur, b_cur, T)
        last_bufs = [g["b_proj"]]
        if stop >= 1 and "mla" not in SKIP:
            phase_mla_prep(P, g, l, T)
            last_bufs = [g["b_qT"], g["b_kT"], g["b_v1"]]
        if stop >= 2 and "mla" not in SKIP:
            phase_mla_attn(P, g, l, T)
            last_bufs = [g["b_omlaT"]]
        if stop >= 3 and not g.get("skip_rwkv"):
            phase_rwkv(P, g, l, T)
            last_bufs.append(g["b_orwkvT"])
            last_bufs.append(g["b_vfirst"])
        if stop >= 4 and "ssm" not in SKIP:
            phase_ssm(P, g, l, T)
            last_bufs.append(g["b_ossmT"])
        if stop >= 5:
            phase_merge(P, g, l, cur, b_cur, g["xa"], g["b_xa"], T)
            last_bufs = [g["b_xa"]]
        if stop >= 6:
            dst, b_dst = (y_ap, g["b_y"]) if final_layer else (g["xb"], g["b_xb"])
            phase_ffn(P, g, l, g["xa"], g["b_xa"], dst, b_dst, T)
            cur, b_cur = dst, b_dst
            last_bufs = [b_dst]
    for eng in ("sp", "pool"):
        P.final_wait(eng, last_bufs + [g["b_" + k] for k in ("proj", "qT", "kT", "v1", "omlaT", "orwkvT", "ossmT", "vfirst", "xa", "xb", "y")])
    P.ops["sp"].append(("o", lambda e: e.nop(), P.sem["sp"], 1))
    P.flush()
    return nc, P


def phase_merge(P, g, l, x_in, b_xin_d, x_out, b_xout_d, T):
    nc = P.nc
    NB = T // 128
    proj, b_proj = g["proj"], g["b_proj"]
    with ExitStack() as es:
        def sb(name, shape, dt):
            return es.enter_context(nc.sbuf_tensor("p5_%d_" % l + name, shape, dt))

        def ps(name, shape, dt):
            return es.enter_context(nc.psum_tensor("p5_%d_" % l + name, shape, dt))

        identf, ident, b_ident = load_ident(P, nc, es, g, "p5_%d_" % l)
        wst = [sb("wst%d" % i, [128, 1024], F32) for i in range(2)]
        b_wst = [P.buf() for _ in range(2)]
        wbr = sb("wbr", [128, 16, 1024], BF16)
        wout = sb("wout", [128, 8, 1024], BF16)
        b_w = P.buf()
        n = 0
        for (nm, k0, nk) in (("w_br_mla", 0, 4), ("w_br_rwkv", 4, 4), ("w_br_ssm", 8, 8)):
            for kc in range(nk):
                i = n % 2
                n += 1
                P.dma(wst[i][:], g[nm][l, kc * 128:(kc + 1) * 128, :], [], [b_wst[i]])
                cast_split(P, n, wbr[:, k0 + kc, :], wst[i][:], [b_wst[i]], [b_w])
        for kc in range(8):
            i = n % 2
            n += 1
            P.dma(wst[i][:], g["w_out"][l, kc * 128:(kc + 1) * 128, :], [], [b_wst[i]])
            cast_split(P, n, wout[:, kc, :], wst[i][:], [b_wst[i]], [b_w])

        pg = [sb("pg%d" % i, [128, 3072], F32) for i in range(2)]
        b_pg = [P.buf() for _ in range(2)]
        oT = [sb("oT%d" % i, [128, 16, 128], BF16) for i in range(2)]
        b_oT = [P.buf() for _ in range(2)]
        xin = [sb("xin%d" % i, [128, D], F32) for i in range(2)]
        b_xin = [P.buf() for _ in range(2)]
        py = [ps("py%d" % i, [128, 1024], F32) for i in range(3)]
        b_py = [P.buf() for _ in range(3)]
        ptr = ps("ptr", [128, 8, 128], BF16)
        b_ptr = P.buf()
        mg = sb("mg", [128, 1024], F32)
        tmp = sb("tmp", [128, 1024], F32)
        b_mg = P.buf()
        b_tmp = P.buf()
        mb = sb("mb", [128, 1024], BF16)
        b_mb = P.buf()
        mT = sb("mT", [128, 8, 128], BF16)
        b_mT = P.buf()
        xo = [sb("xo%d" % i, [128, D], F32) for i in range(2)]
        b_xo = [P.buf() for _ in range(2)]
        for tb in range(NB):
            i = tb % 2
            ts = slice(tb * 128, (tb + 1) * 128)
            P.dma(pg[i][:], proj[ts, C_GATE:IN_DIM], [b_proj], [b_pg[i]])
            P.dma(oT[i][:, 0:4, :], g["omlaT"][:, ts].rearrange("(k p) t -> p k t", p=128), [g["b_omlaT"]], [b_oT[i]])
            P.dma(oT[i][:, 4:8, :], g["orwkvT"][:, ts].rearrange("(k p) t -> p k t", p=128), [g["b_orwkvT"]], [b_oT[i]])
            P.dma(oT[i][:, 8:16, :], g["ossmT"][:, ts].rearrange("(k p) t -> p k t", p=128), [g["b_ossmT"]], [b_oT[i]])
            P.dma(xin[i][:], x_in[ts, :], [b_xin_d], [b_xin[i]])
            P.op("act", lambda e, i=i: e.activation(pg[i][:], pg[i][:], AF.Sigmoid), [b_pg[i]], [b_pg[i]])
            for br, (k0, nk) in enumerate(((0, 4), (4, 4), (8, 8))):
                def mm(e, i=i, br=br, k0=k0, nk=nk):
                    r = None
                    for c0 in (0, 512):
                        for kc in range(nk):
                            r = e.matmul(py[br][:, c0:c0 + 512], oT[i][:, k0 + kc, :], wbr[:, k0 + kc, c0:c0 + 512],
                                         start=(kc == 0), stop=(kc == nk - 1))
                    return r
                P.op("pe", mm, [b_oT[i], b_w], [b_py[br]])
            P.op("dve", lambda e, i=i: e.tensor_tensor(mg[:], py[0][:], pg[i][:, 0:1024], ALU.mult),
                 [b_py[0], b_pg[i]], [b_mg])
            P.op("dve", lambda e, i=i: e.tensor_tensor(tmp[:], py[1][:], pg[i][:, 1024:2048], ALU.mult),
                 [b_py[1], b_pg[i]], [b_tmp])
            P.op("pool", lambda e: e.tensor_tensor(mg[:], mg[:], tmp[:], ALU.add), [b_mg, b_tmp], [b_mg])
            P.op("dve", lambda e, i=i: e.tensor_tensor(tmp[:], py[2][:], pg[i][:, 2048:3072], ALU.mult),
                 [b_py[2], b_pg[i]], [b_tmp])
            P.op("pool", lambda e: e.tensor_tensor(mb[:], mg[:], tmp[:], ALU.add), [b_mg, b_tmp], [b_mb])

            def tr(e):
                r = None
                for kc in range(8):
                    r = e.transpose(ptr[:, kc, :], mb[:, kc * 128:(kc + 1) * 128], ident[:])
                return r
            P.op("pe", tr, [b_mb, b_ident], [b_ptr])
            P.op("act", lambda e: e.copy(mT[:], ptr[:]), [b_ptr], [b_mT])

            def mmo(e):
                r = None
                for c0 in (0, 512):
                    for kc in range(8):
                        r = e.matmul(py[0][:, c0:c0 + 512], mT[:, kc, :], wout[:, kc, c0:c0 + 512],
                                     start=(kc == 0), stop=(kc == 7))
                return r
            P.op("pe", mmo, [b_mT, b_w], [b_py[0]])
            P.op("dve", lambda e, i=i: e.tensor_tensor(xo[i][:], py[0][:], xin[i][:], ALU.add),
                 [b_py[0], b_xin[i]], [b_xo[i]])
            P.dma(x_out[ts, :], xo[i][:], [b_xo[i]], [b_xout_d])
        P.flush()


def phase_ssm(P, g, l, T):
    nc = P.nc
    NB = T // 128
    proj, b_proj = g["proj"], g["b_proj"]
    CX = C_SSM + SSM_D
    CDT = CX + SSM_CONV_DIM
    with ExitStack() as es:
        def sb(name, shape, dt):
            return es.enter_context(nc.sbuf_tensor("p4_%d_" % l + name, shape, dt))

        def ps(name, shape, dt):
            return es.enter_context(nc.psum_tensor("p4_%d_" % l + name, shape, dt))

        identf, ident, b_ident = load_ident(P, nc, es, g, "p4_%d_" % l)
        b_c = P.buf()
        cw = sb("cw", [128, 4, SSM_CONV_DIM], F32)
        cb = sb("cb", [128, SSM_CONV_DIM], F32)
        for j in range(4):
            P.dma(cw[:, j, :], g["ssm_conv_w"][l, j].partition_broadcast(128), [], [b_c])
        P.dma(cb[:], g["ssm_conv_b"][l].partition_broadcast(128), [], [b_c])
        sm = sb("sm", [128, 48], F32)
        P.dma(sm[:, 0:16], g["ssm_dt_bias"][l].partition_broadcast(128), [], [b_c])
        P.dma(sm[:, 16:32], g["ssm_a_log"][l].partition_broadcast(128), [], [b_c])
        P.dma(sm[:, 32:48], g["ssm_d"][l].partition_broadcast(128), [], [b_c])
        P.op("act", lambda e: e.activation(sm[:, 16:32], sm[:, 16:32], AF.Exp), [b_c], [b_c])
        P.op("dve", lambda e: e.tensor_scalar(sm[:, 16:32], sm[:, 16:32], -1.0, None, ALU.mult), [b_c], [b_c])
        ng = sb("ng", [128, SSM_D], F32)
        P.dma(ng[:], g["ssm_norm_g"][l].partition_broadcast(128), [], [b_c])
        tri_incl = sb("tri_incl", [128, 128], F32)
        tri_gt = sb("tri_gt", [128, 128], F32)
        onesf = sb("onesf", [128, 128], F32)
        P.dma(tri_incl[:], g["tri_incl"][:, :], [], [b_c])
        P.dma(tri_gt[:], g["tri_gt"][:, :], [], [b_c])
        P.dma(onesf[:], g["ones"][:, :], [], [b_c])

        xs4 = [sb("xs4_%d" % j, [128, SSM_CONV_DIM], F32) for j in range(4)]
        b_xs4 = [P.buf() for _ in range(4)]
        xc = sb("xc", [128, SSM_CONV_DIM], F32)
        b_xc = P.buf()
        zt = sb("zt", [128, SSM_D], F32)
        b_zt = P.buf()
        dtt = sb("dtt", [128, 128], F32)
        b_dt = P.buf()
        pA = ps("pA", [128, 512], F32)
        b_pA = P.buf()
        ptr = ps("ptr", [128, 8, 128], BF16)
        b_ptr = P.buf()
        pD = [ps("pD%d" % i, [128, 512], F32) for i in range(2)]
        b_pD = [P.buf() for _ in range(2)]
        pY = ps("pY", [128, 1024], F32)
        b_pY = P.buf()
        pOff = ps("pOff", [128, 512], F32)
        b_pOff = P.buf()
        pSt = ps("pSt", [128, 512], F32)
        b_pSt = P.buf()
        xdt = sb("xdt", [128, SSM_D], BF16)
        b_xdt = P.buf()
        xdte = sb("xdte", [128, SSM_D], BF16)
        b_xdte = P.buf()
        bcb = sb("bcb", [128, 512], BF16)
        b_bcb = P.buf()
        bcT = sb("bcT", [128, 4, 128], BF16)
        b_bcT = P.buf()
        cbm = sb("cbm", [128, 2, 128], F32)
        b_cbm = P.buf()
        rhsd = sb("rhsd", [128, 16, 128], F32)
        b_rhsd = P.buf()
        eD = [sb("eD%d" % i, [128, 4, 128], F32) for i in range(2)]
        b_eD = [P.buf() for _ in range(2)]
        MT = sb("MT", [128, 16, 128], BF16)
        b_MT = P.buf()
        ST = sb("ST", [128, SSM_D], F32)
        STb = sb("STb", [128, SSM_D], BF16)
        b_ST = P.buf()
        b_STb = P.buf()
        yb = sb("yb", [128, SSM_D], F32)
        b_yb = P.buf()
        t2 = sb("t2", [128, SSM_D], F32)
        b_t2 = P.buf()
        ssg = sb("ssg", [128, 2], F32)
        b_ssg = P.buf()
        yn = sb("yn", [128, SSM_D], BF16)
        b_yn = P.buf()
        yT = [sb("yT%d" % i, [128, 8, 128], BF16) for i in range(2)]
        b_yT = [P.buf() for _ in range(2)]
        P.op("pool", lambda e: e.memset(ST[:], 0.0), [], [b_ST])
        P.op("pool", lambda e: e.memset(STb[:], 0.0), [], [b_STb])

        for tb in range(NB):
            t0 = tb * 128
            for j in range(4):
                sh = 3 - j
                if tb == 0 and sh > 0:
                    P.op("pool", lambda e, j=j: e.memset(xs4[j][0:4, :], 0.0), [], [b_xs4[j]])
                    P.dma(xs4[j][sh:128, :], proj[0:128 - sh, CX:CX + SSM_CONV_DIM], [b_proj], [b_xs4[j]])
                else:
                    P.dma(xs4[j][:], proj[t0 - sh:t0 - sh + 128, CX:CX + SSM_CONV_DIM], [b_proj], [b_xs4[j]])
            P.dma(zt[:], proj[t0:t0 + 128, C_SSM:C_SSM + SSM_D], [b_proj], [b_zt])
            P.dma(dtt[:, 0:16], proj[t0:t0 + 128, CDT:CDT + 16], [b_proj], [b_dt])
            P.op("dve", lambda e: e.tensor_tensor(xs4[0][:], xs4[0][:], cw[:, 0, :], ALU.mult), [b_xs4[0], b_c], [b_xs4[0]])
            P.op("pool", lambda e: e.tensor_tensor(xs4[1][:], xs4[1][:], cw[:, 1, :], ALU.mult), [b_xs4[1], b_c], [b_xs4[1]])
            P.op("dve", lambda e: e.tensor_tensor(xs4[2][:], xs4[2][:], cw[:, 2, :], ALU.mult), [b_xs4[2], b_c], [b_xs4[2]])
            P.op("pool", lambda e: e.tensor_tensor(xs4[3][:], xs4[3][:], cw[:, 3, :], ALU.mult), [b_xs4[3], b_c], [b_xs4[3]])
            P.op("dve", lambda e: e.tensor_tensor(xs4[0][:], xs4[0][:], xs4[1][:], ALU.add), [b_xs4[0], b_xs4[1]], [b_xs4[0]])
            P.op("pool", lambda e: e.tensor_tensor(xs4[2][:], xs4[2][:], xs4[3][:], ALU.add), [b_xs4[2], b_xs4[3]], [b_xs4[2]])
            P.op("dve", lambda e: e.tensor_tensor(xs4[0][:], xs4[0][:], xs4[2][:], ALU.add), [b_xs4[0], b_xs4[2]], [b_xs4[0]])
            P.op("pool", lambda e: e.tensor_tensor(xs4[0][:], xs4[0][:], cb[:], ALU.add), [b_xs4[0], b_c], [b_xs4[0]])
            P.op("act", lambda e: e.activation(xc[:], xs4[0][:], AF.Silu), [b_xs4[0]], [b_xc])
            P.op("act", lambda e: e.activation(zt[:], zt[:], AF.Silu), [b_zt], [b_zt])
            P.op("dve", lambda e: e.tensor_tensor(dtt[:, 0:16], dtt[:, 0:16], sm[:, 0:16], ALU.add), [b_dt, b_c], [b_dt])
            P.op("act", lambda e: e.activation(dtt[:, 0:16], dtt[:, 0:16], AF.Exp), [b_dt], [b_dt])
            P.op("act", lambda e: e.activation(dtt[:, 0:16], dtt[:, 0:16], AF.Ln, bias=1.0), [b_dt], [b_dt])
            P.op("dve", lambda e: e.tensor_tensor(dtt[:, 16:32], dtt[:, 0:16], sm[:, 16:32], ALU.mult), [b_dt, b_c], [b_dt])

            def mmcs(e):
                e.matmul(pA[:, 0:16], tri_incl[:], dtt[:, 16:32], start=True, stop=True)
                return e.matmul(pA[:, 16:32], onesf[:], dtt[:, 16:32], start=True, stop=True)
            P.op("pe", mmcs, [b_dt, b_c], [b_pA])
            P.op("dve", lambda e: e.tensor_copy(dtt[:, 96:128], pA[:, 0:32]), [b_pA], [b_dt])
            P.op("act", lambda e: e.activation(dtt[:, 32:48], dtt[:, 96:112], AF.Exp), [b_dt], [b_dt])
            P.op("dve", lambda e: e.tensor_tensor(dtt[:, 48:64], dtt[:, 112:128], dtt[:, 96:112], ALU.subtract), [b_dt], [b_dt])
            P.op("act", lambda e: e.activation(dtt[:, 48:64], dtt[:, 48:64], AF.Exp), [b_dt], [b_dt])
            P.op("act", lambda e: e.activation(dtt[:, 64:80], dtt[:, 112:128], AF.Exp), [b_dt], [b_dt])
            P.op("dve", lambda e: e.tensor_tensor(dtt[:, 80:96], dtt[:, 0:16], dtt[:, 48:64], ALU.mult), [b_dt], [b_dt])
            xc3 = xc[:, 0:SSM_D].rearrange("p (e d) -> p e d", e=16)
            P.op("dve", lambda e: e.tensor_tensor(xdt[:].rearrange("p (e d) -> p e d", e=16), xc3,
                                                  bcast_last(dtt[:, 0:16], 64), ALU.mult), [b_xc, b_dt], [b_xdt])
            P.op("pool", lambda e: e.tensor_tensor(xdte[:].rearrange("p (e d) -> p e d", e=16), xc3,
                                                   bcast_last(dtt[:, 80:96], 64), ALU.mult), [b_xc, b_dt], [b_xdte])
            P.op("pool", lambda e: e.tensor_copy(bcb[:], xc[:, SSM_D:SSM_D + 512]), [b_xc], [b_bcb])

            def trbc(e):
                r = None
                for k in range(4):
                    r = e.transpose(ptr[:, k, :], bcb[:, k * 128:(k + 1) * 128], ident[:])
                return r
            P.op("pe", trbc, [b_bcb, b_ident], [b_ptr])
            P.op("act", lambda e: e.copy(bcT[:], ptr[:, 0:4, :]), [b_ptr], [b_bcT])

            def mmcb(e):
                e.matmul(pA[:, 128:256], bcT[:, 0, :], bcT[:, 2, :], start=True, stop=True)
                return e.matmul(pA[:, 256:384], bcT[:, 1, :], bcT[:, 3, :], start=True, stop=True)
            P.op("pe", mmcb, [b_bcT], [b_pA])
            P.op("dve", lambda e: e.tensor_tensor(cbm[:], pA[:, 128:384].rearrange("p (g l) -> p g l", g=2),
                                                  bcast_mid(tri_incl[:], 2), ALU.mult), [b_pA, b_c], [b_cbm])
            P.op("pool", lambda e: e.tensor_tensor(rhsd[:], bcast_mid(tri_incl[:], 16), bcast_last(dtt[:, 16:32], 128),
                                                   ALU.mult), [b_dt, b_c], [b_rhsd])
            for q4 in range(4):
                k = q4 % 2
                P.op("pe", lambda e, q4=q4, k=k: e.matmul(pD[k][:], tri_gt[:],
                                                          rhsd[:, q4 * 4:(q4 + 1) * 4, :].rearrange("p e l -> p (e l)"),
                                                          start=True, stop=True), [b_rhsd, b_c], [b_pD[k]])
                P.op("act", lambda e, k=k: e.activation(eD[k][:].rearrange("p e l -> p (e l)"), pD[k][:], AF.Exp),
                     [b_pD[k]], [b_eD[k]])
                gi = q4 // 2
                P.op("dve", lambda e, q4=q4, k=k, gi=gi: e.tensor_tensor(MT[:, q4 * 4:(q4 + 1) * 4, :], eD[k][:],
                                                                        bcast_mid(cbm[:, gi, :], 4), ALU.mult),
                     [b_eD[k], b_cbm], [b_MT])

            def mmy(e):
                r = None
                for hh in range(16):
                    r = e.matmul(pY[:, hh * 64:(hh + 1) * 64], MT[:, hh, :], xdt[:, hh * 64:(hh + 1) * 64],
                                 start=True, stop=True)
                return r
            P.op("pe", mmy, [b_MT, b_xdt], [b_pY])
            for gi in range(2):
                P.op("pe", lambda e, gi=gi: e.matmul(pOff[:], bcT[:, 2 + gi, :], STb[:, gi * 512:(gi + 1) * 512],
                                                     start=True, stop=True), [b_bcT, b_STb], [b_pOff])
                P.op("dve", lambda e, gi=gi: e.tensor_tensor(
                    yb[:, gi * 512:(gi + 1) * 512].rearrange("p (e d) -> p e d", e=8),
                    pOff[:].rearrange("p (e d) -> p e d", e=8),
                    bcast_last(dtt[:, 32 + gi * 8:32 + (gi + 1) * 8], 64), ALU.mult), [b_pOff, b_dt], [b_yb])
            P.op("dve", lambda e: e.tensor_tensor(yb[:], yb[:], pY[:], ALU.add), [b_yb, b_pY], [b_yb])
            P.op("pool", lambda e: e.tensor_tensor(t2[:].rearrange("p (e d) -> p e d", e=16), xc3,
                                                   bcast_last(sm[:, 32:48], 64), ALU.mult), [b_xc, b_c], [b_t2])
            P.op("pool", lambda e: e.tensor_tensor(yb[:], yb[:], t2[:], ALU.add), [b_yb, b_t2], [b_yb])
            P.op("dve", lambda e: e.tensor_tensor(yb[:], yb[:], zt[:], ALU.mult), [b_yb, b_zt], [b_yb])
            for gi in range(2):
                P.op("act", lambda e, gi=gi: e.activation(t2[:, gi * 512:(gi + 1) * 512], yb[:, gi * 512:(gi + 1) * 512],
                                                          AF.Square, accum_out=ssg[:, gi:gi + 1]), [b_yb], [b_t2, b_ssg])
            rstd_from_sumsq(P, ssg[:], ssg[:], 512, 1e-5, [b_ssg], [b_ssg])
            for gi in range(2):
                P.op("dve", lambda e, gi=gi: e.scalar_tensor_tensor(
                    yn[:, gi * 512:(gi + 1) * 512], yb[:, gi * 512:(gi + 1) * 512], ssg[:, gi:gi + 1],
                    ng[:, gi * 512:(gi + 1) * 512], ALU.mult, ALU.mult), [b_yb, b_ssg, b_c], [b_yn])

            def tro(e):
                r = None
                for kc in range(8):
                    r = e.transpose(ptr[:, kc, :], yn[:, kc * 128:(kc + 1) * 128], ident[:])
                return r
            P.op("pe", tro, [b_yn, b_ident], [b_ptr])
            i = tb % 2
            P.op("act", lambda e, i=i: e.copy(yT[i][:], ptr[:]), [b_ptr], [b_yT[i]])
            P.dma(g["ossmT"][:, t0:t0 + 128].rearrange("(k p) t -> p k t", p=128), yT[i][:], [b_yT[i]], [g["b_ossmT"]])
            for gi in range(2):
                P.op("pe", lambda e, gi=gi: e.matmul(pSt[:], bcb[:, gi * 128:(gi + 1) * 128],
                                                     xdte[:, gi * 512:(gi + 1) * 512], start=True, stop=True),
                     [b_bcb, b_xdte], [b_pSt])
                P.op("dve", lambda e, gi=gi: e.tensor_tensor(
                    ST[:, gi * 512:(gi + 1) * 512].rearrange("p (e d) -> p e d", e=8),
                    ST[:, gi * 512:(gi + 1) * 512].rearrange("p (e d) -> p e d", e=8),
                    bcast_last(dtt[:, 64 + gi * 8:64 + (gi + 1) * 8], 64), ALU.mult), [b_ST, b_dt], [b_ST])
                P.op("dve", lambda e, gi=gi: e.tensor_tensor(ST[:, gi * 512:(gi + 1) * 512], ST[:, gi * 512:(gi + 1) * 512],
                                                            pSt[:], ALU.add), [b_ST, b_pSt], [b_ST])
            P.op("pool", lambda e: e.tensor_copy(STb[:], ST[:]), [b_ST], [b_STb])
        P.flush()


def phase_rwkv(P, g, l, T):
    nc = P.nc
    NB = T // 128
    proj, b_proj = g["proj"], g["b_proj"]
    NEG_E = -math.exp(-0.5)
    with ExitStack() as es:
        def sb(name, shape, dt):
            return es.enter_context(nc.sbuf_tensor("p7_%d_" % l + name, shape, dt))

        def ps(name, shape, dt):
            return es.enter_context(nc.psum_tensor("p7_%d_" % l + name, shape, dt))

        identf, ident, b_ident = load_ident(P, nc, es, g, "p7_%d_" % l)
        b_c = P.buf()

        def const(name, key):
            t = sb(name, [128, 128], F32)
            P.dma(t[:], g[key][:, :], [], [b_c])
            return t
        tri_incl_bd = const("tri_incl_bd", "tri_incl_bd")
        tri_excl_bd = const("tri_excl_bd", "tri_excl_bd")
        tri_after_bd = const("tri_after_bd", "tri_after_bd")
        low_strict_bd = const("low_strict_bd", "low_strict_bd")
        onesf = const("onesf", "ones")

        def bparam(name, ap):
            t = sb(name, [128, RW], F32)
            P.dma(t[:], ap.partition_broadcast(128), [], [b_c])
            return t
        w0 = bparam("w0", g["rwkv_w0"][l])
        a0 = bparam("a0", g["rwkv_a0"][l])
        k_k = bparam("k_k", g["rwkv_k_k"][l])
        k_a = bparam("k_a", g["rwkv_k_a"][l])
        r_k = bparam("r_k", g["rwkv_r_k"][l])
        ln_g = bparam("ln_g", g["rwkv_ln_g"][l])
        ln_b = bparam("ln_b", g["rwkv_ln_b"][l])
        w2p = sb("w2p", [128, RW], F32)
        a2p = sb("a2p", [128, RW], F32)
        P.op("pool", lambda e: e.memset(w2p[:], 0.0), [], [b_c])
        P.op("pool", lambda e: e.memset(a2p[:], 0.0), [], [b_c])
        P.dma(w2p[0:64, :], g["rwkv_w2"][l], [], [b_c])
        P.dma(a2p[64:128, :], g["rwkv_a2"][l], [], [b_c])
        g2 = sb("g2", [128, RW], F32)
        P.dma(g2[:], g["rwkv_g2"][l], [], [b_c])
        if l > 0:
            v0 = bparam("v0", g["rwkv_v0"][l - 1])
            v1t = sb("v1t", [128, 4, 32], F32)
            P.dma(v1t[:], g["rwkv_v1"][l - 1].rearrange("(k p) c -> p k c", p=128), [], [b_c])
            v2t = sb("v2t", [32, RW], F32)
            P.dma(v2t[:], g["rwkv_v2"][l - 1], [], [b_c])

        pb = [ps("pb%d" % i, [128, 512], F32) for i in range(8)]
        b_pb = [P.buf("pb%d" % i) for i in range(8)]
        ptrb = None

        def t512(name):
            return sb(name, [128, RW], F32), P.buf(name)
        pr = sb("pr", [128, RWKV_IN], F32)
        b_pr = P.buf()
        lx, b_lx = sb("lx", [128, 256], F32), P.buf()
        lxT, b_lxT = sb("lxT", [128, 2, 128], F32), P.buf()
        lw, b_lw = t512("lw")
        asg, b_asg = t512("asg")
        gg, b_gg = t512("gg")
        kkn, b_kkn = t512("kkn")
        kf, b_kf = t512("kf")
        bv, b_bv = t512("bv")
        tmp, b_tmp = t512("tmp")
        tmp2, b_tmp2 = t512("tmp2")
        E1, b_E1 = t512("E1")
        E2, b_E2 = t512("E2")
        E3, b_E3 = t512("E3")
        Ee, b_Ee = t512("Ee")
        At, b_At = t512("At")
        Bt, b_Bt = t512("Bt")
        Kt, b_Kt = t512("Kt")
        Rt, b_Rt = t512("Rt")
        Bend, b_Bend = t512("Bend")
        Kend, b_Kend = t512("Kend")
        hs, b_hs = sb("hs", [128, 32], F32), P.buf()
        gCt, b_gCt = sb("gCt", [128, 8], F32), P.buf()
        ART, b_ART = sb("ART", [128, 4, 2, 128], F32), P.buf()
        BtT, b_BtT = sb("BtT", [128, 8, 128], F32), P.buf()
        KtT, b_KtT = sb("KtT", [128, 8, 128], F32), P.buf()
        P.op("pool", lambda e: e.memset(BtT[:], 0.0), [], [b_BtT])
        P.op("pool", lambda e: e.memset(KtT[:], 0.0), [], [b_KtT])

        def t8(name):
            return sb(name, [128, 8, 128], F32), P.buf(name)
        Pm = [t8("Pm0"), t8("Pm1")]
        PTm = [t8("PTm0"), t8("PTm1")]
        TTm = [t8("TTm0"), t8("TTm1")]
        MakT, b_MakT = t8("MakT")
        MrbT, b_MrbT = t8("MrbT")
        MrkT, b_MrkT = t8("MrkT")
        S0T, b_S0T = sb("S0T", [128, 8, 64], F32), P.buf()
        Xs, b_Xs = t512("Xs")
        SAc = [t512("SAs0"), t512("SAs1")]
        Vc = [t512("Vc0"), t512("Vc1")]
        P.op("pool", lambda e: e.memset(Xs[:], 0.0), [], [b_Xs])
        for c_ in range(2):
            P.op("pool", lambda e, c_=c_: e.memset(SAc[c_][0][:], 0.0), [], [SAc[c_][1]])
            P.op("pool", lambda e, c_=c_: e.memset(Vc[c_][0][:], 0.0), [], [Vc[c_][1]])
        Ys, b_Ys = t512("Ys")
        yo, b_yo = sb("yo", [128, RW], BF16), P.buf()
        ptrO = None
        oT = [sb("oT%d" % i, [128, 4, 128], BF16) for i in range(2)]
        b_oT = [P.buf() for _ in range(2)]
        if l > 0:
            vT, b_vT = sb("vT", [128, 4, 128], F32), P.buf()
            vv, b_vv = sb("vv", [128, 32], F32), P.buf()
            vvT, b_vvT = sb("vvT", [32, 128], F32), P.buf()
            vf, b_vf = t512("vf")
        P.op("pool", lambda e: e.memset(S0T[:], 0.0), [], [b_S0T])

        r_ = pr[:, 0:512]
        k_ = pr[:, 512:1024]
        v_ = pr[:, 1024:1536]

        def h3(ap):
            return ap.rearrange("p (h d) -> p h d", h=8)

        for tb in range(NB):
            t0 = tb * 128
            P.dma(pr[:], proj[t0:t0 + 128, C_RWKV:C_SSM], [b_proj], [b_pr])
            P.op("act", lambda e: e.activation(lx[:, 0:64], pr[:, 1536:1600], AF.Tanh), [b_pr], [b_lx])
            P.op("act", lambda e: e.activation(lx[:, 128:256], pr[:, 1664:1792], AF.Sigmoid), [b_pr], [b_lx])
            P.op("pool", lambda e: e.tensor_copy(lx[:, 64:128], pr[:, 1600:1664]), [b_pr], [b_lx])

            def tr_lx(e):
                e.transpose(pb[0][:, 0:128], lx[:, 0:128], identf[:])
                return e.transpose(pb[0][:, 128:256], lx[:, 128:256], identf[:])
            P.op("pe", tr_lx, [b_lx, b_ident], [b_pb[0]])
            P.op("dve", lambda e: e.tensor_copy(lxT[:].rearrange("p a t -> p (a t)"), pb[0][:, 0:256]), [b_pb[0]], [b_lxT])
            P.op("pe", lambda e: e.matmul(pb[1][:], lxT[:, 0, :], w2p[:], start=True, stop=True),
                 [b_lxT, b_c], [b_pb[1]])
            P.op("pe", lambda e: e.matmul(pb[2][:], lxT[:, 0, :], a2p[:], start=True, stop=True),
                 [b_lxT, b_c], [b_pb[2]])
            P.op("pe", lambda e: e.matmul(pb[3][:], lxT[:, 1, :], g2[:], start=True, stop=True), [b_lxT, b_c], [b_pb[3]])
            P.op("dve", lambda e: e.tensor_tensor(lw[:], pb[1][:], w0[:], ALU.add), [b_pb[1], b_c], [b_lw])
            P.op("act", lambda e: e.activation(lw[:], lw[:], AF.Sigmoid), [b_lw], [b_lw])
            P.op("dve", lambda e: e.tensor_scalar(lw[:], lw[:], NEG_E, None, ALU.mult), [b_lw], [b_lw])
            P.op("dve", lambda e: e.tensor_tensor(asg[:], pb[2][:], a0[:], ALU.add), [b_pb[2], b_c], [b_asg])
            P.op("act", lambda e: e.activation(asg[:], asg[:], AF.Sigmoid), [b_asg], [b_asg])
            P.op("act", lambda e: e.copy(gg[:], pb[3][:]), [b_pb[3]], [b_gg])
            if RWKV_STAGE <= 1:
                continue
            if l == 0:
                P.dma(g["vfirst"][t0:t0 + 128, :], v_, [b_pr], [g["b_vfirst"]])
            else:
                P.dma(vf[:], g["vfirst"][t0:t0 + 128, :], [g["b_vfirst"]], [b_vf])

                def tr_v(e):
                    r = None
                    for kc in range(4):
                        r = e.transpose(pb[0][:, kc * 128:(kc + 1) * 128], pr[:, 1024 + kc * 128:1024 + (kc + 1) * 128], identf[:])
                    return r
                P.op("pe", tr_v, [b_pr, b_ident], [b_pb[0]])
                P.op("act", lambda e: e.copy(vT[:].rearrange("p a t -> p (a t)"), pb[0][:]), [b_pb[0]], [b_vT])

                def mm_v1(e):
                    r = None
                    for kc in range(4):
                        r = e.matmul(pb[1][:, 0:32], vT[:, kc, :], v1t[:, kc, :], start=(kc == 0), stop=(kc == 3))
                    return r
                P.op("pe", mm_v1, [b_vT, b_c], [b_pb[1]])
                P.op("dve", lambda e: e.tensor_copy(vv[:], pb[1][:, 0:32]), [b_pb[1]], [b_vv])
                P.op("pe", lambda e: e.transpose(pb[2][0:32, 0:128], vv[:], identf[:]), [b_vv, b_ident], [b_pb[2]])
                P.op("dve", lambda e: e.tensor_copy(vvT[:], pb[2][0:32, 0:128]), [b_pb[2]], [b_vvT])
                P.op("pe", lambda e: e.matmul(pb[3][:], vvT[:], v2t[:], start=True, stop=True), [b_vvT, b_c], [b_pb[3]])
                P.op("dve", lambda e: e.tensor_tensor(tmp[:], pb[3][:], v0[:], ALU.add), [b_pb[3], b_c], [b_tmp])
                P.op("act", lambda e: e.activation(tmp[:], tmp[:], AF.Sigmoid), [b_tmp], [b_tmp])
                P.op("dve", lambda e: e.tensor_tensor(vf[:], vf[:], v_, ALU.subtract), [b_vf, b_pr], [b_vf])
                P.op("dve", lambda e: e.tensor_tensor(vf[:], vf[:], tmp[:], ALU.mult), [b_vf, b_tmp], [b_vf])
                P.op("dve", lambda e: e.tensor_tensor(v_, v_, vf[:], ALU.add), [b_pr, b_vf], [b_pr])
            P.op("pool", lambda e: e.tensor_tensor(kkn[:], k_, k_k[:], ALU.mult), [b_pr, b_c], [b_kkn])
            P.op("act", lambda e: e.activation(tmp2[:], kkn[:], AF.Square), [b_kkn], [b_tmp2])
            P.op("dve", lambda e: e.tensor_reduce(hs[:, 0:8], h3(tmp2[:]), AX.X, ALU.add), [b_tmp2], [b_hs])
            P.op("act", lambda e: e.activation(hs[:, 0:8], hs[:, 0:8], AF.Sqrt), [b_hs], [b_hs])
            P.op("dve", lambda e: e.tensor_scalar(hs[:, 0:8], hs[:, 0:8], 1e-12, None, ALU.max), [b_hs], [b_hs])
            P.op("dve", lambda e: e.reciprocal(hs[:, 0:8], hs[:, 0:8]), [b_hs], [b_hs])
            P.op("dve", lambda e: e.tensor_tensor(h3(kkn[:]), h3(kkn[:]), bcast_last(hs[:, 0:8], 64), ALU.mult),
                 [b_kkn, b_hs], [b_kkn])
            P.op("dve", lambda e: e.scalar_tensor_tensor(tmp2[:], asg[:], -1.0, k_a[:], ALU.add, ALU.mult),
                 [b_asg, b_c], [b_tmp2])
            P.op("dve", lambda e: e.scalar_tensor_tensor(kf[:], tmp2[:], 1.0, k_, ALU.add, ALU.mult),
                 [b_tmp2, b_pr], [b_kf])
            P.op("pool", lambda e: e.tensor_tensor(bv[:], kkn[:], asg[:], ALU.mult), [b_kkn, b_asg], [b_bv])
            P.op("pool", lambda e: e.tensor_tensor(tmp2[:], r_, kf[:], ALU.mult), [b_pr, b_kf], [b_tmp2])
            P.op("pool", lambda e: e.tensor_tensor(tmp2[:], tmp2[:], r_k[:], ALU.mult), [b_tmp2, b_c], [b_tmp2])
            P.op("dve", lambda e: e.tensor_reduce(hs[:, 8:16], h3(tmp2[:]), AX.X, ALU.add), [b_tmp2], [b_hs])
            if RWKV_STAGE <= 2:
                continue
            P.op("pe", lambda e: e.matmul(pb[4][:], tri_incl_bd[:], lw[:], start=True, stop=True), [b_lw, b_c], [b_pb[4]])
            P.op("pe", lambda e: e.matmul(pb[5][:], tri_after_bd[:], lw[:], start=True, stop=True), [b_lw, b_c], [b_pb[5]])

            def mm_gc(e):
                r = None
                for hp in range(4):
                    r = e.matmul(pb[6][:, hp * 128:(hp + 1) * 128], lw[:, hp * 128:(hp + 1) * 128], tri_incl_bd[:],
                                 start=True, stop=True)
                return r
            P.op("pe", mm_gc, [b_lw, b_c], [b_pb[6]])
            P.op("act", lambda e: e.activation(gCt[:].rearrange("p (a c) -> p a c", a=4),
                                               pb[6][:].rearrange("p (a c t) -> p a c t", a=4, c=2)[:, :, :, 63], AF.Exp),
                 [b_pb[6]], [b_gCt])
            P.op("act", lambda e: e.activation(E1[:], pb[4][:], AF.Exp), [b_pb[4]], [b_E1])
            P.op("act", lambda e: e.activation(E2[:], pb[4][:], AF.Exp, scale=-1.0), [b_pb[4]], [b_E2])
            P.op("dve", lambda e: e.tensor_tensor(E3[:], pb[4][:], lw[:], ALU.subtract), [b_pb[4], b_lw], [b_E3])
            P.op("act", lambda e: e.activation(E3[:], E3[:], AF.Exp), [b_E3], [b_E3])
            P.op("act", lambda e: e.activation(Ee[:], pb[5][:], AF.Exp), [b_pb[5]], [b_Ee])
            P.op("dve", lambda e: e.scalar_tensor_tensor(At[:], kkn[:], -1.0, E3[:], ALU.mult, ALU.mult),
                 [b_kkn, b_E3], [b_At])
            P.op("pool", lambda e: e.tensor_tensor(Bt[:], bv[:], E2[:], ALU.mult), [b_bv, b_E2], [b_Bt])
            P.op("dve", lambda e: e.tensor_tensor(Kt[:], kf[:], E2[:], ALU.mult), [b_kf, b_E2], [b_Kt])
            P.op("pool", lambda e: e.tensor_tensor(Rt[:], r_, E1[:], ALU.mult), [b_pr, b_E1], [b_Rt])
            P.op("dve", lambda e: e.tensor_tensor(Bend[:], bv[:], Ee[:], ALU.mult), [b_bv, b_Ee], [b_Bend])
            P.op("pool", lambda e: e.tensor_tensor(Kend[:], kf[:], Ee[:], ALU.mult), [b_kf, b_Ee], [b_Kend])
            for qi, (src, b_src) in enumerate(((At, b_At), (Rt, b_Rt), (Bt, b_Bt), (Kt, b_Kt))):
                def tr4(e, src=src, qi=qi):
                    r = None
                    for hp in range(4):
                        r = e.transpose(pb[qi][:, hp * 128:(hp + 1) * 128], src[:, hp * 128:(hp + 1) * 128], identf[:])
                    return r
                P.op("pe", tr4, [b_src, b_ident], [b_pb[qi]])
            P.op("act", lambda e: e.copy(ART[:, :, 0, :], pb[0][:].rearrange("p (a t) -> p a t", a=4)), [b_pb[0]], [b_ART])
            P.op("dve", lambda e: e.tensor_copy(ART[:, :, 1, :], pb[1][:].rearrange("p (a t) -> p a t", a=4)), [b_pb[1]], [b_ART])
            for (dstm, b_dstm, bank) in ((BtT, b_BtT, 2), (KtT, b_KtT, 3)):
                v3 = pb[bank][:].rearrange("p (a t) -> p a t", a=4)
                d4 = dstm[:].rearrange("p (a b) t -> p a b t", b=2)
                P.op("act", lambda e, d4=d4, v3=v3: e.copy(d4[0:64, :, 0, :], v3[0:64]), [b_pb[bank]], [b_dstm])
                P.op("dve", lambda e, d4=d4, v3=v3: e.tensor_copy(d4[64:128, :, 1, :], v3[64:128]), [b_pb[bank]], [b_dstm])
            for c_ in range(2):
                cs2 = slice(c_ * 64, c_ * 64 + 64)
                P.op("pool", lambda e, c_=c_, cs2=cs2: e.tensor_copy(Vc[c_][0][cs2, :], pr[cs2, 1024:1536]), [b_pr], [Vc[c_][1]])
            if RWKV_STAGE <= 3:
                continue
            P0, b_P0 = Pm[0]
            PT0, b_PT0 = PTm[0]
            TT0, b_TT0 = TTm[0]
            for half in range(2):
                def mm_m(e, half=half):
                    r = None
                    for hh in range(4):
                        h = half * 4 + hh
                        hp = h // 2
                        art = ART[:, hp, :, :].rearrange("p a t -> p (a t)")
                        e.matmul(pb[0][:, hh * 128:(hh + 1) * 128], ART[:, hp, 0, :], BtT[:, h, :], start=True, stop=True)
                        e.matmul(pb[1 + hh // 2][:, (hh % 2) * 256:(hh % 2) * 256 + 256], BtT[:, h, :], art, start=True, stop=True)
                        r = e.matmul(pb[3 + hh // 2][:, (hh % 2) * 256:(hh % 2) * 256 + 256], KtT[:, h, :], art, start=True, stop=True)
                    return r
                P.op("pe", mm_m, [b_ART, b_BtT, b_KtT], [b_pb[0], b_pb[1], b_pb[2], b_pb[3], b_pb[4]])
                hsl = slice(half * 4, half * 4 + 4)
                if RWKV_SUB <= 0:
                    continue
                P.op("dve", lambda e, hsl=hsl: e.tensor_tensor(P0[:, hsl, :], pb[0][:].rearrange("p (a t) -> p a t", a=4),
                                                              bcast_mid(low_strict_bd[:], 4), ALU.mult),
                     [b_pb[0], b_c], [b_P0])
                if RWKV_SUB <= 1:
                    continue
                for pi in range(2):
                    hs2 = slice(half * 4 + pi * 2, half * 4 + pi * 2 + 2)
                    v4 = pb[1 + pi][:].rearrange("p (h a t) -> p h a t", h=2, a=2)
                    P.op("dve", lambda e, hs2=hs2, v4=v4: e.tensor_tensor(PT0[:, hs2, :], v4[:, :, 0, :],
                                                                         bcast_mid(tri_excl_bd[:], 2), ALU.mult),
                         [b_pb[1 + pi], b_c], [b_PT0])
                    P.op("dve", lambda e, hs2=hs2, v4=v4: e.tensor_tensor(MrbT[:, hs2, :], v4[:, :, 1, :],
                                                                         bcast_mid(tri_incl_bd[:], 2), ALU.mult),
                         [b_pb[1 + pi], b_c], [b_MrbT])
                    v5 = pb[3 + pi][:].rearrange("p (h a t) -> p h a t", h=2, a=2)
                    P.op("dve", lambda e, hs2=hs2, v5=v5: e.tensor_tensor(MakT[:, hs2, :], v5[:, :, 0, :],
                                                                         bcast_mid(tri_excl_bd[:], 2), ALU.mult),
                         [b_pb[3 + pi], b_c], [b_MakT])
                    P.op("dve", lambda e, hs2=hs2, v5=v5: e.tensor_tensor(MrkT[:, hs2, :], v5[:, :, 1, :],
                                                                         bcast_mid(tri_incl_bd[:], 2), ALU.mult),
                         [b_pb[3 + pi], b_c], [b_MrkT])
            if RWKV_STAGE <= 4:
                continue
            P.op("pool", lambda e: e.tensor_tensor(TT0[:], PT0[:], bcast_mid(identf[:], 8), ALU.add), [b_PT0, b_ident], [b_TT0])
            cur = 0
            for j in range(5):
                nxt = 1 - cur
                Pc, b_Pc = Pm[cur]
                PTc, b_PTc = PTm[cur]
                TTc, b_TTc = TTm[cur]
                Pn, b_Pn = Pm[nxt]
                PTn, b_PTn = PTm[nxt]
                TTn, b_TTn = TTm[nxt]
                for half in range(2):
                    bP, bPT, bTT = half * 3, half * 3 + 1, half * 3 + 2

                    def mm_sq(e, half=half, Pc=Pc, PTc=PTc, bP=bP, bPT=bPT, j=j):
                        r = None
                        for hh in range(4):
                            h = half * 4 + hh
                            r = e.matmul(pb[bP][:, hh * 128:(hh + 1) * 128], PTc[:, h, :], Pc[:, h, :], start=True, stop=True)
                            if j < 4:
                                r = e.matmul(pb[bPT][:, hh * 128:(hh + 1) * 128], Pc[:, h, :], PTc[:, h, :], start=True, stop=True)
                        return r
                    P.op("pe", mm_sq, [b_Pc, b_PTc], [b_pb[bP], b_pb[bPT]])
                    hsl = slice(half * 4, half * 4 + 4)
                    P.op("act", lambda e, Pn=Pn, hsl=hsl, bP=bP: e.copy(Pn[:, hsl, :], pb[bP][:].rearrange("p (a t) -> p a t", a=4)),
                         [b_pb[bP]], [b_Pn])
                    if j < 4:
                        P.op("dve", lambda e, PTn=PTn, hsl=hsl, bPT=bPT: e.tensor_copy(
                            PTn[:, hsl, :], pb[bPT][:].rearrange("p (a t) -> p a t", a=4)), [b_pb[bPT]], [b_PTn])

                    def mm_tt(e, half=half, Pn=Pn, TTc=TTc, bTT=bTT):
                        r = None
                        for hh in range(4):
                            h = half * 4 + hh
                            r = e.matmul(pb[bTT][:, hh * 128:(hh + 1) * 128], Pn[:, h, :], TTc[:, h, :], start=True, stop=True)
                        return r
                    P.op("pe", mm_tt, [b_Pn, b_TTc], [b_pb[bTT]])
                    P.op("dve", lambda e, TTn=TTn, TTc=TTc, hsl=hsl, bTT=bTT: e.tensor_tensor(
                        TTn[:, hsl, :], pb[bTT][:].rearrange("p (a t) -> p a t", a=4), TTc[:, hsl, :], ALU.add),
                        [b_pb[bTT], b_TTc], [b_TTn])
                cur = nxt
            TT, b_TT = TTm[cur]
            if RWKV_STAGE <= 5:
                continue
            for c in range(2):
                cs_ = slice(c * 64, c * 64 + 64)
                SAs, b_SAs = SAc[c]
                Vm, b_Vm = Vc[c]

                def mm_x(e):
                    r = None
                    for h in range(8):
                        hp = h // 2
                        o = pb[6][:, h * 64:(h + 1) * 64]
                        e.matmul(o, ART[:, hp, 0, :], S0T[:, h, :], start=True, stop=False)
                        r = e.matmul(o, MakT[:, h, :], pr[:, 1024 + h * 64:1024 + (h + 1) * 64], start=False, stop=True)
                    return r
                P.op("pe", mm_x, [b_ART, b_S0T, b_MakT, b_pr], [b_pb[6]])
                P.op("act", lambda e, cs_=cs_: e.copy(Xs[cs_, :], pb[6][cs_, :]), [b_pb[6]], [b_Xs])

                def mm_sa(e):
                    r = None
                    for h in range(8):
                        r = e.matmul(pb[7][:, h * 64:(h + 1) * 64], TT[:, h, :], Xs[:, h * 64:(h + 1) * 64], start=True, stop=True)
                    return r
                P.op("pe", mm_sa, [b_TT, b_Xs], [b_pb[7]])
                P.op("dve", lambda e, cs_=cs_, SAs=SAs: e.tensor_copy(SAs[cs_, :], pb[7][cs_, :]), [b_pb[7]], [b_SAs])

                def mm_y(e, SAs=SAs):
                    r = None
                    for h in range(8):
                        hp = h // 2
                        o = pb[6][:, h * 64:(h + 1) * 64]
                        e.matmul(o, ART[:, hp, 1, :], S0T[:, h, :], start=True, stop=False)
                        e.matmul(o, MrbT[:, h, :], SAs[:, h * 64:(h + 1) * 64], start=False, stop=False)
                        r = e.matmul(o, MrkT[:, h, :], pr[:, 1024 + h * 64:1024 + (h + 1) * 64], start=False, stop=True)
                    return r
                P.op("pe", mm_y, [b_ART, b_S0T, b_MrbT, b_MrkT, b_SAs, b_pr], [b_pb[6]])
                P.op("act", lambda e, cs_=cs_: e.copy(Ys[cs_, :], pb[6][cs_, :]), [b_pb[6]], [b_Ys])

                def mm_s(e, SAs=SAs, Vm=Vm):
                    r = None
                    for h in range(8):
                        hp = h // 2
                        o = pb[7][:, h * 64:(h + 1) * 64]
                        e.matmul(o, Bend[:, hp * 128:(hp + 1) * 128], SAs[:, h * 64:(h + 1) * 64], start=True, stop=False)
                        r = e.matmul(o, Kend[:, hp * 128:(hp + 1) * 128], Vm[:, h * 64:(h + 1) * 64], start=False, stop=True)
                    return r
                P.op("pe", mm_s, [b_Bend, b_Kend, b_SAs, b_Vm], [b_pb[7]])
                S4 = S0T[:].rearrange("p (a b) v -> p a b v", b=2)
                p4 = pb[7][:].rearrange("p (a b v) -> p a b v", a=4, b=2)
                g3 = gCt[:].rearrange("p (a c) -> p a c", a=4)
                for h2 in range(2):
                    sl = slice(h2 * 64, h2 * 64 + 64)
                    P.op("dve", lambda e, sl=sl, h2=h2, c=c: e.tensor_tensor(S4[sl, :, h2, :], S4[sl, :, h2, :],
                                                                          bcast_last(g3[sl, :, c], 64), ALU.mult),
                         [b_S0T, b_gCt], [b_S0T])
                    P.op("dve", lambda e, sl=sl, h2=h2: e.tensor_tensor(S4[sl, :, h2, :], S4[sl, :, h2, :], p4[sl, :, h2, :], ALU.add),
                         [b_S0T, b_pb[7]], [b_S0T])
            if RWKV_STAGE <= 6:
                continue
            P.op("dve", lambda e: e.tensor_reduce(hs[:, 16:24], h3(Ys[:]), AX.X, ALU.add), [b_Ys], [b_hs])
            P.op("dve", lambda e: e.tensor_scalar(hs[:, 16:24], hs[:, 16:24], -1.0 / 64, None, ALU.mult), [b_hs], [b_hs])
            P.op("dve", lambda e: e.tensor_tensor(h3(Ys[:]), h3(Ys[:]), bcast_last(hs[:, 16:24], 64), ALU.add),
                 [b_Ys, b_hs], [b_Ys])
            P.op("act", lambda e: e.activation(tmp[:], Ys[:], AF.Square), [b_Ys], [b_tmp])
            P.op("dve", lambda e: e.tensor_reduce(hs[:, 24:32], h3(tmp[:]), AX.X, ALU.add), [b_tmp], [b_hs])
            rstd_from_sumsq(P, hs[:, 24:32], hs[:, 24:32], 64, 64e-5, [b_hs], [b_hs])
            P.op("dve", lambda e: e.tensor_tensor(h3(Ys[:]), h3(Ys[:]), bcast_last(hs[:, 24:32], 64), ALU.mult),
                 [b_Ys, b_hs], [b_Ys])
            P.op("pool", lambda e: e.tensor_tensor(Ys[:], Ys[:], ln_g[:], ALU.mult), [b_Ys, b_c], [b_Ys])
            P.op("pool", lambda e: e.tensor_tensor(Ys[:], Ys[:], ln_b[:], ALU.add), [b_Ys, b_c], [b_Ys])
            P.op("dve", lambda e: e.tensor_tensor(h3(tmp[:]), h3(v_), bcast_last(hs[:, 8:16], 64), ALU.mult),
                 [b_pr, b_hs], [b_tmp])
            P.op("pool", lambda e: e.tensor_tensor(Ys[:], Ys[:], tmp[:], ALU.add), [b_Ys, b_tmp], [b_Ys])
            P.op("dve", lambda e: e.tensor_tensor(yo[:], Ys[:], gg[:], ALU.mult), [b_Ys, b_gg], [b_yo])
            P.op("dve", lambda e: e.tensor_copy(tmp[:], yo[:]), [b_yo], [b_tmp])

            def tr_o(e):
                r = None
                for kc in range(4):
                    r = e.transpose(pb[0][:, kc * 128:(kc + 1) * 128], tmp[:, kc * 128:(kc + 1) * 128], identf[:])
                return r
            P.op("pe", tr_o, [b_tmp, b_ident], [b_pb[0]])
            i = tb % 2
            P.op("act", lambda e, i=i: e.copy(oT[i][:].rearrange("p a t -> p (a t)"), pb[0][:]), [b_pb[0]], [b_oT[i]])
            P.dma(g["orwkvT"][:, t0:t0 + 128].rearrange("(k p) t -> p k t", p=128), oT[i][:], [b_oT[i]], [g["b_orwkvT"]])
        P.flush()


_PROG_CACHE = {}


def kernel(**inputs):
    x = np.ascontiguousarray(np.asarray(inputs["x"], dtype=np.float32))
    pos = np.asarray(inputs["positions"]).astype(np.int32)
    B, T, _ = x.shape
    key = (T,)
    if key not in _PROG_CACHE:
        _PROG_CACHE[key] = build_program(T, DEPTH, "all", debug=False)
    nc, _ = _PROG_CACHE[key]
    consts = build_consts()
    shared = {}
    for k, shp in PARAM_SHAPES.items():
        shared[k] = np.ascontiguousarray(np.asarray(inputs[k], dtype=np.float32)).reshape(shp)
    shared.update(consts)
    in_maps = []
    for b in range(B):
        m = dict(shared)
        m["x"] = np.ascontiguousarray(x[b])
        m["positions"] = np.ascontiguousarray(pos[b].reshape(T // 128, 128).T)
        in_maps.append(m)
    res = run_bass_kernel_spmd(nc, in_maps, core_ids=list(range(B)))
    out = np.stack([np.asarray(r["y"], dtype=np.float32) for r in res.results], axis=0)
    return out
```

```python
import math
from contextlib import ExitStack
import numpy as np
import concourse.bass as bass
import concourse.mybir as mybir
from concourse.bass_utils import run_bass_kernel_spmd

F32 = mybir.dt.float32
BF16 = mybir.dt.bfloat16
I32 = mybir.dt.int32
AF = mybir.ActivationFunctionType
ALU = mybir.AluOpType
AX = mybir.AxisListType

D = 1024
DEPTH = 2
H_MLA = 8
QK = 96
NOPE = 64
ROPE = 32
QR = 384
KVR = 256
MLA_IN = QR + KVR + ROPE
RW = 512
RWKV_IN = 3 * RW + 64 + 64 + 128
SSM_D = 1024
SSM_CONV_DIM = 1536
SSM_IN = SSM_D + SSM_CONV_DIM + 16
GATE_IN = 3 * D
IN_DIM = MLA_IN + RWKV_IN + SSM_IN + GATE_IN
C_RWKV = MLA_IN
C_SSM = MLA_IN + RWKV_IN
C_GATE = C_SSM + SSM_IN
DFF = 4096
EPS = 1e-6

NLANES = 12
SKIP_RWKV = False
SKIP = set()
RWKV_STAGE = 9
RWKV_SUB = 9
RWKV_H2 = (0, 1)
RWKV_MM = (0, 1, 2)


class Buf:
    __slots__ = ("name", "w", "r")

    def __init__(self, name):
        self.name = name
        self.w = None
        self.r = {}


class Prog:
    ENGS = ("pe", "dve", "act", "pool", "sp")

    def __init__(self, nc):
        self.nc = nc
        self.es = ExitStack()
        self.sem = {n: self.es.enter_context(nc.semaphore("s_" + n)) for n in self.ENGS}
        self.cnt = {n: 0 for n in self.ENGS}
        self.ops = {n: [] for n in self.ENGS}
        self.seen = {n: {} for n in self.ENGS}
        self.lanes = {}
        for q in ("sp", "pool", "act"):
            self.lanes[q] = [[self.es.enter_context(nc.semaphore("d_%s%d" % (q, i))), 0]
                             for i in range(NLANES)]
        self.lane_rr = {"sp": 0, "pool": 0, "act": 0}
        self.dma_rr = 0
        self.nbuf = 0
        self.ninst = 0

    def buf(self, name=None):
        self.nbuf += 1
        return Buf(name or "b%d" % self.nbuf)

    def _collect(self, eng, reads, writes, is_dma):
        waits = {}

        def need(tok, raw):
            if tok is None:
                return
            key, sem, val, src = tok
            if not is_dma and src == eng:
                if eng == "pe":
                    return
            cur = waits.get(key)
            if cur is None or cur[1] < val:
                waits[key] = (sem, val)

        for b in reads:
            need(b.w, True)
        for b in writes:
            need(b.w, False)
            for t in b.r.values():
                need(t, False)
        return waits

    def _emit_waits(self, eng, waits):
        seen = self.seen[eng]
        for key, (sem, val) in waits.items():
            if seen.get(key, 0) >= val:
                continue
            seen[key] = val
            self.ops[eng].append(("w", sem, val))

    def _commit(self, tok, reads, writes):
        for b in writes:
            b.w = tok
            b.r = {}
        for b in reads:
            b.r[tok[0]] = tok

    def op(self, eng, fn, reads=(), writes=()):
        waits = self._collect(eng, reads, writes, False)
        self._emit_waits(eng, waits)
        self.cnt[eng] += 1
        self.ops[eng].append(("o", fn, self.sem[eng], 1))
        self.ninst += 1
        tok = (eng, self.sem[eng], self.cnt[eng], eng)
        self._commit(tok, reads, writes)

    def dma(self, out, in_, reads=(), writes=(), q=None):
        if q is None:
            q = ("sp", "pool")[self.dma_rr % 2]
            self.dma_rr += 1
        li = self.lane_rr[q]
        self.lane_rr[q] = (li + 1) % NLANES
        lane = self.lanes[q][li]
        key = "d_%s%d" % (q, li)
        waits = self._collect(q, reads, writes, True)
        if lane[1] > 0:
            cur = waits.get(key)
            if cur is None or cur[1] < lane[1]:
                waits[key] = (lane[0], lane[1])
        self._emit_waits(q, waits)
        lane[1] += 16
        self.ops[q].append(("o", lambda e, o=out, i=in_: e.dma_start(out=o, in_=i), lane[0], 16))
        self.ninst += 1
        tok = (key, lane[0], lane[1], "dma")
        self._commit(tok, reads, writes)
        return tok

    def wait_tok(self, eng, tok):
        self._emit_waits(eng, {tok[0]: (tok[1], tok[2])})

    def barrier(self):
        allw = {}
        for n in self.ENGS:
            if self.cnt[n] > 0:
                allw[n] = (self.sem[n], self.cnt[n])
        for q, lanes in self.lanes.items():
            for li, lane in enumerate(lanes):
                if lane[1] > 0:
                    allw["d_%s%d" % (q, li)] = (lane[0], lane[1])
        for eng in self.ENGS:
            w = {k: v for k, v in allw.items() if k != eng}
            self._emit_waits(eng, w)

    def flush(self):
        nc = self.nc
        self.barrier()
        ops = self.ops
        with nc.Block() as block:
            def run(e, lst):
                for it in lst:
                    if it[0] == "w":
                        e.wait_ge(it[1], it[2])
                    else:
                        it[1](e).then_inc(it[2], it[3])

            if ops["sp"]:
                @block.sync
                def _(e):
                    run(e, ops["sp"])
            if ops["pe"]:
                @block.tensor
                def _(e):
                    run(e, ops["pe"])
            if ops["dve"]:
                @block.vector
                def _(e):
                    run(e, ops["dve"])
            if ops["act"]:
                @block.scalar
                def _(e):
                    run(e, ops["act"])
            if ops["pool"]:
                @block.gpsimd
                def _(e):
                    run(e, ops["pool"])
        self.ops = {n: [] for n in self.ENGS}

    def final_wait(self, eng, bufs):
        waits = {}
        for b in bufs:
            if b.w is not None:
                key, sem, val, src = b.w
                cur = waits.get(key)
                if cur is None or cur[1] < val:
                    waits[key] = (sem, val)
        self._emit_waits(eng, waits)


def bc_rows(ap_dram_row, n):
    return ap_dram_row.partition_broadcast(128)


class Ctx:
    pass


def build_consts():
    c = {}
    c["ident_f"] = np.eye(128, dtype=np.float32)
    t = np.arange(128)
    same = (t[:, None] // 64) == (t[None, :] // 64)
    c["tri_incl_bd"] = (same & (t[:, None] <= t[None, :])).astype(np.float32)
    c["tri_excl_bd"] = (same & (t[:, None] < t[None, :])).astype(np.float32)
    c["tri_after_bd"] = (same & (t[:, None] > t[None, :])).astype(np.float32)
    c["low_strict_bd"] = (same & (t[:, None] > t[None, :])).astype(np.float32)
    c["tri_incl"] = (t[:, None] <= t[None, :]).astype(np.float32)
    c["tri_gt"] = (t[:, None] > t[None, :]).astype(np.float32)
    c["ones"] = np.ones((128, 128), dtype=np.float32)
    half = ROPE // 2
    inv = (np.float32(10000.0) ** (-np.arange(half, dtype=np.float32) / np.float32(half))).astype(np.float32)
    c["inv_freq"] = inv.reshape(1, half)
    hm = np.zeros((8, 512), dtype=np.float32)
    for h in range(8):
        hm[h, h * 64:(h + 1) * 64] = 1.0
    c["headmask"] = hm
    return c


CONST_SHAPES = {k: v.shape for k, v in build_consts().items()}

PARAM_SHAPES = {
    "norm_mix_g": (DEPTH, D), "w_in": (DEPTH, D, IN_DIM), "mla_q_norm_g": (DEPTH, QR),
    "mla_kv_norm_g": (DEPTH, KVR), "mla_w_uq": (DEPTH, QR, 768), "mla_w_ukv": (DEPTH, KVR, 1024),
    "mla_q_head_g": (DEPTH, QK), "mla_k_head_g": (DEPTH, QK), "rwkv_mu": (DEPTH, RWKV_IN),
    "rwkv_w0": (DEPTH, RW), "rwkv_w2": (DEPTH, 64, RW), "rwkv_a0": (DEPTH, RW),
    "rwkv_a2": (DEPTH, 64, RW), "rwkv_g2": (DEPTH, 128, RW), "rwkv_v0": (DEPTH - 1, RW),
    "rwkv_v1": (DEPTH - 1, RW, 32), "rwkv_v2": (DEPTH - 1, 32, RW), "rwkv_k_k": (DEPTH, RW),
    "rwkv_k_a": (DEPTH, RW), "rwkv_r_k": (DEPTH, RW), "rwkv_ln_g": (DEPTH, RW),
    "rwkv_ln_b": (DEPTH, RW), "ssm_conv_w": (DEPTH, 4, SSM_CONV_DIM), "ssm_conv_b": (DEPTH, SSM_CONV_DIM),
    "ssm_dt_bias": (DEPTH, 16), "ssm_a_log": (DEPTH, 16), "ssm_d": (DEPTH, 16),
    "ssm_norm_g": (DEPTH, SSM_D), "w_br_mla": (DEPTH, 512, D), "w_br_rwkv": (DEPTH, RW, D),
    "w_br_ssm": (DEPTH, SSM_D, D), "w_out": (DEPTH, D, D), "norm_ffn_g": (DEPTH, D),
    "w_ff1": (DEPTH, D, DFF), "w_ff2": (DEPTH, DFF, D),
}


def cast_split(P, k, out, in_, reads, writes):
    eng = ("dve", "act", "pool")[k % 3]
    if eng == "act":
        P.op("act", lambda e: e.copy(out, in_), reads, writes)
    else:
        P.op(eng, lambda e: e.tensor_copy(out, in_), reads, writes)


def rstd_from_sumsq(P, out, ss, n, eps, reads, writes):
    P.op("dve", lambda e: e.tensor_scalar(out, ss, 1.0 / n, eps, ALU.mult, ALU.add), reads, writes)
    P.op("act", lambda e: e.activation(out, out, AF.Sqrt), writes, writes)
    P.op("dve", lambda e: e.reciprocal(out, out), writes, writes)


def phase_inproj(P, g, l, x_ap, b_x, T):
    nc = P.nc
    NB = T // 128
    with ExitStack() as es:
        def sb(name, shape, dt):
            return es.enter_context(nc.sbuf_tensor("p1_%d_" % l + name, shape, dt))

        def ps(name, shape, dt):
            return es.enter_context(nc.psum_tensor("p1_%d_" % l + name, shape, dt))

        hT = sb("hT", [128, 8, T + 1], BF16)
        b_hT = P.buf("hT")
        gain = sb("gain", [128, D], F32)
        b_gain = P.buf()
        ident = sb("ident", [128, 128], BF16)
        identf = sb("identf", [128, 128], F32)
        b_ident = P.buf()
        mu = sb("mu", [128, RWKV_IN], F32)
        omu = sb("omu", [128, RWKV_IN], F32)
        b_mu = P.buf()
        xin = [sb("xin%d" % i, [128, D], F32) for i in range(2)]
        b_xin = [P.buf() for _ in range(2)]
        sq = sb("sq", [128, D], F32)
        b_sq = P.buf()
        ss = [sb("ss%d" % i, [128, 1], F32) for i in range(2)]
        b_ss = [P.buf() for _ in range(2)]
        hb = [sb("hb%d" % i, [128, D], BF16) for i in range(2)]
        b_hb = [P.buf() for _ in range(2)]
        ptr = [ps("ptr%d" % i, [128, 8, 128], BF16) for i in range(2)]
        b_ptr = [P.buf() for _ in range(2)]

        P.dma(gain[:], g["norm_mix_g"][l].partition_broadcast(128), [], [b_gain])
        P.dma(identf[:], g["ident_f"][:, :], [], [b_ident])
        P.op("dve", lambda e: e.tensor_copy(ident[:], identf[:]), [b_ident], [b_ident])
        P.dma(mu[:], g["rwkv_mu"][l].partition_broadcast(128), [], [b_mu])
        P.op("dve", lambda e: e.tensor_scalar(omu[:], mu[:], -1.0, 1.0, ALU.mult, ALU.add), [b_mu], [b_mu])
        P.op("pool", lambda e: e.memset(hT[:, :, 0:1], 0.0), [], [b_hT])

        for tb in range(NB):
            i = tb % 2
            P.dma(xin[i][:], x_ap[tb * 128:(tb + 1) * 128, :], [b_x], [b_xin[i]])
            P.op("act", lambda e, i=i: e.activation(sq[:], xin[i][:], AF.Square, accum_out=ss[i][:]),
                 [b_xin[i]], [b_sq, b_ss[i]])
            rstd_from_sumsq(P, ss[i][:], ss[i][:], D, EPS, [b_ss[i]], [b_ss[i]])
            P.op("dve", lambda e, i=i: e.scalar_tensor_tensor(hb[i][:], xin[i][:], ss[i][:, 0:1], gain[:],
                                                             ALU.mult, ALU.mult),
                 [b_xin[i], b_ss[i], b_gain], [b_hb[i]])

            def tr(e, i=i):
                r = None
                for kc in range(8):
                    r = e.transpose(ptr[i][:, kc, :], hb[i][:, kc * 128:(kc + 1) * 128], ident[:])
                return r
            P.op("pe", tr, [b_hb[i], b_ident], [b_ptr[i]])
            P.op("act" if tb % 2 else "dve",
                 (lambda e, i=i, tb=tb: e.copy(hT[:, :, 1 + tb * 128:1 + (tb + 1) * 128], ptr[i][:]))
                 if tb % 2 else
                 (lambda e, i=i, tb=tb: e.tensor_copy(hT[:, :, 1 + tb * 128:1 + (tb + 1) * 128], ptr[i][:])),
                 [b_ptr[i]], [b_hT])

        CG = 512
        groups = []
        c0 = 0
        bounds = [0, C_RWKV, C_SSM, IN_DIM]
        for bi in range(3):
            a, b = bounds[bi], bounds[bi + 1]
            c = a
            while c < b:
                w = min(CG, b - c)
                groups.append((c, w, bi == 1))
                c += w
        wst = [sb("wst%d" % i, [128, 8, CG], F32) for i in range(2)]
        b_wst = [P.buf() for _ in range(2)]
        wbf = [sb("wbf%d" % i, [128, 16, CG], BF16) for i in range(2)]
        b_wbf = [P.buf() for _ in range(2)]
        pacc = [ps("pacc%d" % i, [128, CG], F32) for i in range(4)]
        b_pacc = [P.buf() for _ in range(4)]
        ost = [sb("ost%d" % i, [128, CG], F32) for i in range(4)]
        b_ost = [P.buf() for _ in range(4)]
        proj = g["proj"]
        b_proj = g["b_proj"]
        w_in = g["w_in"]
        n = 0
        def load_w(gi):
            c, w, is_rwkv = groups[gi]
            i = gi % 2
            P.dma(wst[i][:, :, 0:w], w_in[l, :, c:c + w].rearrange("(k p) c -> p k c", p=128), [], [b_wst[i]])

        def cast_w(gi):
            c, w, is_rwkv = groups[gi]
            i = gi % 2
            if is_rwkv:
                m0 = c - C_RWKV
                for kc in range(8):
                    eng = ("dve", "pool")[kc % 2]
                    P.op(eng, lambda e, i=i, kc=kc, w=w, m0=m0: e.tensor_tensor(
                        wbf[i][:, kc, 0:w], wst[i][:, kc, 0:w], omu[:, m0:m0 + w], ALU.mult),
                        [b_wst[i], b_mu], [b_wbf[i]])
                    eng = ("pool", "dve")[kc % 2]
                    P.op(eng, lambda e, i=i, kc=kc, w=w, m0=m0: e.tensor_tensor(
                        wbf[i][:, 8 + kc, 0:w], wst[i][:, kc, 0:w], mu[:, m0:m0 + w], ALU.mult),
                        [b_wst[i], b_mu], [b_wbf[i]])
            else:
                for kc in range(8):
                    cast_split(P, kc, wbf[i][:, kc, 0:w], wst[i][:, kc, 0:w], [b_wst[i]], [b_wbf[i]])

        load_w(0)
        cast_w(0)
        if len(groups) > 1:
            load_w(1)
        for gi, (c, w, is_rwkv) in enumerate(groups):
            i = gi % 2
            for tb in range(NB):
                j = n % 4
                n += 1

                def mm(e, i=i, j=j, tb=tb, w=w, is_rwkv=is_rwkv):
                    r = None
                    nk = 16 if is_rwkv else 8
                    for kc in range(nk):
                        if kc < 8:
                            lhsT = hT[:, kc, 1 + tb * 128:1 + (tb + 1) * 128]
                        else:
                            lhsT = hT[:, kc - 8, tb * 128:(tb + 1) * 128]
                        r = e.matmul(pacc[j][:, 0:w], lhsT, wbf[i][:, kc, 0:w], start=(kc == 0), stop=(kc == nk - 1))
                    return r
                P.op("pe", mm, [b_hT, b_wbf[i]], [b_pacc[j]])
                if n % 2:
                    P.op("act", lambda e, j=j, w=w: e.copy(ost[j][:, 0:w], pacc[j][:, 0:w]), [b_pacc[j]], [b_ost[j]])
                else:
                    P.op("dve", lambda e, j=j, w=w: e.tensor_copy(ost[j][:, 0:w], pacc[j][:, 0:w]), [b_pacc[j]], [b_ost[j]])
                P.dma(proj[tb * 128:(tb + 1) * 128, c:c + w], ost[j][:, 0:w], [b_ost[j]], [b_proj])
                if tb == (NB * 3) // 4 - 1 or NB == 1:
                    if gi + 1 < len(groups):
                        cast_w(gi + 1)
            if gi + 2 < len(groups):
                load_w(gi + 2)
        P.flush()


def phase_ffn(P, g, l, x_in, b_xin_d, x_out, b_xout_d, T):
    nc = P.nc
    SBK = 256
    NBS = SBK // 128
    NS = T // SBK
    with ExitStack() as es:
        def sb(name, shape, dt):
            return es.enter_context(nc.sbuf_tensor("p6_%d_" % l + name, shape, dt))

        def ps(name, shape, dt):
            return es.enter_context(nc.psum_tensor("p6_%d_" % l + name, shape, dt))

        w1 = sb("w1", [128, 8, DFF], BF16)
        b_w1 = P.buf()
        w2 = sb("w2", [128, 32, D], BF16)
        b_w2 = P.buf()
        wst = [sb("wst%d" % i, [128, 1024], F32) for i in range(2)]
        b_wst = [P.buf() for _ in range(2)]
        gain = sb("gain", [128, D], F32)
        b_gain = P.buf()
        ident = sb("ident", [128, 128], BF16)
        identf = sb("identf", [128, 128], F32)
        b_ident = P.buf()
        P.dma(gain[:], g["norm_ffn_g"][l].partition_broadcast(128), [], [b_gain])
        P.dma(identf[:], g["ident_f"][:, :], [], [b_ident])
        P.op("dve", lambda e: e.tensor_copy(ident[:], identf[:]), [b_ident], [b_ident])
        n = 0
        for kc in range(8):
            for half in range(4):
                i = n % 2
                n += 1
                P.dma(wst[i][:], g["w_ff1"][l, kc * 128:(kc + 1) * 128, half * 1024:(half + 1) * 1024], [], [b_wst[i]])
                cast_split(P, n, w1[:, kc, half * 1024:(half + 1) * 1024], wst[i][:], [b_wst[i]], [b_w1])
        for fc in range(32):
            i = n % 2
            n += 1
            P.dma(wst[i][:], g["w_ff2"][l, fc * 128:(fc + 1) * 128, :], [], [b_wst[i]])
            cast_split(P, n, w2[:, fc, :], wst[i][:], [b_wst[i]], [b_w2])

        xin = [sb("xin%d" % i, [128, NBS, D], F32) for i in range(2)]
        b_xin = [P.buf() for _ in range(2)]
        sq = sb("sq", [128, D], F32)
        b_sq = P.buf()
        ss = sb("ss", [128, NBS], F32)
        b_ss = P.buf()
        hb = sb("hb", [128, D], BF16)
        b_hb = P.buf()
        hT = [sb("hT%d" % i, [128, 8, SBK], BF16) for i in range(2)]
        b_hT = [P.buf() for _ in range(2)]
        ptr = [ps("ptr%d" % i, [128, 8, 128], BF16) for i in range(2)]
        b_ptr = [P.buf() for _ in range(2)]
        pu = [ps("pu%d" % i, [128, SBK], F32) for i in range(3)]
        b_pu = [P.buf() for _ in range(3)]
        po = [ps("po%d" % i, [128, 512], F32) for i in range(3)]
        b_po = [P.buf() for _ in range(3)]
        rl = [sb("rl%d" % i, [128, SBK], F32) for i in range(3)]
        b_rl = [P.buf() for _ in range(3)]
        aT = [sb("aT%d" % i, [128, 32, SBK], BF16) for i in range(1)]
        b_aT = [P.buf() for _ in range(1)]
        xo = [sb("xo%d" % i, [128, D], F32) for i in range(2)]
        b_xo = [P.buf() for _ in range(2)]
        nt = 0
        nu = 0
        no = 0
        for s in range(NS):
            i = s % 2
            P.dma(xin[i][:], x_in[s * SBK:(s + 1) * SBK, :].rearrange("(b p) c -> p b c", p=128), [b_xin_d], [b_xin[i]])
            for b4 in range(NBS):
                P.op("act", lambda e, i=i, b4=b4: e.activation(sq[:], xin[i][:, b4, :], AF.Square,
                                                             accum_out=ss[:, b4:b4 + 1]),
                     [b_xin[i]], [b_sq, b_ss])
            rstd_from_sumsq(P, ss[:], ss[:], D, EPS, [b_ss], [b_ss])
            for b4 in range(NBS):
                P.op("dve", lambda e, i=i, b4=b4: e.scalar_tensor_tensor(hb[:], xin[i][:, b4, :], ss[:, b4:b4 + 1],
                                                                       gain[:], ALU.mult, ALU.mult),
                     [b_xin[i], b_ss, b_gain], [b_hb])
                j = nt % 2
                nt += 1

                def tr(e, j=j):
                    r = None
                    for kc in range(8):
                        r = e.transpose(ptr[j][:, kc, :], hb[:, kc * 128:(kc + 1) * 128], ident[:])
                    return r
                P.op("pe", tr, [b_hb, b_ident], [b_ptr[j]])
                P.op("act", lambda e, i=i, j=j, b4=b4: e.copy(hT[i][:, :, b4 * 128:(b4 + 1) * 128], ptr[j][:]),
                     [b_ptr[j]], [b_hT[i]])
            for fc in range(32):
                j = nu % 3
                nu += 1

                def mm(e, i=i, j=j, fc=fc):
                    r = None
                    for kc in range(8):
                        r = e.matmul(pu[j][:], w1[:, kc, fc * 128:(fc + 1) * 128], hT[i][:, kc, :],
                                     start=(kc == 0), stop=(kc == 7))
                    return r
                P.op("pe", mm, [b_w1, b_hT[i]], [b_pu[j]])
                P.op("act", lambda e, j=j: e.activation(rl[j][:], pu[j][:], AF.Relu), [b_pu[j]], [b_rl[j]])
                eng = ("dve", "pool")[fc % 2]
                P.op(eng, lambda e, j=j, fc=fc: e.tensor_tensor(aT[0][:, fc, :], rl[j][:], rl[j][:], ALU.mult),
                     [b_rl[j]], [b_aT[0]])
            for b4 in range(NBS):
                k = (s * NBS + b4) % 2
                for cg in range(2):
                    j = no % 3
                    no += 1

                    def mm2(e, j=j, b4=b4, cg=cg):
                        r = None
                        for fc in range(32):
                            r = e.matmul(po[j][:], aT[0][:, fc, b4 * 128:(b4 + 1) * 128],
                                         w2[:, fc, cg * 512:(cg + 1) * 512], start=(fc == 0), stop=(fc == 31))
                        return r
                    P.op("pe", mm2, [b_aT[0], b_w2], [b_po[j]])
                    P.op("dve", lambda e, i=i, j=j, k=k, b4=b4, cg=cg: e.tensor_tensor(
                        xo[k][:, cg * 512:(cg + 1) * 512], po[j][:], xin[i][:, b4, cg * 512:(cg + 1) * 512], ALU.add),
                        [b_po[j], b_xin[i]], [b_xo[k]])
                t0 = s * SBK + b4 * 128
                P.dma(x_out[t0:t0 + 128, :], xo[k][:], [b_xo[k]], [b_xout_d])
        P.flush()


def bcast_mid(ap2d, n):
    p, f = ap2d.shape
    return ap2d.unsqueeze(1).broadcast_to([p, n, f])


def bcast_last(ap2d, n):
    p, f = ap2d.shape
    return ap2d.unsqueeze(2).broadcast_to([p, f, n])


def load_ident(P, nc, es, g, pref):
    identf = es.enter_context(nc.sbuf_tensor(pref + "identf", [128, 128], F32))
    ident = es.enter_context(nc.sbuf_tensor(pref + "ident", [128, 128], BF16))
    b = P.buf()
    P.dma(identf[:], g["ident_f"][:, :], [], [b])
    P.op("dve", lambda e: e.tensor_copy(ident[:], identf[:]), [b], [b])
    return identf, ident, b


TWO_PI = 2.0 * math.pi


def phase_mla_prep(P, g, l, T):
    nc = P.nc
    NB = T // 128
    proj, b_proj = g["proj"], g["b_proj"]
    with ExitStack() as es:
        def sb(name, shape, dt):
            return es.enter_context(nc.sbuf_tensor("p2_%d_" % l + name, shape, dt))

        def ps(name, shape, dt):
            return es.enter_context(nc.psum_tensor("p2_%d_" % l + name, shape, dt))

        identf, ident, b_ident = load_ident(P, nc, es, g, "p2_%d_" % l)
        b_c = P.buf("consts")
        gcn = sb("gcn", [128, 640], F32)
        P.dma(gcn[:, 0:384], g["mla_q_norm_g"][l].partition_broadcast(128), [], [b_c])
        P.dma(gcn[:, 384:640], g["mla_kv_norm_g"][l].partition_broadcast(128), [], [b_c])
        gq = sb("gq", [128, 96], F32)
        gk = sb("gk", [128, 96], F32)
        P.dma(gq[:], g["mla_q_head_g"][l].partition_broadcast(128), [], [b_c])
        P.dma(gk[:], g["mla_k_head_g"][l].partition_broadcast(128), [], [b_c])
        invf = sb("invf", [128, 16], F32)
        P.dma(invf[:], g["inv_freq"][0].partition_broadcast(128), [], [b_c])
        posi = sb("posi", [128, NB], I32)
        posf = sb("posf", [128, NB], F32)
        P.dma(posi[:], g["positions"][:, :], [], [b_c])
        P.op("dve", lambda e: e.tensor_copy(posf[:], posi[:]), [b_c], [b_c])
        negpi = sb("negpi", [128, 1], F32)
        P.op("pool", lambda e: e.memset(negpi[:], -math.pi), [], [b_c])
        wst = sb("wst", [128, 1024], F32)
        b_wst = P.buf()
        wuq = sb("wuq", [128, 3, 768], BF16)
        wukv = sb("wukv", [128, 2, 1024], BF16)
        b_w = P.buf()
        for kc in range(3):
            P.dma(wst[:, 0:768], g["mla_w_uq"][l, kc * 128:(kc + 1) * 128, :], [], [b_wst])
            P.op("dve", lambda e, kc=kc: e.tensor_copy(wuq[:, kc, :], wst[:, 0:768]), [b_wst], [b_w])
        for kc in range(2):
            P.dma(wst[:], g["mla_w_ukv"][l, kc * 128:(kc + 1) * 128, :], [], [b_wst])
            P.op("dve", lambda e, kc=kc: e.tensor_copy(wukv[:, kc, :], wst[:]), [b_wst], [b_w])

        pm = [sb("pm%d" % i, [128, MLA_IN], F32) for i in range(2)]
        b_pm = [P.buf() for _ in range(2)]
        sq = sb("sq", [128, 1024], F32)
        b_sq = P.buf()
        st = sb("st", [128, 4], F32)
        b_st = P.buf()
        cn = sb("cn", [128, 640], BF16)
        b_cn = P.buf()
        cT = sb("cT", [128, 5, 128], BF16)
        b_cT = P.buf()
        ptr = ps("ptr", [128, 8, 128], BF16)
        b_ptr = P.buf()
        pq = ps("pq", [128, 1024], F32)
        b_pq = P.buf()
        pkv = ps("pkv", [128, 1024], F32)
        b_pkv = P.buf()
        hs = sb("hs", [128, 16], F32)
        b_hs = P.buf()
        qn = sb("qn", [128, 8, 96], F32)
        b_qn = P.buf()
        kn = sb("kn", [128, 8, 96], F32)
        b_kn = P.buf()
        qf = sb("qf", [128, 8, 96], BF16)
        b_qf = P.buf()
        kf = sb("kf", [128, 8, 96], BF16)
        b_kf = P.buf()
        v1 = [sb("v1_%d" % i, [128, 8, 65], BF16) for i in range(2)]
        b_v1 = [P.buf() for _ in range(2)]
        for i in range(2):
            P.op("pool", lambda e, i=i: e.memset(v1[i][:, :, 64:65], 1.0), [], [b_v1[i]])
        ang = sb("ang", [128, 32], F32)
        angu = sb("angu", [128, 32], F32)
        angi = sb("angi", [128, 32], I32)
        cs = sb("cs", [128, 32], F32)
        b_cs = P.buf()
        rt = sb("rt", [128, 4, 8, 16], F32)
        b_rt = P.buf()
        qTs = [sb("qTs%d" % i, [96, 8, 512], BF16) for i in range(2)]
        kTs = [sb("kTs%d" % i, [96, 8, 512], BF16) for i in range(2)]
        b_qTs = [P.buf() for _ in range(2)]
        b_kTs = [P.buf() for _ in range(2)]

        def rope(src, dst, b_src, b_dst):
            t1 = src[:, :, 64:80]
            t2 = src[:, :, 80:96]
            cosb = bcast_mid(cs[:, 0:16], 8)
            sinb = bcast_mid(cs[:, 16:32], 8)
            P.op("dve", lambda e: e.tensor_tensor(rt[:, 0], t1, cosb, ALU.mult), [b_src, b_cs], [b_rt])
            P.op("dve", lambda e: e.tensor_tensor(rt[:, 1], t2, sinb, ALU.mult), [b_src, b_cs], [b_rt])
            P.op("dve", lambda e: e.tensor_tensor(rt[:, 2], t1, sinb, ALU.mult), [b_src, b_cs], [b_rt])
            P.op("dve", lambda e: e.tensor_tensor(rt[:, 3], t2, cosb, ALU.mult), [b_src, b_cs], [b_rt])
            P.op("dve", lambda e: e.tensor_tensor(dst[:, :, 64:80], rt[:, 0], rt[:, 1], ALU.subtract), [b_rt], [b_dst])
            P.op("dve", lambda e: e.tensor_tensor(dst[:, :, 80:96], rt[:, 2], rt[:, 3], ALU.add), [b_rt], [b_dst])
            P.op("pool", lambda e: e.tensor_copy(dst[:, :, 0:64], src[:, :, 0:64]), [b_src], [b_dst])

        for tb in range(NB):
            i = tb % 2
            sgi = (tb // 4) % 2
            P.dma(pm[i][:], proj[tb * 128:(tb + 1) * 128, 0:MLA_IN], [b_proj], [b_pm[i]])
            P.op("act", lambda e, i=i: e.activation(sq[:, 0:384], pm[i][:, 0:384], AF.Square, accum_out=st[:, 0:1]),
                 [b_pm[i]], [b_sq, b_st])
            P.op("act", lambda e, i=i: e.activation(sq[:, 384:640], pm[i][:, 384:640], AF.Square, accum_out=st[:, 1:2]),
                 [b_pm[i]], [b_sq, b_st])
            P.op("act", lambda e, i=i: e.activation(sq[:, 640:672], pm[i][:, 640:672], AF.Square, accum_out=st[:, 2:3]),
                 [b_pm[i]], [b_sq, b_st])
            rstd_from_sumsq(P, st[:, 0:1], st[:, 0:1], QR, EPS, [b_st], [b_st])
            rstd_from_sumsq(P, st[:, 1:2], st[:, 1:2], KVR, EPS, [b_st], [b_st])
            P.op("dve", lambda e, i=i: e.scalar_tensor_tensor(cn[:, 0:384], pm[i][:, 0:384], st[:, 0:1], gcn[:, 0:384],
                                                             ALU.mult, ALU.mult), [b_pm[i], b_st, b_c], [b_cn])
            P.op("dve", lambda e, i=i: e.scalar_tensor_tensor(cn[:, 384:640], pm[i][:, 384:640], st[:, 1:2],
                                                             gcn[:, 384:640], ALU.mult, ALU.mult),
                 [b_pm[i], b_st, b_c], [b_cn])

            def tr(e):
                r = None
                for kc in range(5):
                    r = e.transpose(ptr[:, kc, :], cn[:, kc * 128:(kc + 1) * 128], ident[:])
                return r
            P.op("pe", tr, [b_cn, b_ident], [b_ptr])
            P.op("act", lambda e: e.copy(cT[:], ptr[:, 0:5, :]), [b_ptr], [b_cT])

            def mmq(e):
                r = None
                for c0, w in ((0, 512), (512, 256)):
                    for kc in range(3):
                        r = e.matmul(pq[:, c0:c0 + w], cT[:, kc, :], wuq[:, kc, c0:c0 + w], start=(kc == 0), stop=(kc == 2))
                return r
            P.op("pe", mmq, [b_cT, b_w], [b_pq])

            def mmkv(e):
                r = None
                for c0 in (0, 512):
                    for kc in range(2):
                        r = e.matmul(pkv[:, c0:c0 + 512], cT[:, 3 + kc, :], wukv[:, kc, c0:c0 + 512],
                                     start=(kc == 0), stop=(kc == 1))
                return r
            P.op("pe", mmkv, [b_cT, b_w], [b_pkv])

            P.op("dve", lambda e, tb=tb: e.tensor_scalar(ang[:, 16:32], invf[:], posf[:, tb:tb + 1], None, ALU.mult),
                 [b_c], [b_cs])
            P.op("dve", lambda e: e.tensor_scalar(ang[:, 0:16], ang[:, 16:32], math.pi / 2, None, ALU.add),
                 [b_cs], [b_cs])
            P.op("dve", lambda e: e.tensor_scalar(angu[:], ang[:], 1.0 / TWO_PI, None, ALU.mult), [b_cs], [b_cs])
            P.op("dve", lambda e: e.tensor_copy(angi[:], angu[:]), [b_cs], [b_cs])
            P.op("dve", lambda e: e.tensor_copy(angu[:], angi[:]), [b_cs], [b_cs])
            P.op("dve", lambda e: e.scalar_tensor_tensor(ang[:], angu[:], -TWO_PI, ang[:], ALU.mult, ALU.add),
                 [b_cs], [b_cs])
            P.op("dve", lambda e: e.tensor_scalar(angu[:], ang[:], math.pi, None, ALU.is_gt), [b_cs], [b_cs])
            P.op("dve", lambda e: e.scalar_tensor_tensor(ang[:], angu[:], -TWO_PI, ang[:], ALU.mult, ALU.add),
                 [b_cs], [b_cs])
            P.op("dve", lambda e: e.tensor_scalar(angu[:], ang[:], -math.pi, None, ALU.is_lt), [b_cs], [b_cs])
            P.op("dve", lambda e: e.scalar_tensor_tensor(ang[:], angu[:], TWO_PI, ang[:], ALU.mult, ALU.add),
                 [b_cs], [b_cs])
            P.op("act", lambda e: e.activation(cs[:], ang[:], AF.Sin), [b_cs], [b_cs])

            pq3 = pq[:, 0:768].rearrange("p (h d) -> p h d", h=8)
            P.op("act", lambda e: e.activation(sq[:, 0:768], pq[:, 0:768], AF.Square), [b_pq], [b_sq])
            P.op("dve", lambda e: e.tensor_reduce(hs[:, 0:8], sq[:, 0:768].rearrange("p (h d) -> p h d", h=8),
                                                  AX.X, ALU.add), [b_sq], [b_hs])
            rstd_from_sumsq(P, hs[:, 0:8], hs[:, 0:8], QK, EPS, [b_hs], [b_hs])
            P.op("dve", lambda e: e.tensor_tensor(qn[:], pq3, bcast_last(hs[:, 0:8], 96), ALU.mult),
                 [b_pq, b_hs], [b_qn])
            P.op("pool", lambda e: e.tensor_tensor(qn[:], qn[:], bcast_mid(gq[:], 8), ALU.mult), [b_qn, b_c], [b_qn])
            rope(qn, qf, b_qn, b_qf)

            pkv3 = pkv[:].rearrange("p (h d) -> p h d", h=8)
            P.op("act", lambda e: e.activation(sq[:, 0:512].rearrange("p (h d) -> p h d", h=8), pkv3[:, :, 0:64],
                                               AF.Square), [b_pkv], [b_sq])
            P.op("dve", lambda e: e.tensor_reduce(hs[:, 8:16], sq[:, 0:512].rearrange("p (h d) -> p h d", h=8),
                                                  AX.X, ALU.add), [b_sq], [b_hs])
            P.op("dve", lambda e: e.tensor_scalar(hs[:, 8:16], hs[:, 8:16], st[:, 2:3], None, ALU.add),
                 [b_hs, b_st], [b_hs])
            rstd_from_sumsq(P, hs[:, 8:16], hs[:, 8:16], QK, EPS, [b_hs], [b_hs])
            P.op("dve", lambda e: e.tensor_tensor(kn[:, :, 0:64], pkv3[:, :, 0:64], bcast_last(hs[:, 8:16], 64), ALU.mult),
                 [b_pkv, b_hs], [b_kn])
            P.op("dve", lambda e, i=i: e.tensor_tensor(kn[:, :, 64:96], bcast_last(hs[:, 8:16], 32),
                                                      bcast_mid(pm[i][:, 640:672], 8), ALU.mult),
                 [b_pm[i], b_hs], [b_kn])
            P.op("pool", lambda e: e.tensor_tensor(kn[:], kn[:], bcast_mid(gk[:], 8), ALU.mult), [b_kn, b_c], [b_kn])
            rope(kn, kf, b_kn, b_kf)
            P.op("act", lambda e, i=i: e.copy(v1[i][:, :, 0:64], pkv3[:, :, 64:128]), [b_pkv], [b_v1[i]])
            P.dma(g["v1"][tb * 128:(tb + 1) * 128, :], v1[i][:].rearrange("p h d -> p (h d)"), [b_v1[i]], [g["b_v1"]])

            for (src, b_src, dstT, b_dstT) in ((qf, b_qf, qTs, b_qTs), (kf, b_kf, kTs, b_kTs)):
                def trq(e, src=src):
                    r = None
                    for h in range(8):
                        r = e.transpose(ptr[0:96, h, :], src[:, h, :], ident[:])
                    return r
                P.op("pe", trq, [b_src, b_ident], [b_ptr])
                c0 = (tb % 4) * 128
                P.op("act", lambda e, dstT=dstT, sgi=sgi, c0=c0: e.copy(dstT[sgi][:, :, c0:c0 + 128], ptr[0:96, :, :]),
                     [b_ptr], [b_dstT[sgi]])
            if tb % 4 == 3:
                t0 = (tb // 4) * 512
                P.dma(g["qT"][:, :, t0:t0 + 512].rearrange("h d t -> d h t"), qTs[sgi][:], [b_qTs[sgi]], [g["b_qT"]])
                P.dma(g["kT"][:, :, t0:t0 + 512].rearrange("h d t -> d h t"), kTs[sgi][:], [b_kTs[sgi]], [g["b_kT"]])
        P.flush()


def phase_mla_attn(P, g, l, T):
    nc = P.nc
    NB = T // 128
    NG = T // 512
    scale = QK ** -0.5
    with ExitStack() as es:
        def sb(name, shape, dt):
            return es.enter_context(nc.sbuf_tensor("p3_%d_" % l + name, shape, dt))

        def ps(name, shape, dt):
            return es.enter_context(nc.psum_tensor("p3_%d_" % l + name, shape, dt))

        b_c = P.buf()
        trif = sb("trif", [128, 128], F32)
        tri = sb("tri", [128, 128], BF16)
        P.dma(trif[:], g["tri_incl"][:, :], [], [b_c])
        P.op("dve", lambda e: e.tensor_copy(tri[:], trif[:]), [b_c], [b_c])
        onesf = sb("onesf", [128, 128], F32)
        P.dma(onesf[:], g["ones"][:, :], [], [b_c])
        vall = sb("vall", [128, NB, 8 * 65], BF16)
        b_vall = P.buf()
        P.dma(vall[:], g["v1"].rearrange("(b p) c -> p b c", p=128), [g["b_v1"]], [b_vall])
        qT = [sb("qT%d" % i, [96, T], BF16) for i in range(2)]
        kT = [sb("kT%d" % i, [96, T], BF16) for i in range(2)]
        b_qk = [P.buf() for _ in range(2)]
        pS = [ps("pS%d" % i, [128, 512], F32) for i in range(4)]
        b_pS = [P.buf() for _ in range(4)]
        pO = [ps("pO%d" % i, [128, 512], F32) for i in range(2)]
        b_pO = [P.buf() for _ in range(2)]
        pB = ps("pB", [128, 512], F32)
        b_pB = P.buf()
        pT = [sb("pT%d" % i, [128, 512], BF16) for i in range(4)]
        b_pT = [P.buf() for _ in range(4)]
        osb = [sb("osb%d" % i, [128, 512], F32) for i in range(2)]
        b_osb = [P.buf() for _ in range(2)]
        on = [sb("on%d" % i, [64, 512], BF16) for i in range(2)]
        b_on = [P.buf() for _ in range(2)]
        ns = 0
        no = 0
        for h in range(8):
            i = h % 2
            P.dma(qT[i][:], g["qT"][h], [g["b_qT"]], [b_qk[i]])
            P.dma(kT[i][:], g["kT"][h], [g["b_kT"]], [b_qk[i]])
            for gq in range(NG):
                o = no % 2
                no += 1
                nkb = 4 * gq + 4
                pend = []

                def emit_pv(item, o=o, h=h, nkb=nkb):
                    j, s_, oc, N = item
                    P.op("pe", lambda e: e.matmul(
                        pO[o][0:65, oc:oc + N], vall[:, j, h * 65:(h + 1) * 65], pT[s_][:, 0:N],
                        start=(j == 0), stop=(j == nkb - 1)), [b_vall, b_pT[s_]], [b_pO[o]])
                for j in range(nkb):
                    jj = j - 4 * gq
                    q0 = gq * 512 + (jj * 128 if jj > 0 else 0)
                    N = (gq + 1) * 512 - q0
                    s = ns % 4
                    ns += 1
                    P.op("pe", lambda e, i=i, s=s, j=j, q0=q0, N=N: e.matmul(
                        pS[s][:, 0:N], kT[i][:, j * 128:(j + 1) * 128], qT[i][:, q0:q0 + N], start=True, stop=True),
                        [b_qk[i]], [b_pS[s]])
                    P.op("act", lambda e, s=s, N=N: e.activation(pT[s][:, 0:N], pS[s][:, 0:N], AF.Exp, scale=scale),
                         [b_pS[s]], [b_pT[s]])
                    if jj >= 0:
                        P.op("dve", lambda e, s=s: e.tensor_tensor(pT[s][:, 0:128], pT[s][:, 0:128], tri[:], ALU.mult),
                             [b_pT[s], b_c], [b_pT[s]])
                    pend.append((j, s, q0 - gq * 512, N))
                    if len(pend) > 2:
                        emit_pv(pend.pop(0))
                while pend:
                    emit_pv(pend.pop(0))
                P.op("act", lambda e, o=o: e.copy(osb[o][0:65, :], pO[o][0:65, :]), [b_pO[o]], [b_osb[o]])
                P.op("dve", lambda e, o=o: e.reciprocal(osb[o][64:65, :], osb[o][64:65, :]), [b_osb[o]], [b_osb[o]])
                P.op("pe", lambda e, o=o: e.matmul(pB[0:64, :], onesf[64:65, 0:64], osb[o][64:65, :], start=True, stop=True),
                     [b_osb[o], b_c], [b_pB])
                P.op("dve", lambda e, o=o: e.tensor_tensor(on[o][:], osb[o][0:64, :], pB[0:64, :], ALU.mult),
                     [b_osb[o], b_pB], [b_on[o]])
                P.dma(g["omlaT"][h * 64:(h + 1) * 64, gq * 512:(gq + 1) * 512], on[o][:], [b_on[o]], [g["b_omlaT"]])
        P.flush()


def build_program(T, nlayers=DEPTH, upto="all", debug=False):
    nc = bass.Bass("TRN2", target_bir_lowering=False)
    g = {}
    x_ap = nc.dram_tensor("x", [T, D], F32, kind="ExternalInput").ap()
    g["positions"] = nc.dram_tensor("positions", [128, T // 128], I32, kind="ExternalInput").ap()
    for k, shp in PARAM_SHAPES.items():
        g[k] = nc.dram_tensor(k, list(shp), F32, kind="ExternalInput").ap()
    for k, shp in CONST_SHAPES.items():
        g[k] = nc.dram_tensor(k, list(shp), F32, kind="ExternalInput").ap()
    y_ap = nc.dram_tensor("y", [T, D], F32, kind="ExternalOutput").ap()
    skind = "ExternalOutput" if debug else "Internal"

    def scratch(name, shape, dt):
        return nc.dram_tensor(name, shape, dt, kind=skind).ap()

    P = Prog(nc)
    g["skip_rwkv"] = SKIP_RWKV
    g["proj"] = scratch("proj", [T, IN_DIM], F32)
    g["qT"] = scratch("qT", [8, 96, T], BF16)
    g["kT"] = scratch("kT", [8, 96, T], BF16)
    g["v1"] = scratch("v1", [T, 8 * 65], BF16)
    g["omlaT"] = scratch("omlaT", [512, T], BF16)
    g["orwkvT"] = scratch("orwkvT", [512, T], BF16)
    g["ossmT"] = scratch("ossmT", [1024, T], BF16)
    g["vfirst"] = scratch("vfirst", [T, RW], F32)
    g["xa"] = scratch("xa", [T, D], F32)
    g["xb"] = scratch("xb", [T, D], F32)
    for k in ("proj", "qT", "kT", "v1", "omlaT", "orwkvT", "ossmT", "vfirst", "xa", "xb", "y", "x"):
        g["b_" + k] = P.buf(k)
    order = ["inproj", "mla_prep", "mla_attn", "rwkv", "ssm", "merge", "ffn"]
    stop = order.index(upto) if upto != "all" else len(order) - 1
    cur, b_cur = x_ap, g["b_x"]
    last_bufs = []
    for l in range(nlayers):
        final_layer = (l == nlayers - 1)
        phase_inproj(P, g, l, cur, b_cur, T)
        last_bufs = [g["b_proj"]]
        if stop >= 1 and "mla" not in SKIP:
            phase_mla_prep(P, g, l, T)
            last_bufs = [g["b_qT"], g["b_kT"], g["b_v1"]]
        if stop >= 2 and "mla" not in SKIP:
            phase_mla_attn(P, g, l, T)
            last_bufs = [g["b_omlaT"]]
        if stop >= 3 and not g.get("skip_rwkv"):
            phase_rwkv(P, g, l, T)
            last_bufs.append(g["b_orwkvT"])
            last_bufs.append(g["b_vfirst"])
        if stop >= 4 and "ssm" not in SKIP:
            phase_ssm(P, g, l, T)
            last_bufs.append(g["b_ossmT"])
        if stop >= 5:
            phase_merge(P, g, l, cur, b_cur, g["xa"], g["b_xa"], T)
            last_bufs = [g["b_xa"]]
        if stop >= 6:
            dst, b_dst = (y_ap, g["b_y"]) if final_layer else (g["xb"], g["b_xb"])
            phase_ffn(P, g, l, g["xa"], g["b_xa"], dst, b_dst, T)
            cur, b_cur = dst, b_dst
            last_bufs = [b_dst]
    for eng in ("sp", "pool"):
        P.final_wait(eng, last_bufs + [g["b_" + k] for k in ("proj", "qT", "kT", "v1", "omlaT", "orwkvT", "ossmT", "vfirst", "xa", "xb", "y")])
    P.ops["sp"].append(("o", lambda e: e.nop(), P.sem["sp"], 1))
    P.flush()
    return nc, P


def phase_merge(P, g, l, x_in, b_xin_d, x_out, b_xout_d, T):
    nc = P.nc
    NB = T // 128
    proj, b_proj = g["proj"], g["b_proj"]
    with ExitStack() as es:
        def sb(name, shape, dt):
            return es.enter_context(nc.sbuf_tensor("p5_%d_" % l + name, shape, dt))

        def ps(name, shape, dt):
            return es.enter_context(nc.psum_tensor("p5_%d_" % l + name, shape, dt))

        identf, ident, b_ident = load_ident(P, nc, es, g, "p5_%d_" % l)
        wst = [sb("wst%d" % i, [128, 1024], F32) for i in range(2)]
        b_wst = [P.buf() for _ in range(2)]
        wbr = sb("wbr", [128, 16, 1024], BF16)
        wout = sb("wout", [128, 8, 1024], BF16)
        b_w = P.buf()
        n = 0
        for (nm, k0, nk) in (("w_br_mla", 0, 4), ("w_br_rwkv", 4, 4), ("w_br_ssm", 8, 8)):
            for kc in range(nk):
                i = n % 2
                n += 1
                P.dma(wst[i][:], g[nm][l, kc * 128:(kc + 1) * 128, :], [], [b_wst[i]])
                cast_split(P, n, wbr[:, k0 + kc, :], wst[i][:], [b_wst[i]], [b_w])
        for kc in range(8):
            i = n % 2
            n += 1
            P.dma(wst[i][:], g["w_out"][l, kc * 128:(kc + 1) * 128, :], [], [b_wst[i]])
            cast_split(P, n, wout[:, kc, :], wst[i][:], [b_wst[i]], [b_w])

        pg = [sb("pg%d" % i, [128, 3072], F32) for i in range(2)]
        b_pg = [P.buf() for _ in range(2)]
        oT = [sb("oT%d" % i, [128, 16, 128], BF16) for i in range(2)]
        b_oT = [P.buf() for _ in range(2)]
        xin = [sb("xin%d" % i, [128, D], F32) for i in range(2)]
        b_xin = [P.buf() for _ in range(2)]
        py = [ps("py%d" % i, [128, 1024], F32) for i in range(3)]
        b_py = [P.buf() for _ in range(3)]
        ptr = ps("ptr", [128, 8, 128], BF16)
        b_ptr = P.buf()
        mg = sb("mg", [128, 1024], F32)
        tmp = sb("tmp", [128, 1024], F32)
        b_mg = P.buf()
        b_tmp = P.buf()
        mb = sb("mb", [128, 1024], BF16)
        b_mb = P.buf()
        mT = sb("mT", [128, 8, 128], BF16)
        b_mT = P.buf()
        xo = [sb("xo%d" % i, [128, D], F32) for i in range(2)]
        b_xo = [P.buf() for _ in range(2)]
        for tb in range(NB):
            i = tb % 2
            ts = slice(tb * 128, (tb + 1) * 128)
            P.dma(pg[i][:], proj[ts, C_GATE:IN_DIM], [b_proj], [b_pg[i]])
            P.dma(oT[i][:, 0:4, :], g["omlaT"][:, ts].rearrange("(k p) t -> p k t", p=128), [g["b_omlaT"]], [b_oT[i]])
            P.dma(oT[i][:, 4:8, :], g["orwkvT"][:, ts].rearrange("(k p) t -> p k t", p=128), [g["b_orwkvT"]], [b_oT[i]])
            P.dma(oT[i][:, 8:16, :], g["ossmT"][:, ts].rearrange("(k p) t -> p k t", p=128), [g["b_ossmT"]], [b_oT[i]])
            P.dma(xin[i][:], x_in[ts, :], [b_xin_d], [b_xin[i]])
            P.op("act", lambda e, i=i: e.activation(pg[i][:], pg[i][:], AF.Sigmoid), [b_pg[i]], [b_pg[i]])
            for br, (k0, nk) in enumerate(((0, 4), (4, 4), (8, 8))):
                def mm(e, i=i, br=br, k0=k0, nk=nk):
                    r = None
                    for c0 in (0, 512):
                        for kc in range(nk):
                            r = e.matmul(py[br][:, c0:c0 + 512], oT[i][:, k0 + kc, :], wbr[:, k0 + kc, c0:c0 + 512],
                                         start=(kc == 0), stop=(kc == nk - 1))
                    return r
                P.op("pe", mm, [b_oT[i], b_w], [b_py[br]])
            P.op("dve", lambda e, i=i: e.tensor_tensor(mg[:], py[0][:], pg[i][:, 0:1024], ALU.mult),
                 [b_py[0], b_pg[i]], [b_mg])
            P.op("dve", lambda e, i=i: e.tensor_tensor(tmp[:], py[1][:], pg[i][:, 1024:2048], ALU.mult),
                 [b_py[1], b_pg[i]], [b_tmp])
            P.op("pool", lambda e: e.tensor_tensor(mg[:], mg[:], tmp[:], ALU.add), [b_mg, b_tmp], [b_mg])
            P.op("dve", lambda e, i=i: e.tensor_tensor(tmp[:], py[2][:], pg[i][:, 2048:3072], ALU.mult),
                 [b_py[2], b_pg[i]], [b_tmp])
            P.op("pool", lambda e: e.tensor_tensor(mb[:], mg[:], tmp[:], ALU.add), [b_mg, b_tmp], [b_mb])

            def tr(e):
                r = None
                for kc in range(8):
                    r = e.transpose(ptr[:, kc, :], mb[:, kc * 128:(kc + 1) * 128], ident[:])
                return r
            P.op("pe", tr, [b_mb, b_ident], [b_ptr])
            P.op("act", lambda e: e.copy(mT[:], ptr[:]), [b_ptr], [b_mT])

            def mmo(e):
                r = None
                for c0 in (0, 512):
                    for kc in range(8):
                        r = e.matmul(py[0][:, c0:c0 + 512], mT[:, kc, :], wout[:, kc, c0:c0 + 512],
                                     start=(kc == 0), stop=(kc == 7))
                return r
            P.op("pe", mmo, [b_mT, b_w], [b_py[0]])
            P.op("dve", lambda e, i=i: e.tensor_tensor(xo[i][:], py[0][:], xin[i][:], ALU.add),
                 [b_py[0], b_xin[i]], [b_xo[i]])
            P.dma(x_out[ts, :], xo[i][:], [b_xo[i]], [b_xout_d])
        P.flush()


def phase_ssm(P, g, l, T):
    nc = P.nc
    NB = T // 128
    proj, b_proj = g["proj"], g["b_proj"]
    CX = C_SSM + SSM_D
    CDT = CX + SSM_CONV_DIM
    with ExitStack() as es:
        def sb(name, shape, dt):
            return es.enter_context(nc.sbuf_tensor("p4_%d_" % l + name, shape, dt))

        def ps(name, shape, dt):
            return es.enter_context(nc.psum_tensor("p4_%d_" % l + name, shape, dt))

        identf, ident, b_ident = load_ident(P, nc, es, g, "p4_%d_" % l)
        b_c = P.buf()
        cw = sb("cw", [128, 4, SSM_CONV_DIM], F32)
        cb = sb("cb", [128, SSM_CONV_DIM], F32)
        for j in range(4):
            P.dma(cw[:, j, :], g["ssm_conv_w"][l, j].partition_broadcast(128), [], [b_c])
        P.dma(cb[:], g["ssm_conv_b"][l].partition_broadcast(128), [], [b_c])
        sm = sb("sm", [128, 48], F32)
        P.dma(sm[:, 0:16], g["ssm_dt_bias"][l].partition_broadcast(128), [], [b_c])
        P.dma(sm[:, 16:32], g["ssm_a_log"][l].partition_broadcast(128), [], [b_c])
        P.dma(sm[:, 32:48], g["ssm_d"][l].partition_broadcast(128), [], [b_c])
        P.op("act", lambda e: e.activation(sm[:, 16:32], sm[:, 16:32], AF.Exp), [b_c], [b_c])
        P.op("dve", lambda e: e.tensor_scalar(sm[:, 16:32], sm[:, 16:32], -1.0, None, ALU.mult), [b_c], [b_c])
        ng = sb("ng", [128, SSM_D], F32)
        P.dma(ng[:], g["ssm_norm_g"][l].partition_broadcast(128), [], [b_c])
        tri_incl = sb("tri_incl", [128, 128], F32)
        tri_gt = sb("tri_gt", [128, 128], F32)
        onesf = sb("onesf", [128, 128], F32)
        P.dma(tri_incl[:], g["tri_incl"][:, :], [], [b_c])
        P.dma(tri_gt[:], g["tri_gt"][:, :], [], [b_c])
        P.dma(onesf[:], g["ones"][:, :], [], [b_c])

        xs4 = [sb("xs4_%d" % j, [128, SSM_CONV_DIM], F32) for j in range(4)]
        b_xs4 = [P.buf() for _ in range(4)]
        xc = sb("xc", [128, SSM_CONV_DIM], F32)
        b_xc = P.buf()
        zt = sb("zt", [128, SSM_D], F32)
        b_zt = P.buf()
        dtt = sb("dtt", [128, 128], F32)
        b_dt = P.buf()
        pA = ps("pA", [128, 512], F32)
        b_pA = P.buf()
        ptr = ps("ptr", [128, 8, 128], BF16)
        b_ptr = P.buf()
        pD = [ps("pD%d" % i, [128, 512], F32) for i in range(2)]
        b_pD = [P.buf() for _ in range(2)]
        pY = ps("pY", [128, 1024], F32)
        b_pY = P.buf()
        pOff = ps("pOff", [128, 512], F32)
        b_pOff = P.buf()
        pSt = ps("pSt", [128, 512], F32)
        b_pSt = P.buf()
        xdt = sb("xdt", [128, SSM_D], BF16)
        b_xdt = P.buf()
        xdte = sb("xdte", [128, SSM_D], BF16)
        b_xdte = P.buf()
        bcb = sb("bcb", [128, 512], BF16)
        b_bcb = P.buf()
        bcT = sb("bcT", [128, 4, 128], BF16)
        b_bcT = P.buf()
        cbm = sb("cbm", [128, 2, 128], F32)
        b_cbm = P.buf()
        rhsd = sb("rhsd", [128, 16, 128], F32)
        b_rhsd = P.buf()
        eD = [sb("eD%d" % i, [128, 4, 128], F32) for i in range(2)]
        b_eD = [P.buf() for _ in range(2)]
        MT = sb("MT", [128, 16, 128], BF16)
        b_MT = P.buf()
        ST = sb("ST", [128, SSM_D], F32)
        STb = sb("STb", [128, SSM_D], BF16)
        b_ST = P.buf()
        b_STb = P.buf()
        yb = sb("yb", [128, SSM_D], F32)
        b_yb = P.buf()
        t2 = sb("t2", [128, SSM_D], F32)
        b_t2 = P.buf()
        ssg = sb("ssg", [128, 2], F32)
        b_ssg = P.buf()
        yn = sb("yn", [128, SSM_D], BF16)
        b_yn = P.buf()
        yT = [sb("yT%d" % i, [128, 8, 128], BF16) for i in range(2)]
        b_yT = [P.buf() for _ in range(2)]
        P.op("pool", lambda e: e.memset(ST[:], 0.0), [], [b_ST])
        P.op("pool", lambda e: e.memset(STb[:], 0.0), [], [b_STb])

        for tb in range(NB):
            t0 = tb * 128
            for j in range(4):
                sh = 3 - j
                if tb == 0 and sh > 0:
                    P.op("pool", lambda e, j=j: e.memset(xs4[j][0:4, :], 0.0), [], [b_xs4[j]])
                    P.dma(xs4[j][sh:128, :], proj[0:128 - sh, CX:CX + SSM_CONV_DIM], [b_proj], [b_xs4[j]])
                else:
                    P.dma(xs4[j][:], proj[t0 - sh:t0 - sh + 128, CX:CX + SSM_CONV_DIM], [b_proj], [b_xs4[j]])
            P.dma(zt[:], proj[t0:t0 + 128, C_SSM:C_SSM + SSM_D], [b_proj], [b_zt])
            P.dma(dtt[:, 0:16], proj[t0:t0 + 128, CDT:CDT + 16], [b_proj], [b_dt])
            P.op("dve", lambda e: e.tensor_tensor(xs4[0][:], xs4[0][:], cw[:, 0, :], ALU.mult), [b_xs4[0], b_c], [b_xs4[0]])
            P.op("pool", lambda e: e.tensor_tensor(xs4[1][:], xs4[1][:], cw[:, 1, :], ALU.mult), [b_xs4[1], b_c], [b_xs4[1]])
            P.op("dve", lambda e: e.tensor_tensor(xs4[2][:], xs4[2][:], cw[:, 2, :], ALU.mult), [b_xs4[2], b_c], [b_xs4[2]])
            P.op("pool", lambda e: e.tensor_tensor(xs4[3][:], xs4[3][:], cw[:, 3, :], ALU.mult), [b_xs4[3], b_c], [b_xs4[3]])
            P.op("dve", lambda e: e.tensor_tensor(xs4[0][:], xs4[0][:], xs4[1][:], ALU.add), [b_xs4[0], b_xs4[1]], [b_xs4[0]])
            P.op("pool", lambda e: e.tensor_tensor(xs4[2][:], xs4[2][:], xs4[3][:], ALU.add), [b_xs4[2], b_xs4[3]], [b_xs4[2]])
            P.op("dve", lambda e: e.tensor_tensor(xs4[0][:], xs4[0][:], xs4[2][:], ALU.add), [b_xs4[0], b_xs4[2]], [b_xs4[0]])
            P.op("pool", lambda e: e.tensor_tensor(xs4[0][:], xs4[0][:], cb[:], ALU.add), [b_xs4[0], b_c], [b_xs4[0]])
            P.op("act", lambda e: e.activation(xc[:], xs4[0][:], AF.Silu), [b_xs4[0]], [b_xc])
            P.op("act", lambda e: e.activation(zt[:], zt[:], AF.Silu), [b_zt], [b_zt])
            P.op("dve", lambda e: e.tensor_tensor(dtt[:, 0:16], dtt[:, 0:16], sm[:, 0:16], ALU.add), [b_dt, b_c], [b_dt])
            P.op("act", lambda e: e.activation(dtt[:, 0:16], dtt[:, 0:16], AF.Exp), [b_dt], [b_dt])
            P.op("act", lambda e: e.activation(dtt[:, 0:16], dtt[:, 0:16], AF.Ln, bias=1.0), [b_dt], [b_dt])
            P.op("dve", lambda e: e.tensor_tensor(dtt[:, 16:32], dtt[:, 0:16], sm[:, 16:32], ALU.mult), [b_dt, b_c], [b_dt])

            def mmcs(e):
                e.matmul(pA[:, 0:16], tri_incl[:], dtt[:, 16:32], start=True, stop=True)
                return e.matmul(pA[:, 16:32], onesf[:], dtt[:, 16:32], start=True, stop=True)
            P.op("pe", mmcs, [b_dt, b_c], [b_pA])
            P.op("dve", lambda e: e.tensor_copy(dtt[:, 96:128], pA[:, 0:32]), [b_pA], [b_dt])
            P.op("act", lambda e: e.activation(dtt[:, 32:48], dtt[:, 96:112], AF.Exp), [b_dt], [b_dt])
            P.op("dve", lambda e: e.tensor_tensor(dtt[:, 48:64], dtt[:, 112:128], dtt[:, 96:112], ALU.subtract), [b_dt], [b_dt])
            P.op("act", lambda e: e.activation(dtt[:, 48:64], dtt[:, 48:64], AF.Exp), [b_dt], [b_dt])
            P.op("act", lambda e: e.activation(dtt[:, 64:80], dtt[:, 112:128], AF.Exp), [b_dt], [b_dt])
            P.op("dve", lambda e: e.tensor_tensor(dtt[:, 80:96], dtt[:, 0:16], dtt[:, 48:64], ALU.mult), [b_dt], [b_dt])
            xc3 = xc[:, 0:SSM_D].rearrange("p (e d) -> p e d", e=16)
            P.op("dve", lambda e: e.tensor_tensor(xdt[:].rearrange("p (e d) -> p e d", e=16), xc3,
                                                  bcast_last(dtt[:, 0:16], 64), ALU.mult), [b_xc, b_dt], [b_xdt])
            P.op("pool", lambda e: e.tensor_tensor(xdte[:].rearrange("p (e d) -> p e d", e=16), xc3,
                                                   bcast_last(dtt[:, 80:96], 64), ALU.mult), [b_xc, b_dt], [b_xdte])
            P.op("pool", lambda e: e.tensor_copy(bcb[:], xc[:, SSM_D:SSM_D + 512]), [b_xc], [b_bcb])

            def trbc(e):
                r = None
                for k in range(4):
                    r = e.transpose(ptr[:, k, :], bcb[:, k * 128:(k + 1) * 128], ident[:])
                return r
            P.op("pe", trbc, [b_bcb, b_ident], [b_ptr])
            P.op("act", lambda e: e.copy(bcT[:], ptr[:, 0:4, :]), [b_ptr], [b_bcT])

            def mmcb(e):
                e.matmul(pA[:, 128:256], bcT[:, 0, :], bcT[:, 2, :], start=True, stop=True)
                return e.matmul(pA[:, 256:384], bcT[:, 1, :], bcT[:, 3, :], start=True, stop=True)
            P.op("pe", mmcb, [b_bcT], [b_pA])
            P.op("dve", lambda e: e.tensor_tensor(cbm[:], pA[:, 128:384].rearrange("p (g l) -> p g l", g=2),
                                                  bcast_mid(tri_incl[:], 2), ALU.mult), [b_pA, b_c], [b_cbm])
            P.op("pool", lambda e: e.tensor_tensor(rhsd[:], bcast_mid(tri_incl[:], 16), bcast_last(dtt[:, 16:32], 128),
                                                   ALU.mult), [b_dt, b_c], [b_rhsd])
            for q4 in range(4):
                k = q4 % 2
                P.op("pe", lambda e, q4=q4, k=k: e.matmul(pD[k][:], tri_gt[:],
                                                          rhsd[:, q4 * 4:(q4 + 1) * 4, :].rearrange("p e l -> p (e l)"),
                                                          start=True, stop=True), [b_rhsd, b_c], [b_pD[k]])
                P.op("act", lambda e, k=k: e.activation(eD[k][:].rearrange("p e l -> p (e l)"), pD[k][:], AF.Exp),
                     [b_pD[k]], [b_eD[k]])
                gi = q4 // 2
                P.op("dve", lambda e, q4=q4, k=k, gi=gi: e.tensor_tensor(MT[:, q4 * 4:(q4 + 1) * 4, :], eD[k][:],
                                                                        bcast_mid(cbm[:, gi, :], 4), ALU.mult),
                     [b_eD[k], b_cbm], [b_MT])

            def mmy(e):
                r = None
                for hh in range(16):
                    r = e.matmul(pY[:, hh * 64:(hh + 1) * 64], MT[:, hh, :], xdt[:, hh * 64:(hh + 1) * 64],
                                 start=True, stop=True)
                return r
            P.op("pe", mmy, [b_MT, b_xdt], [b_pY])
            for gi in range(2):
                P.op("pe", lambda e, gi=gi: e.matmul(pOff[:], bcT[:, 2 + gi, :], STb[:, gi * 512:(gi + 1) * 512],
                                                     start=True, stop=True), [b_bcT, b_STb], [b_pOff])
                P.op("dve", lambda e, gi=gi: e.tensor_tensor(
                    yb[:, gi * 512:(gi + 1) * 512].rearrange("p (e d) -> p e d", e=8),
                    pOff[:].rearrange("p (e d) -> p e d", e=8),
                    bcast_last(dtt[:, 32 + gi * 8:32 + (gi + 1) * 8], 64), ALU.mult), [b_pOff, b_dt], [b_yb])
            P.op("dve", lambda e: e.tensor_tensor(yb[:], yb[:], pY[:], ALU.add), [b_yb, b_pY], [b_yb])
            P.op("pool", lambda e: e.tensor_tensor(t2[:].rearrange("p (e d) -> p e d", e=16), xc3,
                                                   bcast_last(sm[:, 32:48], 64), ALU.mult), [b_xc, b_c], [b_t2])
            P.op("pool", lambda e: e.tensor_tensor(yb[:], yb[:], t2[:], ALU.add), [b_yb, b_t2], [b_yb])
            P.op("dve", lambda e: e.tensor_tensor(yb[:], yb[:], zt[:], ALU.mult), [b_yb, b_zt], [b_yb])
            for gi in range(2):
                P.op("act", lambda e, gi=gi: e.activation(t2[:, gi * 512:(gi + 1) * 512], yb[:, gi * 512:(gi + 1) * 512],
                                                          AF.Square, accum_out=ssg[:, gi:gi + 1]), [b_yb], [b_t2, b_ssg])
            rstd_from_sumsq(P, ssg[:], ssg[:], 512, 1e-5, [b_ssg], [b_ssg])
            for gi in range(2):
                P.op("dve", lambda e, gi=gi: e.scalar_tensor_tensor(
                    yn[:, gi * 512:(gi + 1) * 512], yb[:, gi * 512:(gi + 1) * 512], ssg[:, gi:gi + 1],
                    ng[:, gi * 512:(gi + 1) * 512], ALU.mult, ALU.mult), [b_yb, b_ssg, b_c], [b_yn])

            def tro(e):
                r = None
                for kc in range(8):
                    r = e.transpose(ptr[:, kc, :], yn[:, kc * 128:(kc + 1) * 128], ident[:])
                return r
            P.op("pe", tro, [b_yn, b_ident], [b_ptr])
            i = tb % 2
            P.op("act", lambda e, i=i: e.copy(yT[i][:], ptr[:]), [b_ptr], [b_yT[i]])
            P.dma(g["ossmT"][:, t0:t0 + 128].rearrange("(k p) t -> p k t", p=128), yT[i][:], [b_yT[i]], [g["b_ossmT"]])
            for gi in range(2):
                P.op("pe", lambda e, gi=gi: e.matmul(pSt[:], bcb[:, gi * 128:(gi + 1) * 128],
                                                     xdte[:, gi * 512:(gi + 1) * 512], start=True, stop=True),
                     [b_bcb, b_xdte], [b_pSt])
                P.op("dve", lambda e, gi=gi: e.tensor_tensor(
                    ST[:, gi * 512:(gi + 1) * 512].rearrange("p (e d) -> p e d", e=8),
                    ST[:, gi * 512:(gi + 1) * 512].rearrange("p (e d) -> p e d", e=8),
                    bcast_last(dtt[:, 64 + gi * 8:64 + (gi + 1) * 8], 64), ALU.mult), [b_ST, b_dt], [b_ST])
                P.op("dve", lambda e, gi=gi: e.tensor_tensor(ST[:, gi * 512:(gi + 1) * 512], ST[:, gi * 512:(gi + 1) * 512],
                                                            pSt[:], ALU.add), [b_ST, b_pSt], [b_ST])
            P.op("pool", lambda e: e.tensor_copy(STb[:], ST[:]), [b_ST], [b_STb])
        P.flush()


def phase_rwkv(P, g, l, T):
    nc = P.nc
    NB = T // 128
    proj, b_proj = g["proj"], g["b_proj"]
    NEG_E = -math.exp(-0.5)
    with ExitStack() as es:
        def sb(name, shape, dt):
            return es.enter_context(nc.sbuf_tensor("p7_%d_" % l + name, shape, dt))

        def ps(name, shape, dt):
            return es.enter_context(nc.psum_tensor("p7_%d_" % l + name, shape, dt))

        identf, ident, b_ident = load_ident(P, nc, es, g, "p7_%d_" % l)
        b_c = P.buf()

        def const(name, key):
            t = sb(name, [128, 128], F32)
            P.dma(t[:], g[key][:, :], [], [b_c])
            return t
        tri_incl_bd = const("tri_incl_bd", "tri_incl_bd")
        tri_excl_bd = const("tri_excl_bd", "tri_excl_bd")
        tri_after_bd = const("tri_after_bd", "tri_after_bd")
        low_strict_bd = const("low_strict_bd", "low_strict_bd")
        onesf = const("onesf", "ones")

        def bparam(name, ap):
            t = sb(name, [128, RW], F32)
            P.dma(t[:], ap.partition_broadcast(128), [], [b_c])
            return t
        w0 = bparam("w0", g["rwkv_w0"][l])
        a0 = bparam("a0", g["rwkv_a0"][l])
        k_k = bparam("k_k", g["rwkv_k_k"][l])
        k_a = bparam("k_a", g["rwkv_k_a"][l])
        r_k = bparam("r_k", g["rwkv_r_k"][l])
        ln_g = bparam("ln_g", g["rwkv_ln_g"][l])
        ln_b = bparam("ln_b", g["rwkv_ln_b"][l])
        w2p = sb("w2p", [128, RW], F32)
        a2p = sb("a2p", [128, RW], F32)
        P.op("pool", lambda e: e.memset(w2p[:], 0.0), [], [b_c])
        P.op("pool", lambda e: e.memset(a2p[:], 0.0), [], [b_c])
        P.dma(w2p[0:64, :], g["rwkv_w2"][l], [], [b_c])
        P.dma(a2p[64:128, :], g["rwkv_a2"][l], [], [b_c])
        g2 = sb("g2", [128, RW], F32)
        P.dma(g2[:], g["rwkv_g2"][l], [], [b_c])
        if l > 0:
            v0 = bparam("v0", g["rwkv_v0"][l - 1])
            v1t = sb("v1t", [128, 4, 32], F32)
            P.dma(v1t[:], g["rwkv_v1"][l - 1].rearrange("(k p) c -> p k c", p=128), [], [b_c])
            v2t = sb("v2t", [32, RW], F32)
            P.dma(v2t[:], g["rwkv_v2"][l - 1], [], [b_c])

        pb = [ps("pb%d" % i, [128, 512], F32) for i in range(8)]
        b_pb = [P.buf("pb%d" % i) for i in range(8)]
        ptrb = None

        def t512(name):
            return sb(name, [128, RW], F32), P.buf(name)
        pr = sb("pr", [128, RWKV_IN], F32)
        b_pr = P.buf()
        lx, b_lx = sb("lx", [128, 256], F32), P.buf()
        lxT, b_lxT = sb("lxT", [128, 2, 128], F32), P.buf()
        lw, b_lw = t512("lw")
        asg, b_asg = t512("asg")
        gg, b_gg = t512("gg")
        kkn, b_kkn = t512("kkn")
        kf, b_kf = t512("kf")
        bv, b_bv = t512("bv")
        tmp, b_tmp = t512("tmp")
        tmp2, b_tmp2 = t512("tmp2")
        E1, b_E1 = t512("E1")
        E2, b_E2 = t512("E2")
        E3, b_E3 = t512("E3")
        Ee, b_Ee = t512("Ee")
        At, b_At = t512("At")
        Bt, b_Bt = t512("Bt")
        Kt, b_Kt = t512("Kt")
        Rt, b_Rt = t512("Rt")
        Bend, b_Bend = t512("Bend")
        Kend, b_Kend = t512("Kend")
        hs, b_hs = sb("hs", [128, 32], F32), P.buf()
        gCt, b_gCt = sb("gCt", [128, 8], F32), P.buf()
        ART, b_ART = sb("ART", [128, 4, 2, 128], F32), P.buf()
        BtT, b_BtT = sb("BtT", [128, 8, 128], F32), P.buf()
        KtT, b_KtT = sb("KtT", [128, 8, 128], F32), P.buf()
        P.op("pool", lambda e: e.memset(BtT[:], 0.0), [], [b_BtT])
        P.op("pool", lambda e: e.memset(KtT[:], 0.0), [], [b_KtT])

        def t8(name, dt=F32):
            return sb(name, [128, 8, 128], dt), P.buf(name)
        Pm = [t8("Pm0", BF16), t8("Pm1", BF16)]
        PTm = [t8("PTm0", BF16), t8("PTm1", BF16)]
        TTm = [t8("TTm0", BF16), t8("TTm1", BF16)]
        TTf = t8("TTf")
        MakT, b_MakT = t8("MakT")
        MrbT, b_MrbT = t8("MrbT")
        MrkT, b_MrkT = t8("MrkT")
        S0T, b_S0T = sb("S0T", [128, 8, 64], F32), P.buf()
        Xs, b_Xs = t512("Xs")
        SAc = [t512("SAs0"), t512("SAs1")]
        Vc = [t512("Vc0"), t512("Vc1")]
        P.op("pool", lambda e: e.memset(Xs[:], 0.0), [], [b_Xs])
        for c_ in range(2):
            P.op("pool", lambda e, c_=c_: e.memset(SAc[c_][0][:], 0.0), [], [SAc[c_][1]])
            P.op("pool", lambda e, c_=c_: e.memset(Vc[c_][0][:], 0.0), [], [Vc[c_][1]])
        Ys, b_Ys = t512("Ys")
        yo, b_yo = sb("yo", [128, RW], BF16), P.buf()
        ptrO = None
        oT = [sb("oT%d" % i, [128, 4, 128], BF16) for i in range(2)]
        b_oT = [P.buf() for _ in range(2)]
        if l > 0:
            vT, b_vT = sb("vT", [128, 4, 128], F32), P.buf()
            vv, b_vv = sb("vv", [128, 32], F32), P.buf()
            vvT, b_vvT = sb("vvT", [32, 128], F32), P.buf()
            vf, b_vf = t512("vf")
        P.op("pool", lambda e: e.memset(S0T[:], 0.0), [], [b_S0T])

        r_ = pr[:, 0:512]
        k_ = pr[:, 512:1024]
        v_ = pr[:, 1024:1536]

        def h3(ap):
            return ap.rearrange("p (h d) -> p h d", h=8)

        for tb in range(NB):
            t0 = tb * 128
            P.dma(pr[:], proj[t0:t0 + 128, C_RWKV:C_SSM], [b_proj], [b_pr])
            P.op("act", lambda e: e.activation(lx[:, 0:64], pr[:, 1536:1600], AF.Tanh), [b_pr], [b_lx])
            P.op("act", lambda e: e.activation(lx[:, 128:256], pr[:, 1664:1792], AF.Sigmoid), [b_pr], [b_lx])
            P.op("pool", lambda e: e.tensor_copy(lx[:, 64:128], pr[:, 1600:1664]), [b_pr], [b_lx])

            def tr_lx(e):
                e.transpose(pb[0][:, 0:128], lx[:, 0:128], identf[:])
                return e.transpose(pb[0][:, 128:256], lx[:, 128:256], identf[:])
            P.op("pe", tr_lx, [b_lx, b_ident], [b_pb[0]])
            P.op("dve", lambda e: e.tensor_copy(lxT[:].rearrange("p a t -> p (a t)"), pb[0][:, 0:256]), [b_pb[0]], [b_lxT])
            P.op("pe", lambda e: e.matmul(pb[1][:], lxT[:, 0, :], w2p[:], start=True, stop=True),
                 [b_lxT, b_c], [b_pb[1]])
            P.op("pe", lambda e: e.matmul(pb[2][:], lxT[:, 0, :], a2p[:], start=True, stop=True),
                 [b_lxT, b_c], [b_pb[2]])
            P.op("pe", lambda e: e.matmul(pb[3][:], lxT[:, 1, :], g2[:], start=True, stop=True), [b_lxT, b_c], [b_pb[3]])
            P.op("dve", lambda e: e.tensor_tensor(lw[:], pb[1][:], w0[:], ALU.add), [b_pb[1], b_c], [b_lw])
            P.op("act", lambda e: e.activation(lw[:], lw[:], AF.Sigmoid), [b_lw], [b_lw])
            P.op("dve", lambda e: e.tensor_scalar(lw[:], lw[:], NEG_E, None, ALU.mult), [b_lw], [b_lw])
            P.op("dve", lambda e: e.tensor_tensor(asg[:], pb[2][:], a0[:], ALU.add), [b_pb[2], b_c], [b_asg])
            P.op("act", lambda e: e.activation(asg[:], asg[:], AF.Sigmoid), [b_asg], [b_asg])
            P.op("act", lambda e: e.copy(gg[:], pb[3][:]), [b_pb[3]], [b_gg])
            if RWKV_STAGE <= 1:
                continue
            if l == 0:
                P.dma(g["vfirst"][t0:t0 + 128, :], v_, [b_pr], [g["b_vfirst"]])
            else:
                P.dma(vf[:], g["vfirst"][t0:t0 + 128, :], [g["b_vfirst"]], [b_vf])

                def tr_v(e):
                    r = None
                    for kc in range(4):
                        r = e.transpose(pb[0][:, kc * 128:(kc + 1) * 128], pr[:, 1024 + kc * 128:1024 + (kc + 1) * 128], identf[:])
                    return r
                P.op("pe", tr_v, [b_pr, b_ident], [b_pb[0]])
                P.op("act", lambda e: e.copy(vT[:].rearrange("p a t -> p (a t)"), pb[0][:]), [b_pb[0]], [b_vT])

                def mm_v1(e):
                    r = None
                    for kc in range(4):
                        r = e.matmul(pb[1][:, 0:32], vT[:, kc, :], v1t[:, kc, :], start=(kc == 0), stop=(kc == 3))
                    return r
                P.op("pe", mm_v1, [b_vT, b_c], [b_pb[1]])
                P.op("dve", lambda e: e.tensor_copy(vv[:], pb[1][:, 0:32]), [b_pb[1]], [b_vv])
                P.op("pe", lambda e: e.transpose(pb[2][0:32, 0:128], vv[:], identf[:]), [b_vv, b_ident], [b_pb[2]])
                P.op("dve", lambda e: e.tensor_copy(vvT[:], pb[2][0:32, 0:128]), [b_pb[2]], [b_vvT])
                P.op("pe", lambda e: e.matmul(pb[3][:], vvT[:], v2t[:], start=True, stop=True), [b_vvT, b_c], [b_pb[3]])
                P.op("dve", lambda e: e.tensor_tensor(tmp[:], pb[3][:], v0[:], ALU.add), [b_pb[3], b_c], [b_tmp])
                P.op("act", lambda e: e.activation(tmp[:], tmp[:], AF.Sigmoid), [b_tmp], [b_tmp])
                P.op("dve", lambda e: e.tensor_tensor(vf[:], vf[:], v_, ALU.subtract), [b_vf, b_pr], [b_vf])
                P.op("dve", lambda e: e.tensor_tensor(vf[:], vf[:], tmp[:], ALU.mult), [b_vf, b_tmp], [b_vf])
                P.op("dve", lambda e: e.tensor_tensor(v_, v_, vf[:], ALU.add), [b_pr, b_vf], [b_pr])
            P.op("pool", lambda e: e.tensor_tensor(kkn[:], k_, k_k[:], ALU.mult), [b_pr, b_c], [b_kkn])
            P.op("act", lambda e: e.activation(tmp2[:], kkn[:], AF.Square), [b_kkn], [b_tmp2])
            P.op("dve", lambda e: e.tensor_reduce(hs[:, 0:8], h3(tmp2[:]), AX.X, ALU.add), [b_tmp2], [b_hs])
            P.op("act", lambda e: e.activation(hs[:, 0:8], hs[:, 0:8], AF.Sqrt), [b_hs], [b_hs])
            P.op("dve", lambda e: e.tensor_scalar(hs[:, 0:8], hs[:, 0:8], 1e-12, None, ALU.max), [b_hs], [b_hs])
            P.op("dve", lambda e: e.reciprocal(hs[:, 0:8], hs[:, 0:8]), [b_hs], [b_hs])
            P.op("dve", lambda e: e.tensor_tensor(h3(kkn[:]), h3(kkn[:]), bcast_last(hs[:, 0:8], 64), ALU.mult),
                 [b_kkn, b_hs], [b_kkn])
            P.op("dve", lambda e: e.scalar_tensor_tensor(tmp2[:], asg[:], -1.0, k_a[:], ALU.add, ALU.mult),
                 [b_asg, b_c], [b_tmp2])
            P.op("dve", lambda e: e.scalar_tensor_tensor(kf[:], tmp2[:], 1.0, k_, ALU.add, ALU.mult),
                 [b_tmp2, b_pr], [b_kf])
            P.op("pool", lambda e: e.tensor_tensor(bv[:], kkn[:], asg[:], ALU.mult), [b_kkn, b_asg], [b_bv])
            P.op("pool", lambda e: e.tensor_tensor(tmp2[:], r_, kf[:], ALU.mult), [b_pr, b_kf], [b_tmp2])
            P.op("pool", lambda e: e.tensor_tensor(tmp2[:], tmp2[:], r_k[:], ALU.mult), [b_tmp2, b_c], [b_tmp2])
            P.op("dve", lambda e: e.tensor_reduce(hs[:, 8:16], h3(tmp2[:]), AX.X, ALU.add), [b_tmp2], [b_hs])
            if RWKV_STAGE <= 2:
                continue
            P.op("pe", lambda e: e.matmul(pb[4][:], tri_incl_bd[:], lw[:], start=True, stop=True), [b_lw, b_c], [b_pb[4]])
            P.op("pe", lambda e: e.matmul(pb[5][:], tri_after_bd[:], lw[:], start=True, stop=True), [b_lw, b_c], [b_pb[5]])

            def mm_gc(e):
                r = None
                for hp in range(4):
                    r = e.matmul(pb[6][:, hp * 128:(hp + 1) * 128], lw[:, hp * 128:(hp + 1) * 128], tri_incl_bd[:],
                                 start=True, stop=True)
                return r
            P.op("pe", mm_gc, [b_lw, b_c], [b_pb[6]])
            P.op("act", lambda e: e.activation(gCt[:].rearrange("p (a c) -> p a c", a=4),
                                               pb[6][:].rearrange("p (a c t) -> p a c t", a=4, c=2)[:, :, :, 63], AF.Exp),
                 [b_pb[6]], [b_gCt])
            P.op("act", lambda e: e.activation(E1[:], pb[4][:], AF.Exp), [b_pb[4]], [b_E1])
            P.op("act", lambda e: e.activation(E2[:], pb[4][:], AF.Exp, scale=-1.0), [b_pb[4]], [b_E2])
            P.op("dve", lambda e: e.tensor_tensor(E3[:], pb[4][:], lw[:], ALU.subtract), [b_pb[4], b_lw], [b_E3])
            P.op("act", lambda e: e.activation(E3[:], E3[:], AF.Exp), [b_E3], [b_E3])
            P.op("act", lambda e: e.activation(Ee[:], pb[5][:], AF.Exp), [b_pb[5]], [b_Ee])
            P.op("dve", lambda e: e.scalar_tensor_tensor(At[:], kkn[:], -1.0, E3[:], ALU.mult, ALU.mult),
                 [b_kkn, b_E3], [b_At])
            P.op("pool", lambda e: e.tensor_tensor(Bt[:], bv[:], E2[:], ALU.mult), [b_bv, b_E2], [b_Bt])
            P.op("dve", lambda e: e.tensor_tensor(Kt[:], kf[:], E2[:], ALU.mult), [b_kf, b_E2], [b_Kt])
            P.op("pool", lambda e: e.tensor_tensor(Rt[:], r_, E1[:], ALU.mult), [b_pr, b_E1], [b_Rt])
            P.op("dve", lambda e: e.tensor_tensor(Bend[:], bv[:], Ee[:], ALU.mult), [b_bv, b_Ee], [b_Bend])
            P.op("pool", lambda e: e.tensor_tensor(Kend[:], kf[:], Ee[:], ALU.mult), [b_kf, b_Ee], [b_Kend])
            for qi, (src, b_src) in enumerate(((At, b_At), (Rt, b_Rt), (Bt, b_Bt), (Kt, b_Kt))):
                def tr4(e, src=src, qi=qi):
                    r = None
                    for hp in range(4):
                        r = e.transpose(pb[qi][:, hp * 128:(hp + 1) * 128], src[:, hp * 128:(hp + 1) * 128], identf[:])
                    return r
                P.op("pe", tr4, [b_src, b_ident], [b_pb[qi]])
            P.op("act", lambda e: e.copy(ART[:, :, 0, :], pb[0][:].rearrange("p (a t) -> p a t", a=4)), [b_pb[0]], [b_ART])
            P.op("dve", lambda e: e.tensor_copy(ART[:, :, 1, :], pb[1][:].rearrange("p (a t) -> p a t", a=4)), [b_pb[1]], [b_ART])
            for (dstm, b_dstm, bank) in ((BtT, b_BtT, 2), (KtT, b_KtT, 3)):
                v3 = pb[bank][:].rearrange("p (a t) -> p a t", a=4)
                d4 = dstm[:].rearrange("p (a b) t -> p a b t", b=2)
                P.op("act", lambda e, d4=d4, v3=v3: e.copy(d4[0:64, :, 0, :], v3[0:64]), [b_pb[bank]], [b_dstm])
                P.op("dve", lambda e, d4=d4, v3=v3: e.tensor_copy(d4[64:128, :, 1, :], v3[64:128]), [b_pb[bank]], [b_dstm])
            for c_ in range(2):
                cs2 = slice(c_ * 64, c_ * 64 + 64)
                P.op("pool", lambda e, c_=c_, cs2=cs2: e.tensor_copy(Vc[c_][0][cs2, :], pr[cs2, 1024:1536]), [b_pr], [Vc[c_][1]])
            if RWKV_STAGE <= 3:
                continue
            P0, b_P0 = Pm[0]
            PT0, b_PT0 = PTm[0]
            TT0, b_TT0 = TTm[0]
            for half in range(2):
                def mm_m(e, half=half):
                    r = None
                    for hh in range(4):
                        h = half * 4 + hh
                        hp = h // 2
                        art = ART[:, hp, :, :].rearrange("p a t -> p (a t)")
                        e.matmul(pb[0][:, hh * 128:(hh + 1) * 128], ART[:, hp, 0, :], BtT[:, h, :], start=True, stop=True)
                        e.matmul(pb[1 + hh // 2][:, (hh % 2) * 256:(hh % 2) * 256 + 256], BtT[:, h, :], art, start=True, stop=True)
                        r = e.matmul(pb[3 + hh // 2][:, (hh % 2) * 256:(hh % 2) * 256 + 256], KtT[:, h, :], art, start=True, stop=True)
                    return r
                P.op("pe", mm_m, [b_ART, b_BtT, b_KtT], [b_pb[0], b_pb[1], b_pb[2], b_pb[3], b_pb[4]])
                hsl = slice(half * 4, half * 4 + 4)
                if RWKV_SUB <= 0:
                    continue
                P.op("dve", lambda e, hsl=hsl: e.tensor_tensor(P0[:, hsl, :], pb[0][:].rearrange("p (a t) -> p a t", a=4),
                                                              bcast_mid(low_strict_bd[:], 4), ALU.mult),
                     [b_pb[0], b_c], [b_P0])
                if RWKV_SUB <= 1:
                    continue
                for pi in range(2):
                    hs2 = slice(half * 4 + pi * 2, half * 4 + pi * 2 + 2)
                    v4 = pb[1 + pi][:].rearrange("p (h a t) -> p h a t", h=2, a=2)
                    P.op("dve", lambda e, hs2=hs2, v4=v4: e.tensor_tensor(PT0[:, hs2, :], v4[:, :, 0, :],
                                                                         bcast_mid(tri_excl_bd[:], 2), ALU.mult),
                         [b_pb[1 + pi], b_c], [b_PT0])
                    P.op("dve", lambda e, hs2=hs2, v4=v4: e.tensor_tensor(MrbT[:, hs2, :], v4[:, :, 1, :],
                                                                         bcast_mid(tri_incl_bd[:], 2), ALU.mult),
                         [b_pb[1 + pi], b_c], [b_MrbT])
                    v5 = pb[3 + pi][:].rearrange("p (h a t) -> p h a t", h=2, a=2)
                    P.op("dve", lambda e, hs2=hs2, v5=v5: e.tensor_tensor(MakT[:, hs2, :], v5[:, :, 0, :],
                                                                         bcast_mid(tri_excl_bd[:], 2), ALU.mult),
                         [b_pb[3 + pi], b_c], [b_MakT])
                    P.op("dve", lambda e, hs2=hs2, v5=v5: e.tensor_tensor(MrkT[:, hs2, :], v5[:, :, 1, :],
                                                                         bcast_mid(tri_incl_bd[:], 2), ALU.mult),
                         [b_pb[3 + pi], b_c], [b_MrkT])
            if RWKV_STAGE <= 4:
                continue
            P.op("pool", lambda e: e.tensor_tensor(TT0[:], PT0[:], bcast_mid(identf[:], 8), ALU.add), [b_PT0, b_ident], [b_TT0])
            cur = 0
            for j in range(5):
                nxt = 1 - cur
                Pc, b_Pc = Pm[cur]
                PTc, b_PTc = PTm[cur]
                TTc, b_TTc = TTm[cur]
                Pn, b_Pn = Pm[nxt]
                PTn, b_PTn = PTm[nxt]
                TTn, b_TTn = TTm[nxt] if j < 4 else TTf
                for half in range(2):
                    bP, bPT, bTT = half * 3, half * 3 + 1, half * 3 + 2

                    def mm_sq(e, half=half, Pc=Pc, PTc=PTc, bP=bP, bPT=bPT, j=j):
                        r = None
                        for hh in range(4):
                            h = half * 4 + hh
                            r = e.matmul(pb[bP][:, hh * 128:(hh + 1) * 128], PTc[:, h, :], Pc[:, h, :], start=True, stop=True)
                            if j < 4:
                                r = e.matmul(pb[bPT][:, hh * 128:(hh + 1) * 128], Pc[:, h, :], PTc[:, h, :], start=True, stop=True)
                        return r
                    P.op("pe", mm_sq, [b_Pc, b_PTc], [b_pb[bP], b_pb[bPT]])
                    hsl = slice(half * 4, half * 4 + 4)
                    P.op("act", lambda e, Pn=Pn, hsl=hsl, bP=bP: e.copy(Pn[:, hsl, :], pb[bP][:].rearrange("p (a t) -> p a t", a=4)),
                         [b_pb[bP]], [b_Pn])
                    if j < 4:
                        P.op("dve", lambda e, PTn=PTn, hsl=hsl, bPT=bPT: e.tensor_copy(
                            PTn[:, hsl, :], pb[bPT][:].rearrange("p (a t) -> p a t", a=4)), [b_pb[bPT]], [b_PTn])

                    def mm_tt(e, half=half, Pn=Pn, TTc=TTc, bTT=bTT):
                        r = None
                        for hh in range(4):
                            h = half * 4 + hh
                            r = e.matmul(pb[bTT][:, hh * 128:(hh + 1) * 128], Pn[:, h, :], TTc[:, h, :], start=True, stop=True)
                        return r
                    P.op("pe", mm_tt, [b_Pn, b_TTc], [b_pb[bTT]])
                    P.op("dve", lambda e, TTn=TTn, TTc=TTc, hsl=hsl, bTT=bTT: e.tensor_tensor(
                        TTn[:, hsl, :], pb[bTT][:].rearrange("p (a t) -> p a t", a=4), TTc[:, hsl, :], ALU.add),
                        [b_pb[bTT], b_TTc], [b_TTn])
                cur = nxt
            TT, b_TT = TTf
            if RWKV_STAGE <= 5:
                continue
            for c in range(2):
                cs_ = slice(c * 64, c * 64 + 64)
                SAs, b_SAs = SAc[c]
                Vm, b_Vm = Vc[c]

                def mm_x(e):
                    r = None
                    for h in range(8):
                        hp = h // 2
                        o = pb[6][:, h * 64:(h + 1) * 64]
                        e.matmul(o, ART[:, hp, 0, :], S0T[:, h, :], start=True, stop=False)
                        r = e.matmul(o, MakT[:, h, :], pr[:, 1024 + h * 64:1024 + (h + 1) * 64], start=False, stop=True)
                    return r
                P.op("pe", mm_x, [b_ART, b_S0T, b_MakT, b_pr], [b_pb[6]])
                P.op("act", lambda e, cs_=cs_: e.copy(Xs[cs_, :], pb[6][cs_, :]), [b_pb[6]], [b_Xs])

                def mm_sa(e):
                    r = None
                    for h in range(8):
                        r = e.matmul(pb[7][:, h * 64:(h + 1) * 64], TT[:, h, :], Xs[:, h * 64:(h + 1) * 64], start=True, stop=True)
                    return r
                P.op("pe", mm_sa, [b_TT, b_Xs], [b_pb[7]])
                P.op("dve", lambda e, cs_=cs_, SAs=SAs: e.tensor_copy(SAs[cs_, :], pb[7][cs_, :]), [b_pb[7]], [b_SAs])

                def mm_y(e, SAs=SAs):
                    r = None
                    for h in range(8):
                        hp = h // 2
                        o = pb[6][:, h * 64:(h + 1) * 64]
                        e.matmul(o, ART[:, hp, 1, :], S0T[:, h, :], start=True, stop=False)
                        e.matmul(o, MrbT[:, h, :], SAs[:, h * 64:(h + 1) * 64], start=False, stop=False)
                        r = e.matmul(o, MrkT[:, h, :], pr[:, 1024 + h * 64:1024 + (h + 1) * 64], start=False, stop=True)
                    return r
                P.op("pe", mm_y, [b_ART, b_S0T, b_MrbT, b_MrkT, b_SAs, b_pr], [b_pb[6]])
                P.op("act", lambda e, cs_=cs_: e.copy(Ys[cs_, :], pb[6][cs_, :]), [b_pb[6]], [b_Ys])

                def mm_s(e, SAs=SAs, Vm=Vm):
                    r = None
                    for h in range(8):
                        hp = h // 2
                        o = pb[7][:, h * 64:(h + 1) * 64]
                        e.matmul(o, Bend[:, hp * 128:(hp + 1) * 128], SAs[:, h * 64:(h + 1) * 64], start=True, stop=False)
                        r = e.matmul(o, Kend[:, hp * 128:(hp + 1) * 128], Vm[:, h * 64:(h + 1) * 64], start=False, stop=True)
                    return r
                P.op("pe", mm_s, [b_Bend, b_Kend, b_SAs, b_Vm], [b_pb[7]])
                S4 = S0T[:].rearrange("p (a b) v -> p a b v", b=2)
                p4 = pb[7][:].rearrange("p (a b v) -> p a b v", a=4, b=2)
                g3 = gCt[:].rearrange("p (a c) -> p a c", a=4)
                for h2 in range(2):
                    sl = slice(h2 * 64, h2 * 64 + 64)
                    P.op("dve", lambda e, sl=sl, h2=h2, c=c: e.tensor_tensor(S4[sl, :, h2, :], S4[sl, :, h2, :],
                                                                          bcast_last(g3[sl, :, c], 64), ALU.mult),
                         [b_S0T, b_gCt], [b_S0T])
                    P.op("dve", lambda e, sl=sl, h2=h2: e.tensor_tensor(S4[sl, :, h2, :], S4[sl, :, h2, :], p4[sl, :, h2, :], ALU.add),
                         [b_S0T, b_pb[7]], [b_S0T])
            if RWKV_STAGE <= 6:
                continue
            P.op("dve", lambda e: e.tensor_reduce(hs[:, 16:24], h3(Ys[:]), AX.X, ALU.add), [b_Ys], [b_hs])
            P.op("dve", lambda e: e.tensor_scalar(hs[:, 16:24], hs[:, 16:24], -1.0 / 64, None, ALU.mult), [b_hs], [b_hs])
            P.op("dve", lambda e: e.tensor_tensor(h3(Ys[:]), h3(Ys[:]), bcast_last(hs[:, 16:24], 64), ALU.add),
                 [b_Ys, b_hs], [b_Ys])
            P.op("act", lambda e: e.activation(tmp[:], Ys[:], AF.Square), [b_Ys], [b_tmp])
            P.op("dve", lambda e: e.tensor_reduce(hs[:, 24:32], h3(tmp[:]), AX.X, ALU.add), [b_tmp], [b_hs])
            rstd_from_sumsq(P, hs[:, 24:32], hs[:, 24:32], 64, 64e-5, [b_hs], [b_hs])
            P.op("dve", lambda e: e.tensor_tensor(h3(Ys[:]), h3(Ys[:]), bcast_last(hs[:, 24:32], 64), ALU.mult),
                 [b_Ys, b_hs], [b_Ys])
            P.op("pool", lambda e: e.tensor_tensor(Ys[:], Ys[:], ln_g[:], ALU.mult), [b_Ys, b_c], [b_Ys])
            P.op("pool", lambda e: e.tensor_tensor(Ys[:], Ys[:], ln_b[:], ALU.add), [b_Ys, b_c], [b_Ys])
            P.op("dve", lambda e: e.tensor_tensor(h3(tmp[:]), h3(v_), bcast_last(hs[:, 8:16], 64), ALU.mult),
                 [b_pr, b_hs], [b_tmp])
            P.op("pool", lambda e: e.tensor_tensor(Ys[:], Ys[:], tmp[:], ALU.add), [b_Ys, b_tmp], [b_Ys])
            P.op("dve", lambda e: e.tensor_tensor(yo[:], Ys[:], gg[:], ALU.mult), [b_Ys, b_gg], [b_yo])
            P.op("dve", lambda e: e.tensor_copy(tmp[:], yo[:]), [b_yo], [b_tmp])

            def tr_o(e):
                r = None
                for kc in range(4):
                    r = e.transpose(pb[0][:, kc * 128:(kc + 1) * 128], tmp[:, kc * 128:(kc + 1) * 128], identf[:])
                return r
            P.op("pe", tr_o, [b_tmp, b_ident], [b_pb[0]])
            i = tb % 2
            P.op("act", lambda e, i=i: e.copy(oT[i][:].rearrange("p a t -> p (a t)"), pb[0][:]), [b_pb[0]], [b_oT[i]])
            P.dma(g["orwkvT"][:, t0:t0 + 128].rearrange("(k p) t -> p k t", p=128), oT[i][:], [b_oT[i]], [g["b_orwkvT"]])
        P.flush()


_PROG_CACHE = {}


def kernel(**inputs):
    x = np.ascontiguousarray(np.asarray(inputs["x"], dtype=np.float32))
    pos = np.asarray(inputs["positions"]).astype(np.int32)
    B, T, _ = x.shape
    key = (T,)
    if key not in _PROG_CACHE:
        _PROG_CACHE[key] = build_program(T, DEPTH, "all", debug=False)
    nc, _ = _PROG_CACHE[key]
    consts = build_consts()
    shared = {}
    for k, shp in PARAM_SHAPES.items():
        shared[k] = np.ascontiguousarray(np.asarray(inputs[k], dtype=np.float32)).reshape(shp)
    shared.update(consts)
    in_maps = []
    for b in range(B):
        m = dict(shared)
        m["x"] = np.ascontiguousarray(x[b])
        m["positions"] = np.ascontiguousarray(pos[b].reshape(T // 128, 128).T)
        in_maps.append(m)
    res = run_bass_kernel_spmd(nc, in_maps, core_ids=list(range(B)))
    out = np.stack([np.asarray(r["y"], dtype=np.float32) for r in res.results], axis=0)
    return out
```
